# Optimizing a Trainium2 kernel written in Bass

```python
import math
import jax, jax.numpy as jnp
from jax import lax
import numpy as np

D_MODEL = 1024
BATCH = 8
SEQ = 2048
DEPTH = 2

N_MEM = 256
H_A = 4
D_A = 64
DV_A = 2 * D_A
Q_BLOCK = 128
ROPE_THETA = 500000.0
ROPE_DIM = D_A // 4
H_B = 4
DK_B = 64
DV_B = 128
GATE_RANK = 16
GATE_NORMALIZER = 16.0
H_C = 8
DK_C = 128
DV_C = 128
CONV_C = 4
CHUNK = 64
N_XA = 4
D_XA = D_MODEL // N_XA
D_FF = int(math.ceil(8 * D_MODEL / 3 / 128)) * 128
FFN_CONV = 3
LN_EPS = 1e-5
RMS_EPS = 1e-6
DEEPNORM_ALPHA = (2.0 * DEPTH) ** 0.25
DEEPNORM_BETA = (8.0 * DEPTH) ** -0.25
P0_SPLITS = (H_A * 2 * D_A, H_A * 2 * D_A, H_A * DV_A,
             H_B * DK_B, H_B * DK_B, H_B * DV_B, H_B * DV_B, GATE_RANK)
P1_SPLITS = (H_C * DK_C, H_C * DK_C, H_C * DV_C, H_C * DV_C, H_C, H_C)
P0 = sum(P0_SPLITS)
P1 = sum(P1_SPLITS)

kernel_name = 'hybrid_diffattn_gla_gdn_deepnorm'

F32 = jnp.float32


def _split(h, sizes):
    out, start = [], 0
    for s in sizes:
        out.append(h[..., start:start + s])
        start += s
    return out


def layer_norm(x, g, b):
    xf = x.astype(F32)
    xc = xf - jnp.mean(xf, axis=-1, keepdims=True)
    var = jnp.mean(xc * xc, axis=-1, keepdims=True)
    return (xc * lax.rsqrt(var + LN_EPS)).astype(x.dtype) * g + b


def rms_norm(x, g):
    xf = x.astype(F32)
    y = xf * lax.rsqrt(jnp.mean(xf * xf, axis=-1, keepdims=True) + RMS_EPS)
    return y * g.astype(F32)


def l2norm(x):
    return x * lax.rsqrt(jnp.sum(x * x, axis=-1, keepdims=True) + RMS_EPS)


def causal_dwconv(x, w, b=None):
    K = w.shape[0]
    S_ = x.shape[1]
    xp = jnp.pad(x, ((0, 0), (K - 1, 0), (0, 0)))
    y = xp[:, 0:S_] * w[0]
    for j in range(1, K):
        y = y + xp[:, j:j + S_] * w[j]
    if b is not None:
        y = y + b
    return y


def rope_cos_sin(pos):
    inv_freq = ROPE_THETA ** (-jnp.arange(0, ROPE_DIM, 2, dtype=F32) / ROPE_DIM)
    ang = pos.astype(F32)[..., None] * inv_freq
    return jnp.cos(ang), jnp.sin(ang)


def apply_partial_rope(x, cos, sin):
    half = ROPE_DIM // 2
    xf = x.astype(F32)
    x1 = xf[..., :half]
    x2 = xf[..., half:ROPE_DIM]
    rot = jnp.concatenate([x1 * cos - x2 * sin, x2 * cos + x1 * sin, xf[..., ROPE_DIM:]], axis=-1)
    return rot.astype(x.dtype)


def diff_attention(q, k, v, lam_vecs, norm_g, positions, layer_idx):
    B_, S_, _ = q.shape
    q = q.reshape(B_, S_, H_A, 2, D_A)
    k = k.reshape(B_, S_, H_A, 2, D_A)
    v = v.reshape(B_, S_, H_A, DV_A)
    cos, sin = rope_cos_sin(positions)
    cos = cos[:, :, None, None, :]
    sin = sin[:, :, None, None, :]
    q = apply_partial_rope(q, cos, sin) * (D_A ** -0.5)
    k = apply_partial_rope(k, cos, sin)
    lam_init = 0.8 - 0.6 * math.exp(-0.3 * layer_idx)
    lf = lam_vecs.astype(F32)
    lam = jnp.exp(jnp.sum(lf[0] * lf[1])) - jnp.exp(jnp.sum(lf[2] * lf[3])) + lam_init
    qh = q.transpose(0, 2, 3, 1, 4)
    kh = k.transpose(0, 2, 3, 1, 4)
    vh = v.transpose(0, 2, 1, 3)
    nb = S_ // Q_BLOCK
    qb = jnp.moveaxis(qh.reshape(B_, H_A, 2, nb, Q_BLOCK, D_A), 3, 0)
    kpos = jnp.arange(S_)

    def one_block(args):
        qi, bi = args
        s = jnp.einsum('bhmqd,bhmkd->bhmqk', qi, kh).astype(F32)
        qpos = bi * Q_BLOCK + jnp.arange(Q_BLOCK)
        s = jnp.where(kpos[None, :] <= qpos[:, None], s, -jnp.inf)
        p = jax.nn.softmax(s, axis=-1)
        a = p[:, :, 0] - lam * p[:, :, 1]
        return jnp.einsum('bhqk,bhkv->bhqv', a.astype(vh.dtype), vh)

    ob = lax.map(one_block, (qb, jnp.arange(nb)))
    o = jnp.moveaxis(ob, 0, 2).reshape(B_, H_A, S_, DV_A).transpose(0, 2, 1, 3)
    o = rms_norm(o, norm_g) * (1.0 - lam_init)
    return o.reshape(B_, S_, H_A * DV_A).astype(q.dtype)


def gla(q, k, v, g_low, r, w2, b2, norm_g):
    B_, S_, _ = q.shape
    N = S_ // CHUNK
    gk = jax.nn.log_sigmoid((g_low @ w2 + b2).astype(F32)) / GATE_NORMALIZER

    def heads(t, d):
        return t.astype(F32).reshape(B_, N, CHUNK, H_B, d).transpose(1, 0, 3, 2, 4)

    qc = heads(q, DK_B) * (DK_B ** -0.5)
    kc = heads(k, DK_B)
    vc = heads(v, DV_B)
    gc = heads(gk, DK_B)
    causal = jnp.tril(jnp.ones((CHUNK, CHUNK), dtype=bool))

    def step(state, inp):
        qi, ki, vi, gi = inp
        b = jnp.cumsum(gi, axis=2)
        inter = jnp.einsum('bhtk,bhkv->bhtv', qi * jnp.exp(b), state)
        diff = jnp.where(causal[:, :, None], b[:, :, :, None, :] - b[:, :, None, :, :], -jnp.inf)
        att = jnp.einsum('bhtk,bhsk,bhtsk->bhts', qi, ki, jnp.exp(diff))
        out = inter + jnp.einsum('bhts,bhsv->bhtv', att, vi)
        b_last = b[:, :, -1:, :]
        state = state * jnp.exp(b_last)[:, :, 0, :, None] + jnp.einsum('bhsk,bhsv->bhkv', ki * jnp.exp(b_last - b), vi)
        return state, out

    s0 = jnp.zeros((B_, H_B, DK_B, DV_B), F32)
    _, oc = lax.scan(step, s0, (qc, kc, vc, gc))
    o = oc.transpose(1, 0, 3, 2, 4).reshape(B_, S_, H_B, DV_B)
    o = rms_norm(o, norm_g) * jax.nn.silu(r.astype(F32)).reshape(B_, S_, H_B, DV_B)
    return o.reshape(B_, S_, H_B * DV_B).astype(q.dtype)


def gated_deltanet(q, k, v, beta_logit, a_logit, z, a_log, dt_bias, norm_g):
    B_, S_, _ = q.shape
    N = S_ // CHUNK

    def heads(t, d):
        return t.astype(F32).reshape(B_, N, CHUNK, H_C, d).transpose(0, 3, 1, 2, 4)

    def scal(t):
        return t.reshape(B_, N, CHUNK, H_C).transpose(0, 3, 1, 2)

    qh = l2norm(heads(q, DK_C)) * (DK_C ** -0.5)
    kh = l2norm(heads(k, DK_C))
    vh = heads(v, DV_C)
    beta = scal(jax.nn.sigmoid(beta_logit.astype(F32)))
    g = -jnp.exp(a_log.astype(F32)) * jax.nn.softplus(a_logit.astype(F32) + dt_bias.astype(F32))
    gc = jnp.cumsum(scal(g), axis=-1)
    incl = jnp.tril(jnp.ones((CHUNK, CHUNK), dtype=bool))
    strict = jnp.tril(jnp.ones((CHUNK, CHUNK), dtype=bool), k=-1)
    decay = jnp.exp(jnp.where(incl, gc[..., :, None] - gc[..., None, :], -jnp.inf))
    kb = kh * beta[..., None]
    m = jnp.where(strict, jnp.einsum('bhntk,bhnsk->bhnts', kb, kh) * decay, 0.0)
    eye = jnp.eye(CHUNK, dtype=F32)
    tinv = lax.linalg.triangular_solve(eye + m, jnp.broadcast_to(eye, m.shape), left_side=True, lower=True, unit_diagonal=True)
    u = jnp.einsum('bhnts,bhnsv->bhntv', tinv, vh * beta[..., None])
    w = jnp.einsum('bhnts,bhnsk->bhntk', tinv, kb * jnp.exp(gc)[..., None])
    qk = jnp.einsum('bhntk,bhnsk->bhnts', qh, kh) * decay
    qg = qh * jnp.exp(gc)[..., None]
    kg = kh * jnp.exp(gc[..., -1:] - gc)[..., None]
    g_last = jnp.exp(gc[..., -1])

    def step(state, inp):
        ui, wi, qgi, qki, kgi, gli = inp
        v_new = ui - jnp.einsum('bhck,bhkv->bhcv', wi, state)
        out = jnp.einsum('bhck,bhkv->bhcv', qgi, state) + jnp.einsum('bhts,bhsv->bhtv', qki, v_new)
        state = state * gli[:, :, None, None] + jnp.einsum('bhck,bhcv->bhkv', kgi, v_new)
        return state, out

    xs = (jnp.moveaxis(u, 2, 0), jnp.moveaxis(w, 2, 0), jnp.moveaxis(qg, 2, 0),
          jnp.moveaxis(qk, 2, 0), jnp.moveaxis(kg, 2, 0), jnp.moveaxis(g_last, 2, 0))
    s0 = jnp.zeros((B_, H_C, DK_C, DV_C), F32)
    _, oc = lax.scan(step, s0, xs)
    o = oc.transpose(1, 0, 3, 2, 4).reshape(B_, S_, H_C, DV_C)
    o = rms_norm(o, norm_g) * jax.nn.silu(z.astype(F32)).reshape(B_, S_, H_C, DV_C)
    return o.reshape(B_, S_, H_C * DV_C).astype(q.dtype)


def mixer_ab(x, positions, layer_idx, w_in, diff_lambda, diff_norm, gla_w2, gla_b2, gla_norm, w_out):
    h = x @ w_in
    aq, ak, av, bq, bk, bv, br, bg = _split(h, P0_SPLITS)
    oa = diff_attention(aq, ak, av, diff_lambda, diff_norm, positions, layer_idx)
    ob = gla(bq, bk, bv, bg, br, gla_w2, gla_b2, gla_norm)
    return jnp.concatenate([oa, ob], axis=-1) @ w_out


def mixer_c(x, w_in, conv_w, a_log, dt_bias, norm_g, w_out):
    h = x @ w_in
    q, k, v, z, bl, al = _split(h, P1_SPLITS)
    qkv = jax.nn.silu(causal_dwconv(jnp.concatenate([q, k, v], axis=-1), conv_w))
    q, k, v = _split(qkv, P1_SPLITS[:3])
    o = gated_deltanet(q, k, v, bl, al, z, a_log, dt_bias, norm_g)
    return o @ w_out


def memory_cross_attention(x, mem, wq, wkv, wo):
    B_, S_, _ = x.shape
    q = (x @ wq).reshape(B_, S_, N_XA, D_XA)
    k, v = _split(mem @ wkv, (D_MODEL, D_MODEL))
    k = k.reshape(B_, N_MEM, N_XA, D_XA)
    v = v.reshape(B_, N_MEM, N_XA, D_XA)
    s = jnp.einsum('bqhd,bkhd->bhqk', q, k).astype(F32) * (D_XA ** -0.5)
    p = jax.nn.softmax(s, axis=-1).astype(v.dtype)
    o = jnp.einsum('bhqk,bkhd->bqhd', p, v).reshape(B_, S_, D_MODEL)
    return o @ wo


def conv_ffn(x, w_in, conv_w, conv_b, w_out):
    h = causal_dwconv(x @ w_in, conv_w, conv_b)
    g, u = _split(h, (D_FF, D_FF))
    return (jax.nn.silu(g) * u) @ w_out


def setup_inputs(seed: int = 0) -> dict:
    key = jax.random.key(seed)
    keys = list(jax.random.split(key, 64))

    def nk():
        return keys.pop()

    def normal(shape, scale):
        return jax.random.normal(nk(), shape, F32) * scale

    def dense(fi, fo, scale=1.0):
        return normal((fi, fo), (fi ** -0.5) * scale)

    def gain(n):
        return 1.0 + normal((n,), 0.02)

    def bias(n):
        return normal((n,), 0.02)

    p = {}
    p['x'] = normal((BATCH, SEQ, D_MODEL), 1.0)
    p['mem'] = normal((BATCH, N_MEM, D_MODEL), 1.0)
    p['positions'] = jnp.broadcast_to(jnp.arange(SEQ, dtype=jnp.int32), (BATCH, SEQ))
    for l in range(DEPTH):
        if l % 2 == 0:
            p[f'w_in_{l}'] = dense(D_MODEL, P0)
            p[f'diff_lambda_{l}'] = normal((4, D_A), 0.1)
            p[f'diff_norm_{l}'] = gain(DV_A)
            p[f'gla_w2_{l}'] = dense(GATE_RANK, H_B * DK_B)
            p[f'gla_b2_{l}'] = bias(H_B * DK_B)
            p[f'gla_norm_{l}'] = gain(DV_B)
            p[f'w_mix_out_{l}'] = dense(H_A * DV_A + H_B * DV_B, D_MODEL, DEEPNORM_BETA)
        else:
            p[f'w_in_{l}'] = dense(D_MODEL, P1)
            p[f'gdn_conv_w_{l}'] = normal((CONV_C, 3 * H_C * DK_C), CONV_C ** -0.5)
            p[f'gdn_a_log_{l}'] = jnp.log(jax.random.uniform(nk(), (H_C,), F32, 1.0, 16.0))
            dt = jnp.exp(jax.random.uniform(nk(), (H_C,), F32, math.log(0.001), math.log(0.1)))
            p[f'gdn_dt_bias_{l}'] = dt + jnp.log(-jnp.expm1(-dt))
            p[f'gdn_norm_{l}'] = gain(DV_C)
            p[f'w_mix_out_{l}'] = dense(H_C * DV_C, D_MODEL, DEEPNORM_BETA)
        p[f'ln1_g_{l}'] = gain(D_MODEL)
        p[f'ln1_b_{l}'] = bias(D_MODEL)
        p[f'xa_wq_{l}'] = dense(D_MODEL, D_MODEL)
        p[f'xa_wkv_{l}'] = dense(D_MODEL, 2 * D_MODEL)
        p[f'xa_wo_{l}'] = dense(D_MODEL, D_MODEL, DEEPNORM_BETA)
        p[f'ln2_g_{l}'] = gain(D_MODEL)
        p[f'ln2_b_{l}'] = bias(D_MODEL)
        p[f'ffn_w_in_{l}'] = dense(D_MODEL, 2 * D_FF)
        p[f'ffn_conv_w_{l}'] = normal((FFN_CONV, 2 * D_FF), FFN_CONV ** -0.5)
        p[f'ffn_conv_b_{l}'] = bias(2 * D_FF)
        p[f'ffn_w_out_{l}'] = dense(D_FF, D_MODEL, DEEPNORM_BETA)
        p[f'ln3_g_{l}'] = gain(D_MODEL)
        p[f'ln3_b_{l}'] = bias(D_MODEL)
    return p


def reference(x, mem, positions,
              w_in_0, diff_lambda_0, diff_norm_0, gla_w2_0, gla_b2_0, gla_norm_0, w_mix_out_0,
              ln1_g_0, ln1_b_0, xa_wq_0, xa_wkv_0, xa_wo_0, ln2_g_0, ln2_b_0,
              ffn_w_in_0, ffn_conv_w_0, ffn_conv_b_0, ffn_w_out_0, ln3_g_0, ln3_b_0,
              w_in_1, gdn_conv_w_1, gdn_a_log_1, gdn_dt_bias_1, gdn_norm_1, w_mix_out_1,
              ln1_g_1, ln1_b_1, xa_wq_1, xa_wkv_1, xa_wo_1, ln2_g_1, ln2_b_1,
              ffn_w_in_1, ffn_conv_w_1, ffn_conv_b_1, ffn_w_out_1, ln3_g_1, ln3_b_1):
    mixer_params = [
        (w_in_0, diff_lambda_0, diff_norm_0, gla_w2_0, gla_b2_0, gla_norm_0, w_mix_out_0),
        (w_in_1, gdn_conv_w_1, gdn_a_log_1, gdn_dt_bias_1, gdn_norm_1, w_mix_out_1),
    ]
    common_params = [
        (ln1_g_0, ln1_b_0, xa_wq_0, xa_wkv_0, xa_wo_0, ln2_g_0, ln2_b_0,
         ffn_w_in_0, ffn_conv_w_0, ffn_conv_b_0, ffn_w_out_0, ln3_g_0, ln3_b_0),
        (ln1_g_1, ln1_b_1, xa_wq_1, xa_wkv_1, xa_wo_1, ln2_g_1, ln2_b_1,
         ffn_w_in_1, ffn_conv_w_1, ffn_conv_b_1, ffn_w_out_1, ln3_g_1, ln3_b_1),
    ]
    h = x
    for l in range(DEPTH):
        (ln1_g, ln1_b, xa_wq, xa_wkv, xa_wo, ln2_g, ln2_b,
         f_in, f_cw, f_cb, f_out, ln3_g, ln3_b) = common_params[l]
        if l % 2 == 0:
            mix = mixer_ab(h, positions, l, *mixer_params[l])
        else:
            mix = mixer_c(h, *mixer_params[l])
        h = layer_norm(DEEPNORM_ALPHA * h + mix, ln1_g, ln1_b)
        h = layer_norm(DEEPNORM_ALPHA * h + memory_cross_attention(h, mem, xa_wq, xa_wkv, xa_wo), ln2_g, ln2_b)
        h = layer_norm(DEEPNORM_ALPHA * h + conv_ffn(h, f_in, f_cw, f_cb, f_out), ln3_g, ln3_b)
    return h
```

```python
import math
import os
from contextlib import ExitStack
import numpy as np
import concourse.bass as bass
import concourse.mybir as mybir
from concourse.bass_utils import run_bass_kernel_spmd

F32 = mybir.dt.float32
BF16 = mybir.dt.bfloat16
I32 = mybir.dt.int32
U8 = mybir.dt.uint8
AF = mybir.ActivationFunctionType
ALU = mybir.AluOpType
AX = mybir.AxisListType

ENGS = ("pe", "act", "dve", "pool", "sp")


class Buf:
    __slots__ = ("name", "lastw", "readers", "dsem", "excl")

    def __init__(self, name):
        self.name = name
        self.lastw = []
        self.readers = []
        self.dsem = None
        self.excl = False


class DmaSem:
    __slots__ = ("h", "total", "last", "key")

    def __init__(self, h, key):
        self.h = h
        self.total = 0
        self.last = None
        self.key = key


class Op:
    __slots__ = ("eng", "fn", "deps", "dma", "sig", "cnt", "clock", "signal")

    def __init__(self, eng, fn, dma=None):
        self.eng = eng
        self.fn = fn
        self.deps = []
        self.dma = dma
        self.sig = None
        self.cnt = None
        self.clock = None
        self.signal = False


class Prog:
    def __init__(self, nc, stack):
        self.nc = nc
        self.stack = stack
        self.ops = []
        self.nsem = 0
        self.esem = {e: stack.enter_context(nc.semaphore("es_" + e)) for e in ENGS}
        self.nbuf = 0

    def sb(self, name, shape, dtype):
        return self.stack.enter_context(self.nc.sbuf_tensor(name, list(shape), dtype))

    def ps(self, name, shape, dtype=F32):
        return self.stack.enter_context(self.nc.psum_tensor(name, list(shape), dtype))

    def buf(self, name=None):
        self.nbuf += 1
        return Buf(name or f"b{self.nbuf}")

    def _dsem(self, b):
        if b.dsem is None:
            self.nsem += 1
            h = self.stack.enter_context(self.nc.semaphore(f"ds{self.nsem}"))
            b.dsem = DmaSem(h, self.nsem)
        return b.dsem

    def _deps(self, op, reads, writes):
        deps = []
        for b in reads:
            deps.extend(b.lastw)
            if b.excl:
                deps.extend(b.readers)
        for b in writes:
            deps.extend(b.lastw)
            deps.extend(b.readers)
        seen = set(id(d) for d in op.deps)
        for d in deps:
            if d is op or id(d) in seen:
                continue
            seen.add(id(d))
            op.deps.append(d)

    def _update(self, op, reads, writes):
        for b in reads:
            if b not in writes:
                b.readers.append(op)
        for b in writes:
            b.lastw = [op]
            b.readers = []

    def op(self, eng, fn, reads=(), writes=()):
        o = Op(eng, fn)
        reads = list(reads)
        writes = list(writes)
        self._deps(o, reads, writes)
        self._update(o, reads, writes)
        self.ops.append(o)
        return o

    def dma(self, eng, out, in_, sbuf, reads=(), writes=()):
        return self.dma_group(eng, [(out, in_)], sbuf, reads, writes)

    def dma_group(self, eng, pairs, sbuf, reads=(), writes=()):
        ds = self._dsem(sbuf)
        reads = list(reads)
        writes = list(writes)
        ops = []
        for (out, in_) in pairs:
            o = Op(eng, (lambda e, out=out, in_=in_: e.dma_start(out=out, in_=in_)), dma=ds)
            if ds.last is not None:
                o.deps.append(ds.last)
            self._deps(o, reads, writes)
            ops.append(o)
        for o in ops:
            ds.total += 16
            o.sig = ds.total
            self.ops.append(o)
        last = ops[-1]
        ds.last = last
        for o in ops[:-1]:
            for b in reads:
                if b not in writes:
                    b.readers.append(o)
        self._update(last, reads, writes)
        return last

    def emit(self):
        nc = self.nc
        for o in self.ops:
            for d in o.deps:
                if d.dma is None:
                    if d.eng == "pe" and o.eng == "pe" and o.dma is None:
                        continue
                    d.signal = True
        cnt = {e: 0 for e in ENGS}
        seen = {e: {x: 0 for x in ENGS} for e in ENGS}
        seen_d = {e: {} for e in ENGS}
        per_eng = {e: [] for e in ENGS}
        nwait = 0
        for o in self.ops:
            E = o.eng
            sE = seen[E]
            wm = {}
            for d in o.deps:
                if d.dma is not None:
                    k = d.dma.key
                    if seen_d[E].get(k, 0) < d.sig:
                        seen_d[E][k] = d.sig
                        kk = ("d", k)
                        if kk not in wm or wm[kk][1] < d.sig:
                            wm[kk] = (d.dma.h, d.sig)
                else:
                    if d.eng == "pe" and E == "pe" and o.dma is None:
                        continue
                    if sE[d.eng] < d.cnt:
                        kk = ("e", d.eng)
                        if kk not in wm or wm[kk][1] < d.cnt:
                            wm[kk] = (self.esem[d.eng], d.cnt)
                        for x in ENGS:
                            if d.clock[x] > sE[x]:
                                sE[x] = d.clock[x]
            waits = list(wm.values())
            nwait += len(waits)
            if o.dma is None and o.signal:
                cnt[E] += 1
                o.cnt = cnt[E]
                clk = dict(sE)
                clk[E] = o.cnt
                o.clock = clk
            per_eng[E].append((o, waits))
        import os as _os
        if _os.environ.get("DUMPW"):
            names = {id(self.esem[e]): "E_" + e for e in ENGS}
            for e in ENGS:
                print("ENGINE", e)
                for o, waits in per_eng[e]:
                    ws = [(names.get(id(h), "dsem"), v) for h, v in waits]
                    print("   ", "dma" if o.dma is not None else "op", "sig" if o.signal else "", o.cnt, o.sig, (o.dma.key if o.dma else ""), ws)
        self.stats = dict(nops=len(self.ops), nwait=nwait, cnt=dict(cnt), nsem=self.nsem,
                          per_eng={e: len(per_eng[e]) for e in ENGS})
        assert max(cnt.values()) < 60000, cnt
        esem = self.esem
        with nc.Block() as block:
            def run(e, lst, E):
                for o, waits in lst:
                    for h, v in waits:
                        e.wait_ge(h, v)
                    ins = o.fn(e)
                    if o.dma is not None:
                        ins.then_inc(o.dma.h, 16)
                    elif o.signal:
                        ins.then_inc(esem[E], 1)

            @block.tensor
            def _(e):
                run(e, per_eng["pe"], "pe")

            @block.scalar
            def _(e):
                run(e, per_eng["act"], "act")

            @block.vector
            def _(e):
                run(e, per_eng["dve"], "dve")

            @block.gpsimd
            def _(e):
                run(e, per_eng["pool"], "pool")

            @block.sync
            def _(e):
                run(e, per_eng["sp"], "sp")


class Arena:
    def __init__(self, P, tensor, nbytes):
        self.P = P
        self.t = tensor
        self.n = nbytes
        self.off = 0
        self.hist = []

    def mark(self):
        return self.off

    def release(self, m):
        self.off = m

    def alloc(self, shape, dtype, name=None):
        esz = 4 if dtype in (F32, I32) else 2
        n = int(np.prod(shape[1:])) * esz
        s = (self.off + 63) // 64 * 64
        e = s + n
        assert e <= self.n, f"arena overflow {name} {e} > {self.n}"
        self.off = e
        self.peak = max(getattr(self, "peak", 0), e)
        v = self.t[:, s:e].bitcast(dtype)
        if len(shape) == 3:
            v = v.rearrange("p (a b) -> p a b", a=shape[1])
        if shape[0] < 128:
            v = v[0:shape[0]]
        b = self.P.buf(name)
        keep = []
        for (s2, e2, b2) in self.hist:
            if s2 < e and s < e2:
                b.readers.extend(b2.lastw)
                b.readers.extend(b2.readers)
                if s <= s2 and e2 <= e:
                    continue
            keep.append((s2, e2, b2))
        keep.append((s, e, b))
        self.hist = keep
        return v, b


T = 2048
D = 1024
NMEM = 256
DFF = 2816
P0W = 3088
P1W = 4112
ALPHA = (2.0 * 2) ** 0.25
LN_EPS = 1e-5
RMS_EPS = 1e-6
BIG = 1.0e30

C_ID, C_PERM, C_U, C_TRIS, C_TGTS, C_B1, C_B2, C_B3 = 0, 128, 256, 384, 512, 640, 1152, 1664
C_INVF = 2176
C_ONES = 2177
NCONST = 2305

PV_LN = 0
PV_FCW = 48
PV_FCB = 180
PV_X = 224
PV0_DN, PV0_GN, PV0_LAM, PV0_W2 = 224, 225, 226, 482
PV1_CW, PV1_AL, PV1_DT, PV1_GN = 224, 320, 328, 336
NPV = 768
NPV1 = 384


def make_consts():
    c = np.zeros((128, NCONST), np.float32)
    i = np.arange(128)
    c[:, C_ID:C_ID + 128] = np.eye(128)
    perm = np.zeros((128, 128), np.float32)
    for fo in range(128):
        r = fo % 64
        if r < 8:
            perm[fo + 8, fo] = -1.0
        elif r < 16:
            perm[fo - 8, fo] = 1.0
    c[:, C_PERM:C_PERM + 128] = perm
    le = (i[:, None] <= i[None, :]).astype(np.float32)
    c[:, C_U:C_U + 128] = le
    c[:, C_TRIS:C_TRIS + 128] = le * (-1.0 / 16.0)
    c[:, C_TGTS:C_TGTS + 128] = (i[:, None] > i[None, :]).astype(np.float32) * (-1.0 / 16.0)
    b1 = BIG * (i[None, :] >= i[:, None])
    b2 = -BIG * (i[None, :] <= i[:, None])
    b3 = -BIG * (i[None, :] < i[:, None])
    c[:, C_B1:C_B1 + 512] = np.tile(b1, (1, 4))
    c[:, C_B2:C_B2 + 512] = np.tile(b2, (1, 4))
    c[:, C_B3:C_B3 + 512] = np.tile(b3, (1, 4))
    invf = 500000.0 ** (-np.arange(0, 16, 2, dtype=np.float32) / 16.0)
    col = np.zeros(128, np.float32)
    for p in range(128):
        r = p % 64
        if r < 16:
            col[p] = invf[r % 8]
    c[:, C_INVF] = col
    c[:, C_ONES:C_ONES + 128] = 1.0
    return c


def chunkcols(v):
    return np.ascontiguousarray(v.reshape(-1, 128).T)


def make_pv(inp, l):
    pv = np.zeros((128, NPV), np.float32)
    lnn = (["ln1_g_0", "ln1_b_0", "ln2_g_0", "ln2_b_0", "ln3_g_0", "ln3_b_0"] if l == 0 else
           ["ln1_g_1", "ln1_b_1", "ln2_g_1", "ln2_b_1", "ln3_g_1", "ln3_b_1"])
    for k, nm in enumerate(lnn):
        pv[:, PV_LN + 8 * k:PV_LN + 8 * k + 8] = chunkcols(inp[nm])
    cw = inp[f"ffn_conv_w_{l}"]
    for j in range(3):
        pv[:, PV_FCW + j:PV_FCW + 132:3] = chunkcols(cw[j])
    pv[:, PV_FCB:PV_FCB + 44] = chunkcols(inp[f"ffn_conv_b_{l}"])
    if l == 0:
        pv[:, PV0_DN] = inp["diff_norm_0"]
        pv[:, PV0_GN] = inp["gla_norm_0"]
        pv[:, PV0_LAM:PV0_LAM + 256] = inp["diff_lambda_0"].reshape(1, 256)
        pv[0:16, PV0_W2:PV0_W2 + 256] = inp["gla_w2_0"]
        pv[16, PV0_W2:PV0_W2 + 256] = inp["gla_b2_0"]
    else:
        gw = inp["gdn_conv_w_1"]
        for j in range(4):
            pv[:, PV1_CW + j:PV1_CW + 96:4] = chunkcols(gw[j])
        pv[:, PV1_AL:PV1_AL + 8] = inp["gdn_a_log_1"][None, :]
        pv[:, PV1_DT:PV1_DT + 8] = inp["gdn_dt_bias_1"][None, :]
        pv[:, PV1_GN] = inp["gdn_norm_1"]
    return pv


WNAMES = ["w_in_0", "w_mix_out_0", "xa_wq_0", "xa_wkv_0", "xa_wo_0", "ffn_w_in_0", "ffn_w_out_0",
          "w_in_1", "w_mix_out_1", "xa_wq_1", "xa_wkv_1", "xa_wo_1", "ffn_w_in_1", "ffn_w_out_1"]
WSHAPES = {"w_in_0": (D, P0W), "w_in_1": (D, P1W)}
for _l in range(2):
    WSHAPES[f"w_mix_out_{_l}"] = (D, D)
    WSHAPES[f"xa_wq_{_l}"] = (D, D)
    WSHAPES[f"xa_wkv_{_l}"] = (D, 2 * D)
    WSHAPES[f"xa_wo_{_l}"] = (D, D)
    WSHAPES[f"ffn_w_in_{_l}"] = (D, 2 * DFF)
    WSHAPES[f"ffn_w_out_{_l}"] = (DFF, D)


def build(stop=None, dbg=False):
    nc = bass.Bass("TRN2", target_bir_lowering=False)
    dr = {}
    dr["x"] = nc.dram_tensor("x", [T, D], F32, kind="ExternalInput").ap()
    dr["mem"] = nc.dram_tensor("mem", [NMEM, D], F32, kind="ExternalInput").ap()
    dr["pos"] = nc.dram_tensor("pos", [1, T], I32, kind="ExternalInput").ap()
    dr["consts"] = nc.dram_tensor("consts", [128, NCONST], F32, kind="ExternalInput").ap()
    dr["pv0"] = nc.dram_tensor("pv0", [128, NPV], F32, kind="ExternalInput").ap()
    dr["pv1"] = nc.dram_tensor("pv1", [128, NPV], F32, kind="ExternalInput").ap()
    for n in WNAMES:
        dr[n] = nc.dram_tensor(n, list(WSHAPES[n]), F32, kind="ExternalInput").ap()
    out_d = nc.dram_tensor("out", [T, D], F32, kind="ExternalOutput").ap()
    res_d = nc.dram_tensor("resT", [8, 128, T], F32, kind=("ExternalOutput" if dbg else "Internal")).ap()
    sc_v = nc.dram_tensor("sc_v", [8, 128, T], BF16, kind="Internal").ap()
    sc_z = nc.dram_tensor("sc_z", [8, 128, T], BF16, kind="Internal").ap()
    sc_q = nc.dram_tensor("sc_q", [8, 128, T], BF16, kind="Internal").ap()
    sc_k = nc.dram_tensor("sc_k", [8, 128, T], BF16, kind="Internal").ap()
    if dbg:
        dbg_o = nc.dram_tensor("dbg_o", [8, 128, T], BF16, kind="ExternalOutput").ap()

    with ExitStack() as st:
        P = Prog(nc, st)
        hTb = P.sb("hTb", [128, 8, T], BF16)
        hb = [P.buf(f"hb{b}") for b in range(4)]
        cst = P.sb("cst", [128, NCONST], F32)
        b_cst = P.buf("cst")
        cbf = P.sb("cbf", [128, 128 * 4], BF16)
        b_cbf = P.buf("cbf")
        ID_BF = cbf[:, 0:128]
        PERM_BF = cbf[:, 128:256]
        ONES_BF = cbf[:, 256:384]
        U_BF = cbf[:, 384:512]
        ID_F = cst[:, C_ID:C_ID + 128]
        ONES_F = cst[:, C_ONES:C_ONES + 128]
        U_F = cst[:, C_U:C_U + 128]
        small = P.sb("small", [128, 16], F32)
        b_small = P.buf("small")
        C_EPSLN = small[:, 0:1]
        C_EPSRMS = small[:, 1:2]
        C_ONE = small[:, 2:3]
        C_ZERO = small[:, 3:4]
        pv = [P.sb("pv0s", [128, NPV], F32), P.sb("pv1s", [128, NPV1], F32)]
        memT = P.sb("memT", [128, 8, NMEM], BF16)
        memTb = P.buf("memT")
        b_pv = [P.buf("pv0"), P.buf("pv1")]
        RING_SLOT = 16 * 1024
        ring = P.sb("ring", [128, 2 * RING_SLOT], U8)
        ring_b = [P.buf("ring0"), P.buf("ring1")]
        ring_i = [0]
        ARENA_BYTES = 122 * 1024
        arena_t = P.sb("arena", [128, ARENA_BYTES], U8)
        A = Arena(P, arena_t, ARENA_BYTES)
        pst = [P.ps(f"ps{k}", [128, 1024], F32) for k in range(4)]
        pb = [P.buf(f"pb{i}") for i in range(8)]
        for _b in pb:
            _b.excl = True
        bank_i = [0]
        pair_i = [0]

        def bank():
            i = bank_i[0] % 8
            bank_i[0] += 1
            return pst[i // 2][:, (i % 2) * 512:(i % 2) * 512 + 512], pb[i]

        def bank_at(i):
            return pst[i // 2][:, (i % 2) * 512:(i % 2) * 512 + 512], pb[i]

        def pair():
            k = pair_i[0] % 4
            pair_i[0] += 1
            return pst[k][:, :], [pb[2 * k], pb[2 * k + 1]]

        resb = [P.buf(f"res{b}") for b in range(4)]
        outb = P.buf("outd")
        scvb = P.buf("scv")
        sczb = P.buf("scz")
        scqb = P.buf("scq")
        sckb = P.buf("sck")

        cost = {"pe": 0.0, "act": 0.0, "dve": 0.0, "pool": 0.0}
        build.cost = cost

        def fsz(ap):
            n = 1
            for d in ap.shape[1:]:
                n *= d
            return n

        def MM(out, lhsT, rhs, s, e, R, W, **kw):
            cost["pe"] += max(fsz(out), 64) / 2.4e3 * (4 if rhs.dtype == F32 else 1) + 0.01
            P.op("pe", lambda en: en.matmul(out, lhsT=lhsT, rhs=rhs, start=s, stop=e, **kw), R, W)

        def TR(out, in_, ident, R, W):
            P.op("pe", lambda en: en.transpose(out=out, in_=in_, identity=ident), R, W)

        def ACT(out, in_, func, R, W, bias=None, scale=1.0):
            cost["act"] += fsz(out) / 1.2e3 + 0.22
            if bias is None:
                P.op("act", lambda en: en.activation(out=out, in_=in_, func=func, scale=scale), R, W)
            else:
                P.op("act", lambda en: en.activation(out=out, in_=in_, func=func, bias=bias, scale=scale), R, W)

        def TT(eng, out, a, b, op, R, W):
            cost[eng] += fsz(out) / 0.96e3 + 0.1
            P.op(eng, lambda en: en.tensor_tensor(out=out, in0=a, in1=b, op=op), R, W)

        def TS(eng, out, a, s1, s2, op0, op1, R, W):
            cost[eng] += fsz(out) / 0.96e3 + 0.1
            if s2 is None:
                P.op(eng, lambda en: en.tensor_scalar(out=out, in0=a, scalar1=s1, scalar2=None, op0=op0), R, W)
            else:
                P.op(eng, lambda en: en.tensor_scalar(out=out, in0=a, scalar1=s1, scalar2=s2, op0=op0, op1=op1), R, W)

        def STT(eng, out, a, s, b, op0, op1, R, W):
            cost[eng] += fsz(out) / 0.96e3 + 0.1
            P.op(eng, lambda en: en.scalar_tensor_tensor(out=out, in0=a, scalar=s, in1=b, op0=op0, op1=op1), R, W)

        def CP(eng, out, in_, R, W):
            cost[eng] += fsz(out) / (1.2e3 if eng == "act" else 0.96e3) + (0.22 if eng == "act" else 0.1)
            if eng == "act":
                P.op("act", lambda en: en.copy(out=out, in_=in_), R, W)
            else:
                P.op(eng, lambda en: en.tensor_copy(out=out, in_=in_), R, W)

        def MEMSET(eng, ap, val, W):
            P.op(eng, lambda en: en.memset(ap, val), (), W)

        def bc(ap2, n):
            return ap2.unsqueeze(2).to_broadcast([ap2.shape[0], ap2.shape[1], n])

        def bm(ap2, h):
            return ap2.unsqueeze(1).to_broadcast([ap2.shape[0], h, ap2.shape[1]])

        def wload(name, k0, nk, c0, ncols, dcol=0, slot=None, newslot=True):
            if newslot:
                ring_i[0] += 1
            si = ring_i[0] % 2
            base = si * RING_SLOT + dcol
            nbytes = nk * ncols * 2
            assert dcol + nbytes <= RING_SLOT
            v = ring[:, base:base + nbytes].bitcast(BF16).rearrange("p (a b) -> p a b", a=nk)
            src = dr[name][k0 * 128:(k0 + nk) * 128, c0:c0 + ncols].rearrange("(c p) n -> p c n", p=128)
            pairs = []
            step = max(1, 1024 // 128 // 1)
            kk = 0
            while kk < nk:
                k2 = min(nk, kk + 8)
                pairs.append((v[:, kk:k2, :], src[:, kk:k2, :]))
                kk = k2
            P.dma_group("pool", pairs, ring_b[si], writes=[ring_b[si]])
            return v, ring_b[si]

        P.dma("sp", cst[:], dr["consts"], b_cst, writes=[b_cst])
        P.dma("sp", pv[0][:], dr["pv0"], b_pv[0], writes=[b_pv[0]])
        P.dma("sp", pv[1][:], dr["pv1"][:, 0:NPV1], b_pv[1], writes=[b_pv[1]])
        CP("dve", cbf[:, 0:128], cst[:, C_ID:C_ID + 128], [b_cst], [b_cbf])
        CP("dve", cbf[:, 128:256], cst[:, C_PERM:C_PERM + 128], [b_cst], [b_cbf])
        CP("dve", cbf[:, 256:384], cst[:, C_ONES:C_ONES + 128], [b_cst], [b_cbf])
        CP("dve", cbf[:, 384:512], cst[:, C_U:C_U + 128], [b_cst], [b_cbf])
        MEMSET("dve", small[:, 0:1], LN_EPS, [b_small])
        MEMSET("dve", small[:, 1:2], RMS_EPS, [b_small])
        MEMSET("dve", small[:, 2:3], 1.0, [b_small])
        MEMSET("dve", small[:, 3:4], 0.0, [b_small])
        MEMSET("dve", small[:, 4:5], math.pi / 2, [b_small])
        C_HPI = small[:, 4:5]

        def res_ap(b):
            return res_d.rearrange("c p t -> p c t")[:, :, b * 512:(b + 1) * 512]

        def stage_in():
            m0 = A.mark()
            xin = [A.alloc([128, D], F32, f"xin{i}") for i in range(4)]
            xTf = [A.alloc([128, 8, 512], F32, f"xTf{i}") for i in range(2)]
            for b in range(int(os.environ.get("NBLK", "4"))):
                xt, xtb = xTf[b % 2]
                for ti in range(int(os.environ.get("NTI", "4"))):
                    i = 4 * b + ti
                    xi, xib = xin[i % 4]
                    P.dma("sp", xi, dr["x"][i * 128:(i + 1) * 128, :], xib, writes=[xib])
                    for g in range(2):
                        bk, bkb = bank()
                        for k in range(4):
                            TR(bk[:, k * 128:(k + 1) * 128], xi[:, (4 * g + k) * 128:(4 * g + k + 1) * 128], ID_F,
                               [xib, b_cst], [bkb])
                        bk3 = bk.rearrange("p (a b) -> p a b", a=4)
                        if True:
                            CP("act", hTb[:, 4 * g:4 * g + 4, i * 128:(i + 1) * 128], bk3, [bkb], [hb[b]])
                        CP("dve", xt[:, 4 * g:4 * g + 4, ti * 128:(ti + 1) * 128], bk3, [bkb], [xtb])
                P.dma("act", res_ap(b), xt, xtb, reads=[xtb], writes=[resb[b]])
            for i in range(2):
                xi, xib = xin[i]
                P.dma("sp", xi, dr["mem"][i * 128:(i + 1) * 128, :], xib, writes=[xib])
                for g in range(2):
                    bk, bkb = bank()
                    for k in range(4):
                        TR(bk[:, k * 128:(k + 1) * 128], xi[:, (4 * g + k) * 128:(4 * g + k + 1) * 128], ID_F, [xib, b_cst], [bkb])
                    CP("act", memT[:, 4 * g:4 * g + 4, i * 128:(i + 1) * 128], bk.rearrange("p (a b) -> p a b", a=4), [bkb], [memTb])
            A.release(m0)

        def ln_alloc(n):
            out = []
            for i in range(n):
                out.append(dict(zb=A.alloc([128, 8, 512], BF16, f"ln_zb{i}"), zq=A.alloc([128, 8, 512], BF16, f"ln_zq{i}"),
                                mm=A.alloc([128, 512], F32, f"ln_m{i}"), vv=A.alloc([128, 512], F32, f"ln_v{i}"),
                                rs=A.alloc([128, 512], F32, f"ln_r{i}"), nm=A.alloc([128, 512], F32, f"ln_nm{i}")))
            return out

        def ln_head(tmp, z, zb_):
            (zb, zbb), (zq, zqb) = tmp["zb"], tmp["zq"]
            CP("act", zb, z, [zb_], [zbb])
            ACT(zq, z, AF.Square, [zb_], [zqb])

        def ln_tail(tmp, b, z, zb_, lcol, l, final=False, finalbuf=None):
            (zb, zbb), (zq, zqb), (mm_, mmb), (vv, vvb), (rs, rsb), (nm, nmb) = (tmp[k] for k in ("zb", "zq", "mm", "vv", "rs", "nm"))
            s1, s1b = bank()
            s2, s2b = bank()
            for j in range(8):
                MM(s1, ONES_BF, zb[:, j, :], j == 0, j == 7, [b_cbf, zbb], [s1b])
            for j in range(8):
                MM(s2, ONES_BF, zq[:, j, :], j == 0, j == 7, [b_cbf, zqb], [s2b])
            TS("dve", mm_, s1, 1.0 / D, None, ALU.mult, None, [s1b], [mmb])
            TT("dve", vv, mm_, mm_, ALU.mult, [mmb], [vvb])
            STT("dve", vv, s2, 1.0 / D, vv, ALU.mult, ALU.subtract, [s2b, vvb], [vvb])
            ACT(rs, vv, AF.Ln, [vvb, b_small], [rsb], bias=C_EPSLN)
            ACT(rs, rs, AF.Exp, [rsb], [rsb], scale=-0.5)
            TT("dve", nm, mm_, rs, ALU.mult, [mmb, rsb], [nmb])
            TT("dve", z, z, bm(rs, 8), ALU.mult, [zb_, rsb], [zb_])
            TT("dve", z, z, bm(nm, 8), ALU.subtract, [zb_, nmb], [zb_])
            gc_ = pv[l][:, PV_LN + 16 * lcol:PV_LN + 16 * lcol + 8]
            bc_ = pv[l][:, PV_LN + 16 * lcol + 8:PV_LN + 16 * lcol + 16]
            for j in range(8):
                ACT(z[:, j, :], z[:, j, :], AF.Identity, [zb_, b_pv[l]], [zb_], bias=bc_[:, j:j + 1], scale=gc_[:, j:j + 1])
            if not final:
                CP("dve", hTb[:, :, b * 512:(b + 1) * 512], z, [zb_], [hb[b]])
                P.dma("sp", res_ap(b), z, zb_, reads=[zb_], writes=[resb[b]])
            else:
                for ti in range(4):
                    ot, otb = finalbuf[ti % 2]
                    for g in range(2):
                        bk, bkb = bank()
                        for k in range(4):
                            TR(bk[:, k * 128:(k + 1) * 128], z[:, 4 * g + k, ti * 128:(ti + 1) * 128], ID_F, [zb_, b_cst], [bkb])
                        if g == 0:
                            CP("act", ot[:, 0:512], bk, [bkb], [otb])
                        else:
                            CP("dve", ot[:, 512:1024], bk, [bkb], [otb])
                    r0 = b * 512 + ti * 128
                    P.dma("sp", out_d[r0:r0 + 128, :], ot, otb, reads=[otb], writes=[outb])

        def ln_block(b, z, zb_, lcol, l, final=False, finalbuf=None):
            m0 = A.mark()
            tmp = ln_alloc(1)[0]
            ln_head(tmp, z, zb_)
            ln_tail(tmp, b, z, zb_, lcol, l, final=final, finalbuf=finalbuf)
            A.release(m0)

        def proj_z(b, nk, lhs_fn, rhs_fn, z, zb_):
            P.dma("sp", z, res_ap(b), zb_, reads=[resb[b]], writes=[zb_])
            for j in range(8):
                bk, bkb = bank()
                for c in range(nk):
                    la, lr = lhs_fn(c, j)
                    ra, rr = rhs_fn(c, b)
                    MM(bk, la, ra, c == 0, c == nk - 1, lr + rr, [bkb])
                STT("dve", z[:, j, :], z[:, j, :], ALPHA, bk, ALU.mult, ALU.add, [zb_, bkb], [zb_])

        def proj_res_ln(l, lcol, nk, lhs_fn, rhs_fn, blocks, zbufs, final=False, finalbuf=None):
            m0 = A.mark()
            tmps = ln_alloc(2)
            pend = None
            for b in blocks:
                z, zb_ = zbufs[b % len(zbufs)]
                proj_z(b, nk, lhs_fn, rhs_fn, z, zb_)
                ln_head(tmps[b % 2], z, zb_)
                if pend is not None:
                    ln_tail(*pend)
                pend = (tmps[b % 2], b, z, zb_, lcol, l, final, finalbuf)
            ln_tail(*pend)
            A.release(m0)

        def stage_xattn(l):
            m0 = A.mark()
            kT, kTb = A.alloc([128, 8, NMEM], BF16, "xkT")
            vt, vtb = A.alloc([128, 2, D], BF16, "xv")
            w, wb = wload(f"xa_wkv_{l}", 0, 8, 0, D)
            for jj in range(8):
                bk, bkb = bank()
                for c in range(8):
                    MM(bk[:, 0:NMEM], w[:, c, jj * 128:(jj + 1) * 128], memT[:, c, :], c == 0, c == 7, [wb, memTb], [bkb])
                CP("act", kT[:, jj, :], bk[:, 0:NMEM], [bkb], [kTb])
            w, wb = wload(f"xa_wkv_{l}", 0, 8, D, D)
            for gi in range(2):
                for i in range(2):
                    bk, bkb = bank()
                    for c in range(8):
                        MM(bk, memT[:, c, i * 128:(i + 1) * 128], w[:, c, gi * 512:(gi + 1) * 512], c == 0, c == 7, [wb, memTb], [bkb])
                    CP("act", vt[:, i, gi * 512:(gi + 1) * 512], bk, [bkb], [vtb])
            wq, wqb = wload(f"xa_wq_{l}", 0, 8, 0, D)
            wo, wob = wload(f"xa_wo_{l}", 0, 8, 0, D)
            qT = [A.alloc([128, 8, 512], BF16, f"xq{i}") for i in range(2)]
            oT = [A.alloc([128, 8, 512], BF16, f"xo{i}") for i in range(1)]
            E = [A.alloc([128, 2, 512], BF16, f"xE{i}") for i in range(2)]
            rsm = [A.alloc([128, 512], F32, f"xr{i}") for i in range(2)]
            zbufs = [A.alloc([128, 8, 512], F32, f"xz{i}") for i in range(2)]
            tmps = ln_alloc(2)
            pend = None
            def qproj(b, js):
                q, qb_ = qT[b % 2]
                for j in js:
                    bk, bkb = bank()
                    for c in range(8):
                        MM(bk, wq[:, c, j * 128:(j + 1) * 128], hTb[:, c, b * 512:(b + 1) * 512], c == 0, c == 7, [wqb, hb[b]], [bkb])
                    CP("act", q[:, j, :], bk, [bkb], [qb_])
            qproj(0, range(8))
            for b in range(4):
                q, qb_ = qT[b % 2]
                o, ob_ = oT[0]

                def xs(h, q=q, qb_=qb_):
                    e, eb_ = E[h % 2]
                    for kt in range(2):
                        bk, bkb = bank()
                        for dc in range(2):
                            MM(bk, kT[:, 2 * h + dc, kt * 128:(kt + 1) * 128], q[:, 2 * h + dc, :], dc == 0, dc == 1, [kTb, qb_], [bkb])
                        ACT(e[:, kt, :], bk, AF.Exp, [bkb], [eb_], scale=1.0 / 16.0)

                def xa(h, o=o, ob_=ob_):
                    e, eb_ = E[h % 2]
                    r, rb_ = rsm[h % 2]
                    sm, smb = bank()
                    for kt in range(2):
                        MM(sm, ONES_BF, e[:, kt, :], kt == 0, kt == 1, [b_cbf, eb_], [smb])
                    P.op("dve", lambda en, r=r, sm=sm: en.reciprocal(out=r, in_=sm), [smb], [rb_])
                    for dvc in range(2):
                        bk, bkb = bank()
                        for kt in range(2):
                            MM(bk, vt[:, kt, h * 256 + dvc * 128:h * 256 + dvc * 128 + 128], e[:, kt, :], kt == 0, kt == 1, [vtb, eb_], [bkb])
                        TT("dve", o[:, 2 * h + dvc, :], bk, r, ALU.mult, [bkb, rb_], [ob_])
                xs(0)
                for h in range(4):
                    if b + 1 < 4:
                        qproj(b + 1, [2 * h, 2 * h + 1])
                    if h + 1 < 4:
                        xs(h + 1)
                    xa(h)
                z, zb_ = zbufs[b % 2]
                proj_z(b, 8,
                       lambda c, j: (wo[:, c, j * 128:(j + 1) * 128], [wob]),
                       lambda c, bb, o=o, ob_=ob_: (o[:, c, :], [ob_]),
                       z, zb_)
                ln_head(tmps[b % 2], z, zb_)
                if pend is not None:
                    ln_tail(*pend)
                pend = (tmps[b % 2], b, z, zb_, 1, l)
            ln_tail(*pend)
            A.release(m0)

        def stage_ffn2(l, final=False):
            m0 = A.mark()
            halo, halob = A.alloc([128, 44, 2], F32, "halo")
            MEMSET("dve", halo, 0.0, [halob])
            acth, acthb = A.alloc([128, 22, 1024], BF16, "acth")
            cw = pv[l][:, PV_FCW:PV_FCW + 132]
            cb = pv[l][:, PV_FCB:PV_FCB + 44]
            groups = [(g * 4, 4) for g in range(5)] + [(20, 2)]
            pi = 0
            pendf = [None]
            for hf in range(2):
                mh = A.mark()
                NPB = 3
                pre = [[A.alloc([128, 1026], F32, f"pre{i}{k}") for k in range(2)] for i in range(NPB)]
                cv = [[A.alloc([128, 1024], F32, f"cv{i}{k}") for k in range(2)] for i in range(NPB)]
                tpl = [A.alloc([128, 1024], F32, f"tp{i}") for i in range(2)] if os.environ.get("POOLCONV", "0") == "1" else None
                for (c0, ncn) in groups:
                    w, wb = wload(f"ffn_w_in_{l}", 0, 8, c0 * 128, ncn * 128)
                    w2_, _ = wload(f"ffn_w_in_{l}", 0, 8, DFF + c0 * 128, ncn * 128, dcol=8 * ncn * 128 * 2, newslot=False)
                    for cc in range(ncn):
                        ch = c0 + cc
                        pr = pre[pi % NPB]
                        cvv = cv[pi % NPB]
                        pi += 1
                        for k, (wk, chk) in enumerate(((w, ch), (w2_, ch + 22))):
                            p_, pb_ = pr[k]
                            y_, yb_ = cvv[k]
                            CP("act", p_[:, 0:2], halo[:, chk, :], [halob], [pb_])
                            for bb in range(2):
                                bk, bkb = bank()
                                blk = 2 * hf + bb
                                for c in range(8):
                                    MM(bk, wk[:, c, cc * 128:(cc + 1) * 128], hTb[:, c, blk * 512:(blk + 1) * 512], c == 0, c == 7, [wb, hb[blk]], [bkb])
                                CP("act", p_[:, 2 + bb * 512:2 + (bb + 1) * 512], bk, [bkb], [pb_])
                            CP("act", halo[:, chk, :], p_[:, 1024:1026], [pb_], [halob])
                            ce = "pool" if (k == 1 and os.environ.get("POOLCONV", "0") == "1") else "dve"
                            ACT(y_, p_[:, 2:1026], AF.Identity, [pb_, b_pv[l]], [yb_], bias=cb[:, chk:chk + 1], scale=cw[:, chk * 3 + 2:chk * 3 + 3])
                            if ce == "dve":
                                STT("dve", y_, p_[:, 1:1025], cw[:, chk * 3 + 1:chk * 3 + 2], y_, ALU.mult, ALU.add, [pb_, b_pv[l], yb_], [yb_])
                                STT("dve", y_, p_[:, 0:1024], cw[:, chk * 3:chk * 3 + 1], y_, ALU.mult, ALU.add, [pb_, b_pv[l], yb_], [yb_])
                            else:
                                tp_, tpb_ = tpl[pi % 2]
                                for jj in (1, 0):
                                    TS("pool", tp_, p_[:, jj:1024 + jj], cw[:, chk * 3 + jj:chk * 3 + jj + 1], None, ALU.mult, None, [pb_, b_pv[l]], [tpb_])
                                    TT("pool", y_, y_, tp_, ALU.add, [yb_, tpb_], [yb_])
                        if pendf[0] is not None:
                            pendf[0]()

                        def fin(cvv=cvv, ch=ch):
                            g_, gb_ = cvv[0]
                            u_, ub_ = cvv[1]
                            ACT(g_, g_, AF.Silu, [gb_], [gb_])
                            TT("dve", acth[:, ch, :], g_, u_, ALU.mult, [gb_, ub_], [acthb])
                        pendf[0] = fin
                pendf[0]()
                pendf[0] = None
                A.release(mh)
                zh = [A.alloc([128, 8, 512], F32, f"fz{i}") for i in range(2)]
                finalbuf = None
                if final:
                    finalbuf = [A.alloc([128, D], F32, f"fo{i}") for i in range(2)]
                for bb in range(2):
                    blk = 2 * hf + bb
                    z, zb_ = zh[bb]
                    P.dma("sp", z, res_ap(blk), zb_, reads=[resb[blk]], writes=[zb_])
                for jg in range(4):
                    w, wb = wload(f"ffn_w_out_{l}", 0, 22, jg * 256, 256)
                    for j2 in range(2):
                        j = jg * 2 + j2
                        for bb in range(2):
                            z, zb_ = zh[bb]
                            bk, bkb = bank()
                            for c in range(22):
                                MM(bk, w[:, c, j2 * 128:(j2 + 1) * 128], acth[:, c, bb * 512:(bb + 1) * 512], c == 0, c == 21, [wb, acthb], [bkb])
                            STT("dve", z[:, j, :], z[:, j, :], ALPHA, bk, ALU.mult, ALU.add, [zb_, bkb], [zb_])
                for bb in range(2):
                    blk = 2 * hf + bb
                    z, zb_ = zh[bb]
                    ln_block(blk, z, zb_, 2, l, final=final, finalbuf=finalbuf)
                A.release(mh)
            A.release(m0)

        def stage_mixer0():
            l = 0
            m0 = A.mark()
            oT, oTb = A.alloc([128, 8, T], BF16, "oT")
            m1 = A.mark()
            aqT, aqb = A.alloc([128, 4, T], BF16, "aqT")
            akT, akb = A.alloc([128, 4, T], BF16, "akT")
            avt, avb = A.alloc([128, 16, 512], BF16, "avt")
            lamt, lamb = A.alloc([128, 8], F32, "lam")
            m2 = A.mark()
            cosT, cosb = A.alloc([128, T], F32, "cosT")
            sinT, sinb = A.alloc([128, T], F32, "sinT")
            m3 = A.mark()
            posi, posib = A.alloc([128, T], I32, "posi")
            ang, angb = A.alloc([128, T], F32, "ang")
            tmp, tmpb = A.alloc([128, T], F32, "angt")
            P.dma("sp", posi, dr["pos"].partition_broadcast(128), posib, writes=[posib])
            CP("dve", ang, posi, [posib], [angb])
            TS("dve", ang, ang, cst[:, C_INVF:C_INVF + 1], None, ALU.mult, None, [angb, b_cst], [angb])
            MAGIC = 12582912.0
            for (dst, dstb, shift) in ((sinT, sinb, 0.0), (cosT, cosb, math.pi / 2)):
                TS("dve", tmp, ang, shift, 1.0 / (2 * math.pi), ALU.add, ALU.mult, [angb], [tmpb])
                TS("dve", tmp, tmp, MAGIC, None, ALU.add, None, [tmpb], [tmpb])
                TS("dve", tmp, tmp, -MAGIC, None, ALU.add, None, [tmpb], [tmpb])
                STT("dve", tmp, tmp, -2 * math.pi, ang, ALU.mult, ALU.add, [tmpb, angb], [tmpb])
                TS("dve", tmp, tmp, shift, None, ALU.add, None, [tmpb], [tmpb])
                TS("dve", tmp, tmp, math.pi, -math.pi, ALU.min, ALU.max, [tmpb], [tmpb])
                ACT(dst, tmp, AF.Sin, [tmpb], [dstb])
            A.release(m3)
            if os.environ.get("MS") == "rope":
                return
            lam_init = 0.8 - 0.6 * math.exp(-0.3 * 0)
            lp, lpb = A.alloc([128, 128], F32, "lamp")
            lv = pv[0][:, PV0_LAM:PV0_LAM + 256]
            TT("dve", lp[:, 0:64], lv[:, 0:64], lv[:, 64:128], ALU.mult, [b_pv[0]], [lpb])
            TT("dve", lp[:, 64:128], lv[:, 128:192], lv[:, 192:256], ALU.mult, [b_pv[0]], [lpb])
            P.op("dve", lambda en: en.tensor_reduce(out=lamt[:, 0:2], in_=lp.rearrange("p (a b) -> p a b", a=2), axis=AX.X, op=ALU.add), [lpb], [lamb])
            ACT(lamt[:, 2:4], lamt[:, 0:2], AF.Exp, [lamb], [lamb])
            TT("dve", lamt[:, 4:5], lamt[:, 2:3], lamt[:, 3:4], ALU.subtract, [lamb], [lamb])
            TS("dve", lamt[:, 4:5], lamt[:, 4:5], lam_init, None, ALU.add, None, [lamb], [lamb])
            TS("dve", lamt[:, 5:6], lamt[:, 4:5], -1.0, None, ALU.mult, None, [lamb], [lamb])
            TS("dve", lamt[:, 6:7], pv[0][:, PV0_DN:PV0_DN + 1], 1.0 - lam_init, None, ALU.mult, None, [b_pv[0]], [lamb])
            LAMNEG = lamt[:, 5:6]
            DGAIN = lamt[:, 6:7]
            if os.environ.get("MS") == "lam":
                return
            xbf = [A.alloc([128, 512], BF16, f"xbf{i}") for i in range(2)]
            t1 = [A.alloc([128, 512], F32, f"rt1{i}") for i in range(2)]
            t2 = [A.alloc([128, 512], F32, f"rt2{i}") for i in range(2)]
            ii = 0
            for (c0, dst, dstb, scale) in ((0, aqT, aqb, 0.125), (512, akT, akb, 1.0)):
                w, wb = wload("w_in_0", 0, 8, c0, 512)
                for h in range(4):
                    for b in range(4):
                        bk, bkb = bank()
                        for c in range(8):
                            MM(bk, w[:, c, h * 128:(h + 1) * 128], hTb[:, c, b * 512:(b + 1) * 512], c == 0, c == 7, [wb, hb[b]], [bkb])
                        xb_, xbb = xbf[ii % 2]
                        a1, a1b = t1[ii % 2]
                        a2, a2b = t2[ii % 2]
                        ii += 1
                        CP("act", xb_, bk, [bkb], [xbb])
                        bk2, bk2b = bank()
                        MM(bk2, PERM_BF, xb_, True, True, [b_cbf, xbb], [bk2b])
                        STT("dve", a1, bk, scale, cosT[:, b * 512:(b + 1) * 512], ALU.mult, ALU.mult, [bkb, cosb], [a1b])
                        STT("dve", a2, bk2, scale, sinT[:, b * 512:(b + 1) * 512], ALU.mult, ALU.mult, [bk2b, sinb], [a2b])
                        TT("dve", dst[:, h, b * 512:(b + 1) * 512], a1, a2, ALU.add, [a1b, a2b], [dstb])
            A.release(m2)
            if os.environ.get("MS") == "p0a":
                return
            w, wb = wload("w_in_0", 0, 8, 1024, 512)
            for i in range(16):
                bk, bkb = bank()
                for c in range(8):
                    MM(bk, hTb[:, c, i * 128:(i + 1) * 128], w[:, c, :], c == 0, c == 7, [wb, hb[i // 4]], [bkb])
                CP("act", avt[:, i, :], bk, [bkb], [avb])
            if os.environ.get("MS") == "av":
                return
            NE = 3
            Eb = [[A.alloc([128, 512], BF16, f"E{i}{m}") for m in range(2)] for i in range(NE)]
            epi = []
            for i in range(2):
                epi.append(dict(of=A.alloc([128, 512], F32, f"dof{i}"), o2=A.alloc([128, 512], F32, f"do2{i}"),
                                r0=A.alloc([128, 512], F32, f"dr0{i}"), r1=A.alloc([128, 512], F32, f"dr1{i}"),
                                sq=A.alloc([128, 512], BF16, f"dsq{i}"), rs=A.alloc([128, 512], F32, f"drs{i}")))
            steps = [(h, qb, kt) for h in range(4) for qb in range(4) for kt in range(4 * (qb + 1))]
            sums = [bank_at(0), bank_at(1)]
            pvs = [bank_at(2), bank_at(3)]

            def geom(st):
                h, qb, kt = steps[st]
                jd = kt - 4 * qb
                cs = 128 * jd if jd > 0 else 0
                return h, qb, kt, jd, cs, 512 - cs

            def emit_S(st):
                h, qb, kt, jd, cs, n = geom(st)
                q0 = qb * 512
                for m in range(2):
                    bk, bkb = bank_at(4 + 2 * (st % 2) + m)
                    pr = slice(m * 64, (m + 1) * 64)
                    MM(bk[:, 0:n], akT[pr, h, kt * 128:(kt + 1) * 128], aqT[pr, h, q0 + cs:q0 + 512], True, True, [akb, aqb], [bkb])
                    e_, eb_ = Eb[st % NE][m]
                    ACT(e_[:, 0:n], bk[:, 0:n], AF.Exp, [bkb], [eb_])
                    if jd >= 0:
                        TT("dve", e_[:, 0:128], e_[:, 0:128], U_BF, ALU.mult, [eb_, b_cbf], [eb_])

            def emit_A(st):
                h, qb, kt, jd, cs, n = geom(st)
                nkt = 4 * (qb + 1)
                for m in range(2):
                    e_, eb_ = Eb[st % NE][m]
                    sm, smb = sums[m]
                    pv_, pvb = pvs[m]
                    MM(sm[:, cs:512], ONES_BF, e_[:, 0:n], kt == 0, kt == nkt - 1, [b_cbf, eb_], [smb])
                    MM(pv_[:, cs:512], avt[:, kt, h * 128:(h + 1) * 128], e_[:, 0:n], kt == 0, kt == nkt - 1, [avb, eb_], [pvb])

            def emit_epi(st, gi):
                h, qb, kt, jd, cs, n = geom(st)
                q0 = qb * 512
                E_ = epi[gi % 2]
                (of_, ofb), (o2_, o2b), (r0, r0b), (r1, r1b), (sq_, sqb), (rs_, rsb_) = E_["of"], E_["o2"], E_["r0"], E_["r1"], E_["sq"], E_["rs"]
                P.op("dve", lambda en: en.reciprocal(out=r0, in_=sums[0][0]), [sums[0][1]], [r0b])
                P.op("dve", lambda en: en.reciprocal(out=r1, in_=sums[1][0]), [sums[1][1]], [r1b])
                TT("dve", of_, pvs[0][0], r0, ALU.mult, [pvs[0][1], r0b], [ofb])
                TT("dve", o2_, pvs[1][0], r1, ALU.mult, [pvs[1][1], r1b], [o2b])
                STT("dve", of_, o2_, LAMNEG, of_, ALU.mult, ALU.add, [o2b, lamb, ofb], [ofb])
                ACT(sq_, of_, AF.Square, [ofb], [sqb])
                bk, bkb = bank_at(4 + 2 * (st % 2))
                MM(bk, ONES_BF, sq_, True, True, [b_cbf, sqb], [bkb])
                ACT(rs_, bk, AF.Ln, [bkb, b_small], [rsb_], bias=C_EPSRMS, scale=1.0 / 128.0)
                ACT(rs_, rs_, AF.Exp, [rsb_], [rsb_], scale=-0.5)
                STT("dve", oT[:, h, q0:q0 + 512], of_, DGAIN, rs_, ALU.mult, ALU.mult, [ofb, lamb, rsb_], [oTb])

            emit_S(0)
            gi = 0
            for st in range(len(steps)):
                if st + 1 < len(steps):
                    emit_S(st + 1)
                emit_A(st)
                h, qb, kt = steps[st]
                if kt == 4 * (qb + 1) - 1:
                    emit_epi(st, gi)
                    gi += 1
            A.release(m1)
            if os.environ.get("MS") == "attn":
                return
            dec, decb = A.alloc([128, 2, 16], F32, "dec")
            qbT, qbTb = A.alloc([128, 2, T], BF16, "qbT")
            kdT, kdTb = A.alloc([128, 2, T], BF16, "kdT")
            kgt, kgtb = A.alloc([128, 16, 256], BF16, "kgt")
            bvt, bvtb = A.alloc([128, 16, 512], BF16, "bvt")
            m4 = A.mark()
            gaug, gaugb = A.alloc([128, T], F32, "gaug")
            ebT, ebTb = A.alloc([128, 2, T], BF16, "ebT")
            enT, enTb = A.alloc([128, 2, T], BF16, "enT")
            erb, erbb = A.alloc([128, 16, 256], BF16, "erb")
            MEMSET("dve", gaug[0:32, :], 1.0, [gaugb])
            w, wb = wload("w_in_0", 0, 8, 3072, 16)
            for b in range(4):
                bk, bkb = bank()
                for c in range(8):
                    MM(bk[0:16, :], w[:, c, :], hTb[:, c, b * 512:(b + 1) * 512], c == 0, c == 7, [wb, hb[b]], [bkb])
                CP("act", gaug[0:16, b * 512:(b + 1) * 512], bk[0:16, :], [bkb], [gaugb])
            w2aug = pv[0][0:17, PV0_W2:PV0_W2 + 256]
            spb = [A.alloc([128, 256], F32, f"sp{i}") for i in range(2)]
            TRIS = cst[:, C_TRIS:C_TRIS + 128]
            TGTS = cst[:, C_TGTS:C_TGTS + 128]
            for i in range(16):
                sp_, spb_ = spb[i % 2]
                bk, bkb = bank()
                MM(bk[:, 0:256], gaug[0:17, i * 128:(i + 1) * 128], w2aug, True, True, [gaugb, b_pv[0]], [bkb])
                ACT(sp_, bk[:, 0:256], AF.Exp, [bkb], [spb_], scale=-1.0)
                ACT(sp_, sp_, AF.Ln, [spb_, b_small], [spb_], bias=C_ONE)
                bk2, bk2b = bank()
                for c in range(2):
                    MM(bk2[:, c * 128:(c + 1) * 128], sp_[:, c * 128:(c + 1) * 128], TRIS, True, True, [spb_, b_cst], [bk2b])
                b3 = bk2[:, 0:256].rearrange("p (a b) -> p a b", a=2)
                ACT(ebT[:, :, i * 128:(i + 1) * 128], b3, AF.Exp, [bk2b], [ebTb])
                ACT(enT[:, :, i * 128:(i + 1) * 128], b3, AF.Exp, [bk2b], [enTb], scale=-1.0)
                ACT(dec[:, :, i:i + 1], b3[:, :, 127:128], AF.Exp, [bk2b], [decb])
                bk3, bk3b = bank()
                MM(bk3[:, 0:256], TGTS, sp_, True, True, [b_cst, spb_], [bk3b])
                ACT(erb[:, i, :], bk3[:, 0:256], AF.Exp, [bk3b], [erbb])
            if os.environ.get("MS") == "gk":
                return
            w, wb = wload("w_in_0", 0, 8, 1536, 512)
            for jc in range(4):
                for b in range(4):
                    bk, bkb = bank()
                    for c in range(8):
                        MM(bk, w[:, c, jc * 128:(jc + 1) * 128], hTb[:, c, b * 512:(b + 1) * 512], c == 0, c == 7, [wb, hb[b]], [bkb])
                    if jc < 2:
                        STT("dve", qbT[:, jc, b * 512:(b + 1) * 512], bk, 0.125, ebT[:, jc, b * 512:(b + 1) * 512], ALU.mult, ALU.mult, [bkb, ebTb], [qbTb])
                    else:
                        TT("dve", kdT[:, jc - 2, b * 512:(b + 1) * 512], bk, enT[:, jc - 2, b * 512:(b + 1) * 512], ALU.mult, [bkb, enTb], [kdTb])
            for i in range(16):
                bk, bkb = bank()
                for c in range(8):
                    MM(bk[:, 0:256], hTb[:, c, i * 128:(i + 1) * 128], w[:, c, 256:512], c == 0, c == 7, [wb, hb[i // 4]], [bkb])
                TT("dve", kgt[:, i, :], bk[:, 0:256], erb[:, i, :], ALU.mult, [bkb, erbb], [kgtb])
            A.release(m4)
            w, wb = wload("w_in_0", 0, 8, 2048, 512)
            for i in range(16):
                bk, bkb = bank()
                for c in range(8):
                    MM(bk, hTb[:, c, i * 128:(i + 1) * 128], w[:, c, :], c == 0, c == 7, [wb, hb[i // 4]], [bkb])
                CP("act", bvt[:, i, :], bk, [bkb], [bvtb])
            srT, srTb = A.alloc([128, 4, T], BF16, "srT")
            w, wb = wload("w_in_0", 0, 8, 2560, 512)
            for jc in range(4):
                for b in range(4):
                    bk, bkb = bank()
                    for c in range(8):
                        MM(bk, w[:, c, jc * 128:(jc + 1) * 128], hTb[:, c, b * 512:(b + 1) * 512], c == 0, c == 7, [wb, hb[b]], [bkb])
                    ACT(srT[:, jc, b * 512:(b + 1) * 512], bk, AF.Silu, [bkb], [srTb])
            wmo, wmob = wload("w_mix_out_0", 0, 8, 0, D)
            if os.environ.get("MS") == "glaprep":
                return
            S, Sb = A.alloc([128, 2, 128], F32, "glaS")
            Sbf, Sbfb = A.alloc([128, 2, 128], BF16, "glaSb")
            MEMSET("dve", S, 0.0, [Sb])
            CP("act", Sbf, S, [Sb], [Sbfb])
            attm = [A.alloc([128, 4, 128], BF16, f"attm{i}") for i in range(2)]
            gsq, gsqb = A.alloc([128, 512], BF16, "gsq")
            grs, grsb = A.alloc([128, 512], F32, "grs")
            gt_, gtb = A.alloc([128, 512], F32, "gt")
            GG = pv[0][:, PV0_GN:PV0_GN + 1]
            GLS = int(os.environ.get("GLS", "9"))
            for i in range(int(os.environ.get("GLN", "16"))):
                tsl = slice(i * 128, (i + 1) * 128)
                am, amb = attm[i % 2]
                bkA2 = [bank(), bank()]
                for h in range(4):
                    c, hp = h // 2, h % 2
                    pr = slice(hp * 64, (hp + 1) * 64)
                    bkA, bkAb = bkA2[hp]
                    MM(bkA[:, c * 128:(c + 1) * 128], kdT[pr, c, tsl], qbT[pr, c, tsl], True, True, [kdTb, qbTb], [bkAb])
                for hp in range(2):
                    bkA, bkAb = bkA2[hp]
                    TT("dve", am[:, hp::2, :], bkA[:, 0:256].rearrange("p (a b) -> p a b", a=2), bm(U_F, 2), ALU.mult, [bkAb, b_cst], [amb])
                if GLS < 2:
                    continue
                bkB, bkBb = bank()
                for h in range(4):
                    c, hp = h // 2, h % 2
                    pr = slice(hp * 64, (hp + 1) * 64)
                    MM(bkB[:, h * 128:(h + 1) * 128], Sbf[pr, c, :], qbT[pr, c, tsl], True, False, [Sbfb, qbTb], [bkBb])
                    MM(bkB[:, h * 128:(h + 1) * 128], bvt[:, i, h * 128:(h + 1) * 128], am[:, h, :], False, True, [bvtb, amb], [bkBb])
                if GLS < 3:
                    continue
                ACT(gsq, bkB, AF.Square, [bkBb], [gsqb])
                bkC, bkCb = bank()
                MM(bkC, ONES_BF, gsq, True, True, [b_cbf, gsqb], [bkCb])
                ACT(grs, bkC, AF.Ln, [bkCb, b_small], [grsb], bias=C_EPSRMS, scale=1.0 / 128.0)
                ACT(grs, grs, AF.Exp, [grsb], [grsb], scale=-0.5)
                TT("dve", gt_, bkB, grs, ALU.mult, [bkBb, grsb], [gtb])
                STT("dve", oT[:, 4:8, tsl], gt_.rearrange("p (a b) -> p a b", a=4), GG, srT[:, :, tsl], ALU.mult, ALU.mult,
                    [gtb, b_pv[0], srTb], [oTb])
                if i < 15 and GLS >= 4:
                    bkD, bkDb = bank()
                    for c in range(2):
                        MM(bkD[:, c * 256:(c + 1) * 256], kgt[:, i, c * 128:(c + 1) * 128], bvt[:, i, c * 256:(c + 1) * 256], True, True, [kgtb, bvtb], [bkDb])
                    for c in range(2):
                        for hp in range(2):
                            pr = slice(hp * 64, (hp + 1) * 64)
                            STT("dve", S[pr, c, :], S[pr, c, :], dec[pr, c, i:i + 1], bkD[pr, c * 256 + hp * 128:c * 256 + hp * 128 + 128],
                                ALU.mult, ALU.add, [Sb, decb, bkDb], [Sb])
                    CP("act", Sbf, S, [Sb], [Sbfb])
            if dbg and not os.environ.get("NODUMP"):
                for c_ in range(8):
                    P.dma("sp", dbg_o[c_], oT[:, c_, :], oTb, reads=[oTb], writes=[outb])
            A.release(m1)
            if os.environ.get("MS") == "glaloop":
                return
            zbufs = [A.alloc([128, 8, 512], F32, f"mz{i}") for i in range(2)]
            proj_res_ln(0, 0, 8,
                        lambda c, j: (wmo[:, c, j * 128:(j + 1) * 128], [wmob]),
                        lambda c, b: (oT[:, c, b * 512:(b + 1) * 512], [oTb]),
                        range(4), zbufs)
            A.release(m0)

        def stage_mixer1():
            l = 1
            m0 = A.mark()
            ba, bab = A.alloc([128, 16, 16], F32, "gba")
            m1 = A.mark()
            pre = [A.alloc([128, T + 3], F32, f"gpre{i}") for i in range(2)]
            cvb = [A.alloc([128, T], F32, f"gcv{i}") for i in range(2)]
            sqb_ = [A.alloc([128, T], BF16, f"gsq{i}") for i in range(2)]
            rsb2 = [A.alloc([128, 512], F32, f"grs{i}") for i in range(2)]
            stg = [A.alloc([128, T], BF16, f"gst{i}") for i in range(2)]
            cw = pv[1][:, PV1_CW:PV1_CW + 96]
            pi = 0
            pend1 = [None]
            for g in range(6):
                w, wb = wload("w_in_1", 0, 8, g * 512, 512)
                for cc in range(4):
                    ch = g * 4 + cc
                    p_, pb_ = pre[pi % 2]
                    y_, yb_ = cvb[pi % 2]
                    s_, sb_ = sqb_[pi % 2]
                    sg, sgb = stg[pi % 2]
                    pi += 1
                    MEMSET("dve", p_[:, 0:3], 0.0, [pb_])
                    for b in range(4):
                        bk, bkb = bank()
                        for c in range(8):
                            MM(bk, w[:, c, cc * 128:(cc + 1) * 128], hTb[:, c, b * 512:(b + 1) * 512], c == 0, c == 7, [wb, hb[b]], [bkb])
                        CP("act", p_[:, 3 + b * 512:3 + (b + 1) * 512], bk, [bkb], [pb_])
                    TS("dve", y_, p_[:, 3:T + 3], cw[:, ch * 4 + 3:ch * 4 + 4], None, ALU.mult, None, [pb_, b_pv[1]], [yb_])
                    for j in range(3):
                        STT("dve", y_, p_[:, j:T + j], cw[:, ch * 4 + j:ch * 4 + j + 1], y_, ALU.mult, ALU.add, [pb_, b_pv[1], yb_], [yb_])
                    def tail(ch=ch, y_=y_, yb_=yb_, s_=s_, sb_=sb_, sg=sg, sgb=sgb):
                        ACT(y_, y_, AF.Silu, [yb_], [yb_])
                        if ch < 16:
                            hh = ch % 8
                            scale = (128.0 ** -0.5) if ch < 8 else 1.0
                            ACT(s_, y_, AF.Square, [yb_], [sb_])
                            for b in range(4):
                                bk, bkb = bank()
                                MM(bk, ONES_BF, s_[:, b * 512:(b + 1) * 512], True, True, [b_cbf, sb_], [bkb])
                                r_, rb_ = rsb2[b % 2]
                                ACT(r_, bk, AF.Ln, [bkb, b_small], [rb_], bias=C_EPSRMS)
                                ACT(r_, r_, AF.Exp, [rb_], [rb_], scale=-0.5)
                                STT("dve", sg[:, b * 512:(b + 1) * 512], y_[:, b * 512:(b + 1) * 512], scale, r_, ALU.mult, ALU.mult, [yb_, rb_], [sgb])
                            if ch < 8:
                                P.dma("sp", sc_q[hh], sg, sgb, reads=[sgb], writes=[scqb])
                            else:
                                P.dma("sp", sc_k[hh], sg, sgb, reads=[sgb], writes=[sckb])
                        else:
                            CP("dve", sg, y_, [yb_], [sgb])
                            P.dma("sp", sc_v[ch - 16], sg, sgb, reads=[sgb], writes=[scvb])
                    tail()
            for g in range(2):
                w, wb = wload("w_in_1", 0, 8, 3072 + g * 512, 512)
                for cc in range(4):
                    ch = g * 4 + cc
                    sg, sgb = stg[ch % 2]
                    for b in range(4):
                        bk, bkb = bank()
                        for c in range(8):
                            MM(bk, w[:, c, cc * 128:(cc + 1) * 128], hTb[:, c, b * 512:(b + 1) * 512], c == 0, c == 7, [wb, hb[b]], [bkb])
                        ACT(sg[:, b * 512:(b + 1) * 512], bk, AF.Silu, [bkb], [sgb])
                    P.dma("sp", sc_z[ch], sg, sgb, reads=[sgb], writes=[sczb])
            w, wb = wload("w_in_1", 0, 8, 4096, 16)
            for i in range(16):
                bk, bkb = bank()
                for c in range(8):
                    MM(bk[:, 0:16], hTb[:, c, i * 128:(i + 1) * 128], w[:, c, :], c == 0, c == 7, [wb, hb[i // 4]], [bkb])
                CP("act", ba[:, i, :], bk[:, 0:16], [bkb], [bab])
            A.release(m1)
            wmo, wmob = wload("w_mix_out_1", 0, 8, 0, D)
            beta, betab = A.alloc([128, 16, 8], F32, "gbeta")
            lbt, lbtb = A.alloc([128, 16, 8], F32, "glb")
            gg, ggb = A.alloc([128, 16, 8], F32, "gg")
            negA, negAb = A.alloc([128, 8], F32, "gnegA")
            ACT(beta, ba[:, :, 0:8], AF.Exp, [bab], [betab], scale=-1.0)
            TS("dve", beta, beta, 1.0, None, ALU.add, None, [betab], [betab])
            P.op("dve", lambda en: en.reciprocal(out=beta, in_=beta), [betab], [betab])
            ACT(lbt, beta, AF.Ln, [betab], [lbtb])
            TT("dve", gg, ba[:, :, 8:16], pv[1][:, PV1_DT:PV1_DT + 8].unsqueeze(1).to_broadcast([128, 16, 8]), ALU.add, [bab, b_pv[1]], [ggb])
            ACT(gg, gg, AF.Exp, [ggb], [ggb])
            ACT(gg, gg, AF.Ln, [ggb, b_small], [ggb], bias=C_ONE)
            ACT(negA, pv[1][:, PV1_AL:PV1_AL + 8], AF.Exp, [b_pv[1]], [negAb])
            TS("dve", negA, negA, -1.0, None, ALU.mult, None, [negAb], [negAb])
            TT("dve", gg, gg, negA.unsqueeze(1).to_broadcast([128, 16, 8]), ALU.mult, [ggb, negAb], [ggb])
            S, Sb = A.alloc([128, 8, 128], F32, "gS")
            Sbf, Sbfb = A.alloc([128, 8, 128], BF16, "gSbf")
            MEMSET("dve", S, 0.0, [Sb])
            CP("act", Sbf, S, [Sb], [Sbfb])
            vti = [A.alloc([128, 8, 128], BF16, f"gvt{i}") for i in range(2)]
            zti = [A.alloc([128, 8, 128], BF16, f"gzt{i}") for i in range(3)]
            qti = [A.alloc([128, 8, 128], BF16, f"gqt{i}") for i in range(2)]
            kti = [A.alloc([128, 8, 128], BF16, f"gkt{i}") for i in range(2)]
            sc = [A.alloc([128, 48], F32, f"gsc{i}") for i in range(2)]
            Dg, Dgb = A.alloc([128, 8, 128], F32, "gDg")
            Da, Dab = A.alloc([128, 8, 128], F32, "gDa")
            X1, X1b = A.alloc([128, 8, 128], F32, "gX1")
            X2, X2b = A.alloc([128, 8, 128], F32, "gX2")
            X3, X3b = A.alloc([128, 8, 128], F32, "gX3")
            egr, egrb = A.alloc([128, 8, 128], F32, "gegr")
            Qa = [A.alloc([128, 8, 128], F32, f"gQ{i}") for i in range(2)]
            QTa = [A.alloc([128, 8, 128], F32, f"gQT{i}") for i in range(2)]
            RT, RTb = A.alloc([128, 8, 128], F32, "gRT")
            vbt, vbtb = A.alloc([128, 8, 128], F32, "gvb")
            kbg, kbgb = A.alloc([128, 8, 128], F32, "gkbg")
            qkTs = [A.alloc([128, 8, 128], BF16, f"gqk{i}") for i in range(2)]
            qgTs = [A.alloc([128, 8, 128], BF16, f"gqg{i}") for i in range(2)]
            kgbs = [A.alloc([128, 8, 128], BF16, f"gkg{i}") for i in range(2)]
            uus = [A.alloc([128, 8, 128], F32, f"gu{i}") for i in range(2)]
            wTs = [A.alloc([128, 8, 128], BF16, f"gwT{i}") for i in range(2)]
            vn, vnb = A.alloc([128, 8, 128], BF16, "gvn")
            osq, osqb = A.alloc([128, 8, 128], BF16, "gosq")
            ors, orsb = A.alloc([128, 8, 128], F32, "gors")
            ot_, otb_ = A.alloc([128, 8, 128], F32, "got")
            GN = pv[1][:, PV1_GN:PV1_GN + 1]
            B1 = cst[:, C_B1:C_B1 + 512]
            B2 = cst[:, C_B2:C_B2 + 512]
            B3 = cst[:, C_B3:C_B3 + 512]

            def v3(ap):
                return ap.rearrange("p (a b) -> p a b", a=8)

            def ld(i):
                vt_, vtb_ = vti[i % 2]
                zt_, ztb_ = zti[i % 3]
                P.dma("sp", vt_, sc_v.rearrange("c p t -> p c t")[:, :, i * 128:(i + 1) * 128], vtb_, reads=[scvb], writes=[vtb_])
                P.dma("sp", zt_, sc_z.rearrange("c p t -> p c t")[:, :, i * 128:(i + 1) * 128], ztb_, reads=[sczb], writes=[ztb_])
                qt_, qtb_ = qti[i % 2]
                kt_, ktb_ = kti[i % 2]
                P.dma("sp", qt_, sc_q.rearrange("c p t -> p c t")[:, :, i * 128:(i + 1) * 128], qtb_, reads=[scqb], writes=[qtb_])
                P.dma("sp", kt_, sc_k.rearrange("c p t -> p c t")[:, :, i * 128:(i + 1) * 128], ktb_, reads=[sckb], writes=[ktb_])

            def make_prep(i):
                par = i % 2
                vt_, vtb_ = vti[i % 2]
                qt_, qtb_ = qti[i % 2]
                kt_, ktb_ = kti[i % 2]
                s_, sb_ = sc[par]
                gc = s_[:, 0:8]
                aa = s_[:, 8:16]
                glast = s_[:, 16:24]
                kgs = s_[:, 24:32]
                bgc = s_[:, 32:40]
                qkT, qkTb = qkTs[par]
                qgT, qgTb = qgTs[par]
                kgb_, kgbb = kgbs[par]
                uu, uub = uus[par]
                wT, wTb = wTs[par]
                Dg2 = Dg.rearrange("p a b -> p (a b)")
                Da2 = Da.rearrange("p a b -> p (a b)")
                segs = []

                def s0():
                    if i + 1 < 16:
                        ld(i + 1)
                    bk, bkb = bank()
                    MM(bk[:, 0:8], U_F, gg[:, i, :], True, True, [b_cst, ggb], [bkb])
                    CP("act", gc, bk[:, 0:8], [bkb], [sb_])
                    TT("dve", aa, gc, lbt[:, i, :], ALU.add, [sb_, lbtb], [sb_])
                    TT("dve", Dg, bm(ID_F, 8), bc(gc, 128), ALU.mult, [b_cst, sb_], [Dgb])
                    TT("dve", Da, bm(ID_F, 8), bc(aa, 128), ALU.mult, [b_cst, sb_], [Dab])
                    pR, pRb = pair()
                    for hh in range(2):
                        MM(pR[:, hh * 512:(hh + 1) * 512], ONES_F, Dg2[:, hh * 512:(hh + 1) * 512], True, True, [b_cst, Dgb], pRb)
                    ACT(egr, v3(pR), AF.Exp, pRb, [egrb])
                    ACT(glast, v3(pR)[:, :, 127], AF.Exp, pRb, [sb_])
                    TT("dve", kgs, v3(pR)[:, :, 127], gc, ALU.subtract, pRb + [sb_], [sb_])
                    ACT(kgs, kgs, AF.Exp, [sb_], [sb_])
                    ACT(bgc, aa, AF.Exp, [sb_], [sb_])
                    TT("dve", qgT, qt_, egr, ALU.mult, [qtb_, egrb], [qgTb])
                segs.append(s0)

                def s1():
                    p1, p1b = pair()
                    for hh in range(2):
                        MM(p1[:, hh * 512:(hh + 1) * 512], ONES_F, Dg2[:, hh * 512:(hh + 1) * 512], True, False, [b_cst, Dgb], p1b)
                        MM(p1[:, hh * 512:(hh + 1) * 512], ID_F, B1, False, True, [b_cst], p1b)
                    STT("dve", X1, v3(p1), -1.0, bc(aa, 128), ALU.mult, ALU.add, p1b + [sb_], [X1b])
                    ACT(X1, X1, AF.Exp, [X1b], [X1b])
                    p2, p2b = pair()
                    for hh in range(2):
                        MM(p2[:, hh * 512:(hh + 1) * 512], ONES_F, Da2[:, hh * 512:(hh + 1) * 512], True, False, [b_cst, Dab], p2b)
                        MM(p2[:, hh * 512:(hh + 1) * 512], ID_F, B2, False, True, [b_cst], p2b)
                    TT("dve", X2, v3(p2), bc(gc, 128), ALU.subtract, p2b + [sb_], [X2b])
                    ACT(X2, X2, AF.Exp, [X2b], [X2b])
                segs.append(s1)

                def s2():
                    p3, p3b = pair()
                    for hh in range(2):
                        MM(p3[:, hh * 512:(hh + 1) * 512], ONES_F, Dg2[:, hh * 512:(hh + 1) * 512], True, False, [b_cst, Dgb], p3b)
                        MM(p3[:, hh * 512:(hh + 1) * 512], ID_F, B3, False, True, [b_cst], p3b)
                    TT("dve", X3, v3(p3), bc(gc, 128), ALU.subtract, p3b + [sb_], [X3b])
                    ACT(X3, X3, AF.Exp, [X3b], [X3b])
                    pA, pAb = pair()
                    for h in range(8):
                        MM(pA[:, h * 128:(h + 1) * 128], kt_[:, h, :], kt_[:, h, :], True, True, [ktb_], pAb)
                    Q0, Q0b = Qa[0]
                    QT0, QT0b = QTa[0]
                    TT("dve", Q0, v3(pA), X1, ALU.mult, pAb + [X1b], [Q0b])
                    TT("dve", QT0, v3(pA), X2, ALU.mult, pAb + [X2b], [QT0b])
                    pB, pBb = pair()
                    for h in range(8):
                        MM(pB[:, h * 128:(h + 1) * 128], kt_[:, h, :], qt_[:, h, :], True, True, [ktb_, qtb_], pBb)
                    TT("dve", qkT, v3(pB), X3, ALU.mult, pBb + [X3b], [qkTb])
                    TT("dve", RT, bm(ID_F, 8), QT0, ALU.subtract, [b_cst, QT0b], [RTb])
                segs.append(s2)

                def mk_neu(k):
                    def f():
                        cur = (k - 1) % 2
                        Qp, Qpb = Qa[cur]
                        QTp, QTpb = QTa[cur]
                        Qn, Qnb = Qa[1 - cur]
                        QTn, QTnb = QTa[1 - cur]
                        pq, pqb = pair()
                        for h in range(8):
                            MM(pq[:, h * 128:(h + 1) * 128], QTp[:, h, :], Qp[:, h, :], True, True, [QTpb, Qpb], pqb)
                        CP("act", Qn, v3(pq), pqb, [Qnb])
                        if k < 6:
                            pqt, pqtb = pair()
                            if os.environ.get("QTMM", "1") == "1":
                                for h in range(8):
                                    MM(pqt[:, h * 128:(h + 1) * 128], Qp[:, h, :], QTp[:, h, :], True, True, [QTpb, Qpb], pqtb)
                            else:
                                for h in range(8):
                                    TR(pqt[:, h * 128:(h + 1) * 128], Qn[:, h, :], ID_F, [Qnb, b_cst], pqtb)
                            CP("act", QTn, v3(pqt), pqtb, [QTnb])
                        pr_, prb_ = pair()
                        for h in range(8):
                            MM(pr_[:, h * 128:(h + 1) * 128], Qn[:, h, :], RT[:, h, :], True, True, [Qnb, RTb], prb_)
                        TT("dve", RT, RT, v3(pr_), ALU.add, [RTb] + prb_, [RTb])
                    return f
                for k in range(1, 7):
                    segs.append(mk_neu(k))

                def s9():
                    bkk, bkkb = bank()
                    kk3 = bkk.bitcast(BF16).rearrange("p (a b) -> p a b", a=8)
                    for h in range(8):
                        TR(kk3[:, h, :], kt_[:, h, :], ID_BF, [ktb_, b_cbf], [bkkb])
                    TT("dve", kbg, kk3, bc(bgc, 128), ALU.mult, [bkkb, sb_], [kbgb])
                    TT("dve", kgb_, kk3, bc(kgs, 128), ALU.mult, [bkkb, sb_], [kgbb])
                    bkv, bkvb = bank()
                    vv3 = bkv.bitcast(BF16).rearrange("p (a b) -> p a b", a=8)
                    for h in range(8):
                        TR(vv3[:, h, :], vt_[:, h, :], ID_BF, [vtb_, b_cbf], [bkvb])
                    TT("dve", vbt, vv3, bc(beta[:, i, :], 128), ALU.mult, [bkvb, betab], [vbtb])
                segs.append(s9)

                def s10():
                    pu, pub = pair()
                    for h in range(8):
                        MM(pu[:, h * 128:(h + 1) * 128], RT[:, h, :], vbt[:, h, :], True, True, [RTb, vbtb], pub)
                    CP("act", uu, v3(pu), pub, [uub])
                    pw, pwb = pair()
                    for h in range(8):
                        MM(pw[:, h * 128:(h + 1) * 128], kbg[:, h, :], RT[:, h, :], True, True, [kbgb, RTb], pwb)
                    CP("act", wT, v3(pw), pwb, [wTb])
                segs.append(s10)
                return segs

            def make_scan(i):
                par = i % 2
                tsl = slice(i * 128, (i + 1) * 128)
                zt_, ztb_ = zti[i % 3]
                s_, sb_ = sc[par]
                glast = s_[:, 16:24]
                qkT, qkTb = qkTs[par]
                qgT, qgTb = qgTs[par]
                kgb_, kgbb = kgbs[par]
                uu, uub = uus[par]
                wT, wTb = wTs[par]
                hold = {}

                def t0():
                    pv_, pvb_ = pair()
                    for h in range(8):
                        MM(pv_[:, h * 128:(h + 1) * 128], wT[:, h, :], Sbf[:, h, :], True, True, [wTb, Sbfb], pvb_)
                    TT("dve", vn, uu, v3(pv_), ALU.subtract, [uub] + pvb_, [vnb])

                def t1():
                    po, pob = pair()
                    hold["po"] = (po, pob)
                    for h in range(8):
                        MM(po[:, h * 128:(h + 1) * 128], Sbf[:, h, :], qgT[:, h, :], True, False, [Sbfb, qgTb], pob)
                        MM(po[:, h * 128:(h + 1) * 128], vn[:, h, :], qkT[:, h, :], False, True, [vnb, qkTb], pob)
                    if i < 15:
                        pS, pSb = pair()
                        for h in range(8):
                            MM(pS[:, h * 128:(h + 1) * 128], kgb_[:, h, :], vn[:, h, :], True, True, [kgbb, vnb], pSb)
                        TT("dve", S, S, bc(glast, 128), ALU.mult, [Sb, sb_], [Sb])
                        TT("dve", S, S, v3(pS), ALU.add, [Sb] + pSb, [Sb])
                        CP("act", Sbf, S, [Sb], [Sbfb])
                    po, pob = hold["po"]
                    ACT(osq, v3(po), AF.Square, pob, [osqb])
                    CP("act", ot_, v3(po), pob, [otb_])

                def t2():
                    pn, pnb = pair()
                    osq2 = osq.rearrange("p a b -> p (a b)")
                    for hh in range(2):
                        MM(pn[:, hh * 512:(hh + 1) * 512], ONES_BF, osq2[:, hh * 512:(hh + 1) * 512], True, True, [b_cbf, osqb], pnb)
                    ACT(ors, v3(pn), AF.Ln, pnb + [b_small], [orsb], bias=C_EPSRMS, scale=1.0 / 128.0)
                    ACT(ors, ors, AF.Exp, [orsb], [orsb], scale=-0.5)

                def t3():
                    TT("dve", ot_, ot_, ors, ALU.mult, [otb_, orsb], [otb_])
                    STT("dve", hTb[:, :, tsl], ot_, GN, zt_, ALU.mult, ALU.mult, [otb_, b_pv[1], ztb_], [hb[i // 4]])
                return [t0, t1, t2, t3]

            ld(0)
            for f in make_prep(0):
                f()
            for i in range(16):
                Pq = make_prep(i + 1) if i + 1 < 16 else []
                Sq = make_scan(i)
                order = []
                pi_, si_ = 0, 0
                plan = "PPSPPSPPSPPSPPP"
                for ch_ in plan:
                    if ch_ == "P":
                        if pi_ < len(Pq):
                            order.append(Pq[pi_])
                            pi_ += 1
                    else:
                        order.append(Sq[si_])
                        si_ += 1
                while pi_ < len(Pq):
                    order.append(Pq[pi_])
                    pi_ += 1
                while si_ < len(Sq):
                    order.append(Sq[si_])
                    si_ += 1
                for f in order:
                    f()
            if dbg:
                P.dma("sp", dbg_o.rearrange("c p t -> p c t"), hTb[:], hb[0], reads=hb, writes=[outb])
            A.release(m0)
            zbufs = [A.alloc([128, 8, 512], F32, f"mz{i}") for i in range(4)]
            proj_res_ln(1, 0, 8,
                        lambda c, j: (wmo[:, c, j * 128:(j + 1) * 128], [wmob]),
                        lambda c, b: (hTb[:, c, b * 512:(b + 1) * 512], [hb[b]]),
                        range(4), zbufs)
            A.release(m0)

        def stage_touch():
            m0 = A.mark()
            tt_, ttb = A.alloc([128, 64], F32, "touch")
            for n in ["x", "mem"] + WNAMES:
                P.dma("sp", tt_[0:1, 0:16], dr[n][0:1, 0:16], ttb, writes=[ttb])
            ti_, tib = A.alloc([128, 64], I32, "touchi")
            P.dma("sp", ti_[0:1, 0:16], dr["pos"][0:1, 0:16], tib, writes=[tib])
            P.dma("sp", out_d[0:1, 0:16], tt_[0:1, 0:16], ttb, reads=[ttb], writes=[outb])
            A.release(m0)

        stages = [("touch", stage_touch), ("none", lambda: None), ("in", stage_in), ("mix0", stage_mixer0), ("xa0", lambda: stage_xattn(0)), ("ffn0", lambda: stage_ffn2(0)),
                  ("mix1", stage_mixer1), ("xa1", lambda: stage_xattn(1)), ("ffn1", lambda: stage_ffn2(1, final=True))]
        build.stage_cost = {}
        for nm, fn in stages:
            c0 = dict(cost)
            A.peak = 0
            fn()
            build.stage_cost.setdefault("_peak", {})[nm] = A.peak
            build.stage_cost[nm] = {k: round(cost[k] - c0[k], 1) for k in cost}
            if stop == nm:
                break
        P.op("sp", lambda en: en.nop(), reads=[outb] + resb + [scvb, sczb, scqb, sckb])
        P.emit()
        build.stats = P.stats
    return nc


_NC_CACHE = {}


def kernel(**inputs):
    inp = {k: np.asarray(v) for k, v in inputs.items()}
    if "full" not in _NC_CACHE:
        _NC_CACHE["full"] = build()
    nc = _NC_CACHE["full"]
    consts = make_consts()
    pv0 = make_pv(inp, 0)
    pv1 = make_pv(inp, 1)
    shared = {"consts": consts, "pv0": pv0, "pv1": pv1}
    for n in WNAMES:
        shared[n] = np.ascontiguousarray(inp[n], dtype=np.float32)
    in_maps = []
    for b in range(8):
        m = dict(shared)
        m["x"] = np.ascontiguousarray(inp["x"][b], dtype=np.float32)
        m["mem"] = np.ascontiguousarray(inp["mem"][b], dtype=np.float32)
        m["pos"] = np.ascontiguousarray(inp["positions"][b].reshape(1, T).astype(np.int32))
        in_maps.append(m)
    res = run_bass_kernel_spmd(nc, in_maps, core_ids=list(range(8)))
    return np.stack([np.asarray(r["out"], dtype=np.float32) for r in res.results], axis=0)
```

```python
import math
import os
from contextlib import ExitStack
import numpy as np
import concourse.bass as bass
import concourse.mybir as mybir
from concourse.bass_utils import run_bass_kernel_spmd

F32 = mybir.dt.float32
BF16 = mybir.dt.bfloat16
I32 = mybir.dt.int32
U8 = mybir.dt.uint8
AF = mybir.ActivationFunctionType
ALU = mybir.AluOpType
AX = mybir.AxisListType

ENGS = ("pe", "act", "dve", "pool", "sp")


class Buf:
    __slots__ = ("name", "lastw", "readers", "dsem", "excl")

    def __init__(self, name):
        self.name = name
        self.lastw = []
        self.readers = []
        self.dsem = None
        self.excl = False


class DmaSem:
    __slots__ = ("h", "total", "last", "key")

    def __init__(self, h, key):
        self.h = h
        self.total = 0
        self.last = None
        self.key = key


class Op:
    __slots__ = ("eng", "fn", "deps", "dma", "sig", "cnt", "clock", "signal")

    def __init__(self, eng, fn, dma=None):
        self.eng = eng
        self.fn = fn
        self.deps = []
        self.dma = dma
        self.sig = None
        self.cnt = None
        self.clock = None
        self.signal = False


class Prog:
    def __init__(self, nc, stack):
        self.nc = nc
        self.stack = stack
        self.ops = []
        self.nsem = 0
        self.esem = {e: stack.enter_context(nc.semaphore("es_" + e)) for e in ENGS}
        self.nbuf = 0

    def sb(self, name, shape, dtype):
        return self.stack.enter_context(self.nc.sbuf_tensor(name, list(shape), dtype))

    def ps(self, name, shape, dtype=F32):
        return self.stack.enter_context(self.nc.psum_tensor(name, list(shape), dtype))

    def buf(self, name=None):
        self.nbuf += 1
        return Buf(name or f"b{self.nbuf}")

    def _dsem(self, b):
        if b.dsem is None:
            self.nsem += 1
            h = self.stack.enter_context(self.nc.semaphore(f"ds{self.nsem}"))
            b.dsem = DmaSem(h, self.nsem)
        return b.dsem

    def _deps(self, op, reads, writes):
        deps = []
        for b in reads:
            deps.extend(b.lastw)
            if b.excl:
                deps.extend(b.readers)
        for b in writes:
            deps.extend(b.lastw)
            deps.extend(b.readers)
        seen = set(id(d) for d in op.deps)
        for d in deps:
            if d is op or id(d) in seen:
                continue
            seen.add(id(d))
            op.deps.append(d)

    def _update(self, op, reads, writes):
        for b in reads:
            if b not in writes:
                b.readers.append(op)
        for b in writes:
            b.lastw = [op]
            b.readers = []

    def op(self, eng, fn, reads=(), writes=()):
        o = Op(eng, fn)
        reads = list(reads)
        writes = list(writes)
        self._deps(o, reads, writes)
        self._update(o, reads, writes)
        self.ops.append(o)
        return o

    def dma(self, eng, out, in_, sbuf, reads=(), writes=()):
        return self.dma_group(eng, [(out, in_)], sbuf, reads, writes)

    def dma_group(self, eng, pairs, sbuf, reads=(), writes=()):
        ds = self._dsem(sbuf)
        reads = list(reads)
        writes = list(writes)
        ops = []
        for (out, in_) in pairs:
            o = Op(eng, (lambda e, out=out, in_=in_: e.dma_start(out=out, in_=in_)), dma=ds)
            if ds.last is not None:
                o.deps.append(ds.last)
            self._deps(o, reads, writes)
            ops.append(o)
        for o in ops:
            ds.total += 16
            o.sig = ds.total
            self.ops.append(o)
        last = ops[-1]
        ds.last = last
        for o in ops[:-1]:
            for b in reads:
                if b not in writes:
                    b.readers.append(o)
        self._update(last, reads, writes)
        return last

    def emit(self):
        nc = self.nc
        for o in self.ops:
            for d in o.deps:
                if d.dma is None:
                    if d.eng == "pe" and o.eng == "pe" and o.dma is None:
                        continue
                    d.signal = True
        cnt = {e: 0 for e in ENGS}
        seen = {e: {x: 0 for x in ENGS} for e in ENGS}
        seen_d = {e: {} for e in ENGS}
        per_eng = {e: [] for e in ENGS}
        nwait = 0
        for o in self.ops:
            E = o.eng
            sE = seen[E]
            wm = {}
            for d in o.deps:
                if d.dma is not None:
                    k = d.dma.key
                    if seen_d[E].get(k, 0) < d.sig:
                        seen_d[E][k] = d.sig
                        kk = ("d", k)
                        if kk not in wm or wm[kk][1] < d.sig:
                            wm[kk] = (d.dma.h, d.sig)
                else:
                    if d.eng == "pe" and E == "pe" and o.dma is None:
                        continue
                    if sE[d.eng] < d.cnt:
                        kk = ("e", d.eng)
                        if kk not in wm or wm[kk][1] < d.cnt:
                            wm[kk] = (self.esem[d.eng], d.cnt)
                        for x in ENGS:
                            if d.clock[x] > sE[x]:
                                sE[x] = d.clock[x]
            waits = list(wm.values())
            nwait += len(waits)
            if o.dma is None and o.signal:
                cnt[E] += 1
                o.cnt = cnt[E]
                clk = dict(sE)
                clk[E] = o.cnt
                o.clock = clk
            per_eng[E].append((o, waits))
        import os as _os
        if _os.environ.get("DUMPW"):
            names = {id(self.esem[e]): "E_" + e for e in ENGS}
            for e in ENGS:
                print("ENGINE", e)
                for o, waits in per_eng[e]:
                    ws = [(names.get(id(h), "dsem"), v) for h, v in waits]
                    print("   ", "dma" if o.dma is not None else "op", "sig" if o.signal else "", o.cnt, o.sig, (o.dma.key if o.dma else ""), ws)
        self.stats = dict(nops=len(self.ops), nwait=nwait, cnt=dict(cnt), nsem=self.nsem,
                          per_eng={e: len(per_eng[e]) for e in ENGS})
        assert max(cnt.values()) < 60000, cnt
        esem = self.esem
        with nc.Block() as block:
            def run(e, lst, E):
                for o, waits in lst:
                    for h, v in waits:
                        e.wait_ge(h, v)
                    ins = o.fn(e)
                    if o.dma is not None:
                        ins.then_inc(o.dma.h, 16)
                    elif o.signal:
                        ins.then_inc(esem[E], 1)

            @block.tensor
            def _(e):
                run(e, per_eng["pe"], "pe")

            @block.scalar
            def _(e):
                run(e, per_eng["act"], "act")

            @block.vector
            def _(e):
                run(e, per_eng["dve"], "dve")

            @block.gpsimd
            def _(e):
                run(e, per_eng["pool"], "pool")

            @block.sync
            def _(e):
                run(e, per_eng["sp"], "sp")


class Arena:
    def __init__(self, P, tensor, nbytes):
        self.P = P
        self.t = tensor
        self.n = nbytes
        self.off = 0
        self.hist = []

    def mark(self):
        return self.off

    def release(self, m):
        self.off = m

    def alloc(self, shape, dtype, name=None):
        esz = 4 if dtype in (F32, I32) else 2
        n = int(np.prod(shape[1:])) * esz
        s = (self.off + 63) // 64 * 64
        e = s + n
        assert e <= self.n, f"arena overflow {name} {e} > {self.n}"
        self.off = e
        self.peak = max(getattr(self, "peak", 0), e)
        v = self.t[:, s:e].bitcast(dtype)
        if len(shape) == 3:
            v = v.rearrange("p (a b) -> p a b", a=shape[1])
        if shape[0] < 128:
            v = v[0:shape[0]]
        b = self.P.buf(name)
        keep = []
        for (s2, e2, b2) in self.hist:
            if s2 < e and s < e2:
                b.readers.extend(b2.lastw)
                b.readers.extend(b2.readers)
                if s <= s2 and e2 <= e:
                    continue
            keep.append((s2, e2, b2))
        keep.append((s, e, b))
        self.hist = keep
        return v, b


T = 2048
D = 1024
NMEM = 256
DFF = 2816
P0W = 3088
P1W = 4112
ALPHA = (2.0 * 2) ** 0.25
LN_EPS = 1e-5
RMS_EPS = 1e-6
BIG = 1.0e30

C_ID, C_PERM, C_U, C_TRIS, C_TGTS, C_B1, C_B2, C_B3 = 0, 128, 256, 384, 512, 640, 1152, 1664
C_INVF = 2176
C_ONES = 2177
NCONST = 2305

PV_LN = 0
PV_FCW = 48
PV_FCB = 180
PV_X = 224
PV0_DN, PV0_GN, PV0_LAM, PV0_W2 = 224, 225, 226, 482
PV1_CW, PV1_AL, PV1_DT, PV1_GN = 224, 320, 328, 336
NPV = 768
NPV1 = 384


def make_consts():
    c = np.zeros((128, NCONST), np.float32)
    i = np.arange(128)
    c[:, C_ID:C_ID + 128] = np.eye(128)
    perm = np.zeros((128, 128), np.float32)
    for fo in range(128):
        r = fo % 64
        if r < 8:
            perm[fo + 8, fo] = -1.0
        elif r < 16:
            perm[fo - 8, fo] = 1.0
    c[:, C_PERM:C_PERM + 128] = perm
    le = (i[:, None] <= i[None, :]).astype(np.float32)
    c[:, C_U:C_U + 128] = le
    c[:, C_TRIS:C_TRIS + 128] = le * (-1.0 / 16.0)
    c[:, C_TGTS:C_TGTS + 128] = (i[:, None] > i[None, :]).astype(np.float32) * (-1.0 / 16.0)
    b1 = BIG * (i[None, :] >= i[:, None])
    b2 = -BIG * (i[None, :] <= i[:, None])
    b3 = -BIG * (i[None, :] < i[:, None])
    c[:, C_B1:C_B1 + 512] = np.tile(b1, (1, 4))
    c[:, C_B2:C_B2 + 512] = np.tile(b2, (1, 4))
    c[:, C_B3:C_B3 + 512] = np.tile(b3, (1, 4))
    invf = 500000.0 ** (-np.arange(0, 16, 2, dtype=np.float32) / 16.0)
    col = np.zeros(128, np.float32)
    for p in range(128):
        r = p % 64
        if r < 16:
            col[p] = invf[r % 8]
    c[:, C_INVF] = col
    c[:, C_ONES:C_ONES + 128] = 1.0
    return c


def chunkcols(v):
    return np.ascontiguousarray(v.reshape(-1, 128).T)


def make_pv(inp, l):
    pv = np.zeros((128, NPV), np.float32)
    lnn = (["ln1_g_0", "ln1_b_0", "ln2_g_0", "ln2_b_0", "ln3_g_0", "ln3_b_0"] if l == 0 else
           ["ln1_g_1", "ln1_b_1", "ln2_g_1", "ln2_b_1", "ln3_g_1", "ln3_b_1"])
    for k, nm in enumerate(lnn):
        pv[:, PV_LN + 8 * k:PV_LN + 8 * k + 8] = chunkcols(inp[nm])
    cw = inp[f"ffn_conv_w_{l}"]
    for j in range(3):
        pv[:, PV_FCW + j:PV_FCW + 132:3] = chunkcols(cw[j])
    pv[:, PV_FCB:PV_FCB + 44] = chunkcols(inp[f"ffn_conv_b_{l}"])
    if l == 0:
        pv[:, PV0_DN] = inp["diff_norm_0"]
        pv[:, PV0_GN] = inp["gla_norm_0"]
        pv[:, PV0_LAM:PV0_LAM + 256] = inp["diff_lambda_0"].reshape(1, 256)
        pv[0:16, PV0_W2:PV0_W2 + 256] = inp["gla_w2_0"]
        pv[16, PV0_W2:PV0_W2 + 256] = inp["gla_b2_0"]
    else:
        gw = inp["gdn_conv_w_1"]
        for j in range(4):
            pv[:, PV1_CW + j:PV1_CW + 96:4] = chunkcols(gw[j])
        pv[:, PV1_AL:PV1_AL + 8] = inp["gdn_a_log_1"][None, :]
        pv[:, PV1_DT:PV1_DT + 8] = inp["gdn_dt_bias_1"][None, :]
        pv[:, PV1_GN] = inp["gdn_norm_1"]
    return pv


WNAMES = ["w_in_0", "w_mix_out_0", "xa_wq_0", "xa_wkv_0", "xa_wo_0", "ffn_w_in_0", "ffn_w_out_0",
          "w_in_1", "w_mix_out_1", "xa_wq_1", "xa_wkv_1", "xa_wo_1", "ffn_w_in_1", "ffn_w_out_1"]
WSHAPES = {"w_in_0": (D, P0W), "w_in_1": (D, P1W)}
for _l in range(2):
    WSHAPES[f"w_mix_out_{_l}"] = (D, D)
    WSHAPES[f"xa_wq_{_l}"] = (D, D)
    WSHAPES[f"xa_wkv_{_l}"] = (D, 2 * D)
    WSHAPES[f"xa_wo_{_l}"] = (D, D)
    WSHAPES[f"ffn_w_in_{_l}"] = (D, 2 * DFF)
    WSHAPES[f"ffn_w_out_{_l}"] = (DFF, D)


def build(stop=None, dbg=False):
    nc = bass.Bass("TRN2", target_bir_lowering=False)
    dr = {}
    dr["x"] = nc.dram_tensor("x", [T, D], F32, kind="ExternalInput").ap()
    dr["mem"] = nc.dram_tensor("mem", [NMEM, D], F32, kind="ExternalInput").ap()
    dr["pos"] = nc.dram_tensor("pos", [1, T], I32, kind="ExternalInput").ap()
    dr["consts"] = nc.dram_tensor("consts", [128, NCONST], F32, kind="ExternalInput").ap()
    dr["pv0"] = nc.dram_tensor("pv0", [128, NPV], F32, kind="ExternalInput").ap()
    dr["pv1"] = nc.dram_tensor("pv1", [128, NPV], F32, kind="ExternalInput").ap()
    for n in WNAMES:
        dr[n] = nc.dram_tensor(n, list(WSHAPES[n]), F32, kind="ExternalInput").ap()
    out_d = nc.dram_tensor("out", [T, D], F32, kind="ExternalOutput").ap()
    res_d = nc.dram_tensor("resT", [8, 128, T], F32, kind=("ExternalOutput" if dbg else "Internal")).ap()
    sc_v = nc.dram_tensor("sc_v", [8, 128, T], BF16, kind="Internal").ap()
    sc_z = nc.dram_tensor("sc_z", [8, 128, T], BF16, kind="Internal").ap()
    sc_q = nc.dram_tensor("sc_q", [8, 128, T], BF16, kind="Internal").ap()
    sc_k = nc.dram_tensor("sc_k", [8, 128, T], BF16, kind="Internal").ap()
    if dbg:
        dbg_o = nc.dram_tensor("dbg_o", [8, 128, T], BF16, kind="ExternalOutput").ap()

    with ExitStack() as st:
        P = Prog(nc, st)
        hTb = P.sb("hTb", [128, 8, T], BF16)
        hb = [P.buf(f"hb{b}") for b in range(4)]
        cst = P.sb("cst", [128, NCONST], F32)
        b_cst = P.buf("cst")
        cbf = P.sb("cbf", [128, 128 * 4], BF16)
        b_cbf = P.buf("cbf")
        ID_BF = cbf[:, 0:128]
        PERM_BF = cbf[:, 128:256]
        ONES_BF = cbf[:, 256:384]
        U_BF = cbf[:, 384:512]
        ID_F = cst[:, C_ID:C_ID + 128]
        ONES_F = cst[:, C_ONES:C_ONES + 128]
        U_F = cst[:, C_U:C_U + 128]
        small = P.sb("small", [128, 16], F32)
        b_small = P.buf("small")
        C_EPSLN = small[:, 0:1]
        C_EPSRMS = small[:, 1:2]
        C_ONE = small[:, 2:3]
        C_ZERO = small[:, 3:4]
        pv = [P.sb("pv0s", [128, NPV], F32), P.sb("pv1s", [128, NPV1], F32)]
        memT = P.sb("memT", [128, 8, NMEM], BF16)
        memTb = P.buf("memT")
        b_pv = [P.buf("pv0"), P.buf("pv1")]
        RING_SLOT = 16 * 1024
        ring = P.sb("ring", [128, 2 * RING_SLOT], U8)
        ring_b = [P.buf("ring0"), P.buf("ring1")]
        ring_i = [0]
        ARENA_BYTES = 122 * 1024
        arena_t = P.sb("arena", [128, ARENA_BYTES], U8)
        A = Arena(P, arena_t, ARENA_BYTES)
        pst = [P.ps(f"ps{k}", [128, 1024], F32) for k in range(4)]
        pb = [P.buf(f"pb{i}") for i in range(8)]
        for _b in pb:
            _b.excl = True
        bank_i = [0]
        pair_i = [0]

        def bank():
            i = bank_i[0] % 8
            bank_i[0] += 1
            return pst[i // 2][:, (i % 2) * 512:(i % 2) * 512 + 512], pb[i]

        def bank_at(i):
            return pst[i // 2][:, (i % 2) * 512:(i % 2) * 512 + 512], pb[i]

        def pair():
            k = pair_i[0] % 4
            pair_i[0] += 1
            return pst[k][:, :], [pb[2 * k], pb[2 * k + 1]]

        resb = [P.buf(f"res{b}") for b in range(4)]
        outb = P.buf("outd")
        scvb = P.buf("scv")
        sczb = P.buf("scz")
        scqb = P.buf("scq")
        sckb = P.buf("sck")

        cost = {"pe": 0.0, "act": 0.0, "dve": 0.0, "pool": 0.0}
        build.cost = cost

        def fsz(ap):
            n = 1
            for d in ap.shape[1:]:
                n *= d
            return n

        def MM(out, lhsT, rhs, s, e, R, W, **kw):
            cost["pe"] += max(fsz(out), 64) / 2.4e3 * (2 if rhs.dtype == F32 else 1) + 0.01
            P.op("pe", lambda en: en.matmul(out, lhsT=lhsT, rhs=rhs, start=s, stop=e, **kw), R, W)

        def TR(out, in_, ident, R, W):
            P.op("pe", lambda en: en.transpose(out=out, in_=in_, identity=ident), R, W)

        def ACT(out, in_, func, R, W, bias=None, scale=1.0):
            cost["act"] += fsz(out) / 1.2e3 + 0.22
            if bias is None:
                P.op("act", lambda en: en.activation(out=out, in_=in_, func=func, scale=scale), R, W)
            else:
                P.op("act", lambda en: en.activation(out=out, in_=in_, func=func, bias=bias, scale=scale), R, W)

        def TT(eng, out, a, b, op, R, W):
            cost[eng] += fsz(out) / 0.96e3 + 0.1
            P.op(eng, lambda en: en.tensor_tensor(out=out, in0=a, in1=b, op=op), R, W)

        def TS(eng, out, a, s1, s2, op0, op1, R, W):
            cost[eng] += fsz(out) / 0.96e3 + 0.1
            if s2 is None:
                P.op(eng, lambda en: en.tensor_scalar(out=out, in0=a, scalar1=s1, scalar2=None, op0=op0), R, W)
            else:
                P.op(eng, lambda en: en.tensor_scalar(out=out, in0=a, scalar1=s1, scalar2=s2, op0=op0, op1=op1), R, W)

        def STT(eng, out, a, s, b, op0, op1, R, W):
            cost[eng] += fsz(out) / 0.96e3 + 0.1
            P.op(eng, lambda en: en.scalar_tensor_tensor(out=out, in0=a, scalar=s, in1=b, op0=op0, op1=op1), R, W)

        def CP(eng, out, in_, R, W):
            cost[eng] += fsz(out) / (1.2e3 if eng == "act" else 0.96e3) + (0.22 if eng == "act" else 0.1)
            if eng == "act":
                P.op("act", lambda en: en.copy(out=out, in_=in_), R, W)
            else:
                P.op(eng, lambda en: en.tensor_copy(out=out, in_=in_), R, W)

        def MEMSET(eng, ap, val, W):
            P.op(eng, lambda en: en.memset(ap, val), (), W)

        def bc(ap2, n):
            return ap2.unsqueeze(2).to_broadcast([ap2.shape[0], ap2.shape[1], n])

        def bm(ap2, h):
            return ap2.unsqueeze(1).to_broadcast([ap2.shape[0], h, ap2.shape[1]])

        def wload(name, k0, nk, c0, ncols, dcol=0, slot=None, newslot=True):
            if newslot:
                ring_i[0] += 1
            si = ring_i[0] % 2
            base = si * RING_SLOT + dcol
            nbytes = nk * ncols * 2
            assert dcol + nbytes <= RING_SLOT
            v = ring[:, base:base + nbytes].bitcast(BF16).rearrange("p (a b) -> p a b", a=nk)
            src = dr[name][k0 * 128:(k0 + nk) * 128, c0:c0 + ncols].rearrange("(c p) n -> p c n", p=128)
            pairs = []
            step = max(1, 1024 // 128 // 1)
            kk = 0
            while kk < nk:
                k2 = min(nk, kk + 8)
                pairs.append((v[:, kk:k2, :], src[:, kk:k2, :]))
                kk = k2
            P.dma_group("pool", pairs, ring_b[si], writes=[ring_b[si]])
            return v, ring_b[si]

        P.dma("sp", cst[:], dr["consts"], b_cst, writes=[b_cst])
        P.dma("sp", pv[0][:], dr["pv0"], b_pv[0], writes=[b_pv[0]])
        P.dma("sp", pv[1][:], dr["pv1"][:, 0:NPV1], b_pv[1], writes=[b_pv[1]])
        CP("dve", cbf[:, 0:128], cst[:, C_ID:C_ID + 128], [b_cst], [b_cbf])
        CP("dve", cbf[:, 128:256], cst[:, C_PERM:C_PERM + 128], [b_cst], [b_cbf])
        CP("dve", cbf[:, 256:384], cst[:, C_ONES:C_ONES + 128], [b_cst], [b_cbf])
        CP("dve", cbf[:, 384:512], cst[:, C_U:C_U + 128], [b_cst], [b_cbf])
        MEMSET("dve", small[:, 0:1], LN_EPS, [b_small])
        MEMSET("dve", small[:, 1:2], RMS_EPS, [b_small])
        MEMSET("dve", small[:, 2:3], 1.0, [b_small])
        MEMSET("dve", small[:, 3:4], 0.0, [b_small])
        MEMSET("dve", small[:, 4:5], math.pi / 2, [b_small])
        C_HPI = small[:, 4:5]

        def res_ap(b):
            return res_d.rearrange("c p t -> p c t")[:, :, b * 512:(b + 1) * 512]

        def stage_in():
            m0 = A.mark()
            xin = [A.alloc([128, D], F32, f"xin{i}") for i in range(4)]
            xTf = [A.alloc([128, 8, 512], F32, f"xTf{i}") for i in range(2)]
            for b in range(int(os.environ.get("NBLK", "4"))):
                xt, xtb = xTf[b % 2]
                for ti in range(int(os.environ.get("NTI", "4"))):
                    i = 4 * b + ti
                    xi, xib = xin[i % 4]
                    P.dma("sp", xi, dr["x"][i * 128:(i + 1) * 128, :], xib, writes=[xib])
                    for g in range(2):
                        bk, bkb = bank()
                        for k in range(4):
                            TR(bk[:, k * 128:(k + 1) * 128], xi[:, (4 * g + k) * 128:(4 * g + k + 1) * 128], ID_F,
                               [xib, b_cst], [bkb])
                        bk3 = bk.rearrange("p (a b) -> p a b", a=4)
                        if True:
                            CP("act", hTb[:, 4 * g:4 * g + 4, i * 128:(i + 1) * 128], bk3, [bkb], [hb[b]])
                        CP("dve", xt[:, 4 * g:4 * g + 4, ti * 128:(ti + 1) * 128], bk3, [bkb], [xtb])
                P.dma("act", res_ap(b), xt, xtb, reads=[xtb], writes=[resb[b]])
            for i in range(2):
                xi, xib = xin[i]
                P.dma("sp", xi, dr["mem"][i * 128:(i + 1) * 128, :], xib, writes=[xib])
                for g in range(2):
                    bk, bkb = bank()
                    for k in range(4):
                        TR(bk[:, k * 128:(k + 1) * 128], xi[:, (4 * g + k) * 128:(4 * g + k + 1) * 128], ID_F, [xib, b_cst], [bkb])
                    CP("act", memT[:, 4 * g:4 * g + 4, i * 128:(i + 1) * 128], bk.rearrange("p (a b) -> p a b", a=4), [bkb], [memTb])
            A.release(m0)

        def ln_alloc(n):
            out = []
            small_ = dict(mm=A.alloc([128, 512], F32, "ln_m"), vv=A.alloc([128, 512], F32, "ln_v"),
                          rs=A.alloc([128, 512], F32, "ln_r"), nm=A.alloc([128, 512], F32, "ln_nm"))
            for i in range(n):
                d_ = dict(zb=A.alloc([128, 8, 512], BF16, f"ln_zb{i}"), zq=A.alloc([128, 8, 512], BF16, f"ln_zq{i}"))
                d_.update(small_)
                out.append(d_)
            return out

        def ln_head(tmp, z, zb_):
            (zb, zbb), (zq, zqb) = tmp["zb"], tmp["zq"]
            CP("act", zb, z, [zb_], [zbb])
            ACT(zq, z, AF.Square, [zb_], [zqb])

        def ln_tail(tmp, b, z, zb_, lcol, l, final=False, finalbuf=None):
            (zb, zbb), (zq, zqb), (mm_, mmb), (vv, vvb), (rs, rsb), (nm, nmb) = (tmp[k] for k in ("zb", "zq", "mm", "vv", "rs", "nm"))
            s1, s1b = bank()
            s2, s2b = bank()
            for j in range(8):
                MM(s1, ONES_BF, zb[:, j, :], j == 0, j == 7, [b_cbf, zbb], [s1b])
            for j in range(8):
                MM(s2, ONES_BF, zq[:, j, :], j == 0, j == 7, [b_cbf, zqb], [s2b])
            TS("dve", mm_, s1, 1.0 / D, None, ALU.mult, None, [s1b], [mmb])
            TT("dve", vv, mm_, mm_, ALU.mult, [mmb], [vvb])
            STT("dve", vv, s2, 1.0 / D, vv, ALU.mult, ALU.subtract, [s2b, vvb], [vvb])
            ACT(rs, vv, AF.Ln, [vvb, b_small], [rsb], bias=C_EPSLN)
            ACT(rs, rs, AF.Exp, [rsb], [rsb], scale=-0.5)
            TT("dve", nm, mm_, rs, ALU.mult, [mmb, rsb], [nmb])
            TT("dve", z, z, bm(rs, 8), ALU.mult, [zb_, rsb], [zb_])
            TT("dve", z, z, bm(nm, 8), ALU.subtract, [zb_, nmb], [zb_])
            gc_ = pv[l][:, PV_LN + 16 * lcol:PV_LN + 16 * lcol + 8]
            bc_ = pv[l][:, PV_LN + 16 * lcol + 8:PV_LN + 16 * lcol + 16]
            for j in range(8):
                ACT(z[:, j, :], z[:, j, :], AF.Identity, [zb_, b_pv[l]], [zb_], bias=bc_[:, j:j + 1], scale=gc_[:, j:j + 1])
            if not final:
                CP("dve", hTb[:, :, b * 512:(b + 1) * 512], z, [zb_], [hb[b]])
                P.dma("sp", res_ap(b), z, zb_, reads=[zb_], writes=[resb[b]])
            else:
                for ti in range(4):
                    ot, otb = finalbuf[ti % 2]
                    for g in range(2):
                        bk, bkb = bank()
                        for k in range(4):
                            TR(bk[:, k * 128:(k + 1) * 128], z[:, 4 * g + k, ti * 128:(ti + 1) * 128], ID_F, [zb_, b_cst], [bkb])
                        if g == 0:
                            CP("act", ot[:, 0:512], bk, [bkb], [otb])
                        else:
                            CP("dve", ot[:, 512:1024], bk, [bkb], [otb])
                    r0 = b * 512 + ti * 128
                    P.dma("sp", out_d[r0:r0 + 128, :], ot, otb, reads=[otb], writes=[outb])

        def ln_block(b, z, zb_, lcol, l, final=False, finalbuf=None):
            m0 = A.mark()
            tmp = ln_alloc(1)[0]
            ln_head(tmp, z, zb_)
            ln_tail(tmp, b, z, zb_, lcol, l, final=final, finalbuf=finalbuf)
            A.release(m0)

        def proj_z(b, nk, lhs_fn, rhs_fn, z, zb_):
            P.dma("sp", z, res_ap(b), zb_, reads=[resb[b]], writes=[zb_])
            for j in range(8):
                bk, bkb = bank()
                for c in range(nk):
                    la, lr = lhs_fn(c, j)
                    ra, rr = rhs_fn(c, b)
                    MM(bk, la, ra, c == 0, c == nk - 1, lr + rr, [bkb])
                STT("dve", z[:, j, :], z[:, j, :], ALPHA, bk, ALU.mult, ALU.add, [zb_, bkb], [zb_])

        def proj_res_ln(l, lcol, nk, lhs_fn, rhs_fn, blocks, zbufs, final=False, finalbuf=None):
            m0 = A.mark()
            tmps = ln_alloc(2)
            pend = None
            for b in blocks:
                z, zb_ = zbufs[b % len(zbufs)]
                proj_z(b, nk, lhs_fn, rhs_fn, z, zb_)
                ln_head(tmps[b % 2], z, zb_)
                if pend is not None:
                    ln_tail(*pend)
                pend = (tmps[b % 2], b, z, zb_, lcol, l, final, finalbuf)
            ln_tail(*pend)
            A.release(m0)

        def stage_xattn(l):
            m0 = A.mark()
            kT, kTb = A.alloc([128, 8, NMEM], BF16, "xkT")
            vt, vtb = A.alloc([128, 2, D], BF16, "xv")
            w, wb = wload(f"xa_wkv_{l}", 0, 8, 0, D)
            for jj in range(8):
                bk, bkb = bank()
                for c in range(8):
                    MM(bk[:, 0:NMEM], w[:, c, jj * 128:(jj + 1) * 128], memT[:, c, :], c == 0, c == 7, [wb, memTb], [bkb])
                CP("act", kT[:, jj, :], bk[:, 0:NMEM], [bkb], [kTb])
            w, wb = wload(f"xa_wkv_{l}", 0, 8, D, D)
            for gi in range(2):
                for i in range(2):
                    bk, bkb = bank()
                    for c in range(8):
                        MM(bk, memT[:, c, i * 128:(i + 1) * 128], w[:, c, gi * 512:(gi + 1) * 512], c == 0, c == 7, [wb, memTb], [bkb])
                    CP("act", vt[:, i, gi * 512:(gi + 1) * 512], bk, [bkb], [vtb])
            wq, wqb = wload(f"xa_wq_{l}", 0, 8, 0, D)
            wo, wob = wload(f"xa_wo_{l}", 0, 8, 0, D)
            qT = [A.alloc([128, 8, 512], BF16, f"xq{i}") for i in range(2)]
            oT = [A.alloc([128, 8, 512], BF16, f"xo{i}") for i in range(1)]
            E = [A.alloc([128, 2, 512], BF16, f"xE{i}") for i in range(2)]
            rsm = [A.alloc([128, 512], F32, f"xr{i}") for i in range(2)]
            zbufs = [A.alloc([128, 8, 512], F32, f"xz{i}") for i in range(2)]
            tmps = ln_alloc(2)
            pend = None
            def qproj(b, js):
                q, qb_ = qT[b % 2]
                for j in js:
                    bk, bkb = bank()
                    for c in range(8):
                        MM(bk, wq[:, c, j * 128:(j + 1) * 128], hTb[:, c, b * 512:(b + 1) * 512], c == 0, c == 7, [wqb, hb[b]], [bkb])
                    CP("act", q[:, j, :], bk, [bkb], [qb_])
            qproj(0, range(8))
            for b in range(4):
                q, qb_ = qT[b % 2]
                o, ob_ = oT[0]

                def xs(h, q=q, qb_=qb_):
                    e, eb_ = E[h % 2]
                    for kt in range(2):
                        bk, bkb = bank()
                        for dc in range(2):
                            MM(bk, kT[:, 2 * h + dc, kt * 128:(kt + 1) * 128], q[:, 2 * h + dc, :], dc == 0, dc == 1, [kTb, qb_], [bkb])
                        ACT(e[:, kt, :], bk, AF.Exp, [bkb], [eb_], scale=1.0 / 16.0)

                def xa(h, o=o, ob_=ob_):
                    e, eb_ = E[h % 2]
                    r, rb_ = rsm[h % 2]
                    sm, smb = bank()
                    for kt in range(2):
                        MM(sm, ONES_BF, e[:, kt, :], kt == 0, kt == 1, [b_cbf, eb_], [smb])
                    P.op("dve", lambda en, r=r, sm=sm: en.reciprocal(out=r, in_=sm), [smb], [rb_])
                    for dvc in range(2):
                        bk, bkb = bank()
                        for kt in range(2):
                            MM(bk, vt[:, kt, h * 256 + dvc * 128:h * 256 + dvc * 128 + 128], e[:, kt, :], kt == 0, kt == 1, [vtb, eb_], [bkb])
                        TT("dve", o[:, 2 * h + dvc, :], bk, r, ALU.mult, [bkb, rb_], [ob_])
                xs(0)
                for h in range(4):
                    if b + 1 < 4:
                        qproj(b + 1, [2 * h, 2 * h + 1])
                    if h + 1 < 4:
                        xs(h + 1)
                    xa(h)
                z, zb_ = zbufs[b % 2]
                proj_z(b, 8,
                       lambda c, j: (wo[:, c, j * 128:(j + 1) * 128], [wob]),
                       lambda c, bb, o=o, ob_=ob_: (o[:, c, :], [ob_]),
                       z, zb_)
                ln_head(tmps[b % 2], z, zb_)
                if pend is not None:
                    ln_tail(*pend)
                pend = (tmps[b % 2], b, z, zb_, 1, l)
            ln_tail(*pend)
            A.release(m0)

        def stage_ffn2(l, final=False):
            m0 = A.mark()
            halo, halob = A.alloc([128, 44, 2], F32, "halo")
            MEMSET("dve", halo, 0.0, [halob])
            acth, acthb = A.alloc([128, 22, 1024], BF16, "acth")
            cw = pv[l][:, PV_FCW:PV_FCW + 132]
            cb = pv[l][:, PV_FCB:PV_FCB + 44]
            groups = [(g * 4, 4) for g in range(5)] + [(20, 2)]
            pi = 0
            pendf = [None]
            for hf in range(2):
                mh = A.mark()
                NPB = 3
                pre = [[A.alloc([128, 1026], F32, f"pre{i}{k}") for k in range(2)] for i in range(NPB)]
                cv = [[A.alloc([128, 1024], F32, f"cv{i}{k}") for k in range(2)] for i in range(NPB)]
                tpl = [A.alloc([128, 1024], F32, f"tp{i}") for i in range(2)] if os.environ.get("POOLCONV", "0") == "1" else None
                for (c0, ncn) in groups:
                    w, wb = wload(f"ffn_w_in_{l}", 0, 8, c0 * 128, ncn * 128)
                    w2_, _ = wload(f"ffn_w_in_{l}", 0, 8, DFF + c0 * 128, ncn * 128, dcol=8 * ncn * 128 * 2, newslot=False)
                    for cc in range(ncn):
                        ch = c0 + cc
                        pr = pre[pi % NPB]
                        cvv = cv[pi % NPB]
                        pi += 1
                        for k, (wk, chk) in enumerate(((w, ch), (w2_, ch + 22))):
                            p_, pb_ = pr[k]
                            y_, yb_ = cvv[k]
                            CP("act", p_[:, 0:2], halo[:, chk, :], [halob], [pb_])
                            for bb in range(2):
                                bk, bkb = bank()
                                blk = 2 * hf + bb
                                for c in range(8):
                                    MM(bk, wk[:, c, cc * 128:(cc + 1) * 128], hTb[:, c, blk * 512:(blk + 1) * 512], c == 0, c == 7, [wb, hb[blk]], [bkb])
                                CP("act", p_[:, 2 + bb * 512:2 + (bb + 1) * 512], bk, [bkb], [pb_])
                            CP("act", halo[:, chk, :], p_[:, 1024:1026], [pb_], [halob])
                            ce = "pool" if (k == 1 and os.environ.get("POOLCONV", "0") == "1") else "dve"
                            ACT(y_, p_[:, 2:1026], AF.Identity, [pb_, b_pv[l]], [yb_], bias=cb[:, chk:chk + 1], scale=cw[:, chk * 3 + 2:chk * 3 + 3])
                            if ce == "dve":
                                STT("dve", y_, p_[:, 1:1025], cw[:, chk * 3 + 1:chk * 3 + 2], y_, ALU.mult, ALU.add, [pb_, b_pv[l], yb_], [yb_])
                                STT("dve", y_, p_[:, 0:1024], cw[:, chk * 3:chk * 3 + 1], y_, ALU.mult, ALU.add, [pb_, b_pv[l], yb_], [yb_])
                            else:
                                tp_, tpb_ = tpl[pi % 2]
                                for jj in (1, 0):
                                    TS("pool", tp_, p_[:, jj:1024 + jj], cw[:, chk * 3 + jj:chk * 3 + jj + 1], None, ALU.mult, None, [pb_, b_pv[l]], [tpb_])
                                    TT("pool", y_, y_, tp_, ALU.add, [yb_, tpb_], [yb_])
                        if pendf[0] is not None:
                            pendf[0]()

                        def fin(cvv=cvv, ch=ch):
                            g_, gb_ = cvv[0]
                            u_, ub_ = cvv[1]
                            ACT(g_, g_, AF.Silu, [gb_], [gb_])
                            TT("dve", acth[:, ch, :], g_, u_, ALU.mult, [gb_, ub_], [acthb])
                        pendf[0] = fin
                pendf[0]()
                pendf[0] = None
                A.release(mh)
                zh = [A.alloc([128, 8, 512], F32, f"fz{i}") for i in range(2)]
                finalbuf = None
                if final:
                    finalbuf = [A.alloc([128, D], F32, f"fo{i}") for i in range(2)]
                for bb in range(2):
                    blk = 2 * hf + bb
                    z, zb_ = zh[bb]
                    P.dma("sp", z, res_ap(blk), zb_, reads=[resb[blk]], writes=[zb_])
                for jg in range(4):
                    w, wb = wload(f"ffn_w_out_{l}", 0, 22, jg * 256, 256)
                    for j2 in range(2):
                        j = jg * 2 + j2
                        for bb in range(2):
                            z, zb_ = zh[bb]
                            bk, bkb = bank()
                            for c in range(22):
                                MM(bk, w[:, c, j2 * 128:(j2 + 1) * 128], acth[:, c, bb * 512:(bb + 1) * 512], c == 0, c == 21, [wb, acthb], [bkb])
                            STT("dve", z[:, j, :], z[:, j, :], ALPHA, bk, ALU.mult, ALU.add, [zb_, bkb], [zb_])
                for bb in range(2):
                    blk = 2 * hf + bb
                    z, zb_ = zh[bb]
                    ln_block(blk, z, zb_, 2, l, final=final, finalbuf=finalbuf)
                A.release(mh)
            A.release(m0)

        def stage_mixer0():
            l = 0
            m0 = A.mark()
            oT, oTb = A.alloc([128, 8, T], BF16, "oT")
            m1 = A.mark()
            aqT, aqb = A.alloc([128, 4, T], BF16, "aqT")
            akT, akb = A.alloc([128, 4, T], BF16, "akT")
            avt, avb = A.alloc([128, 16, 512], BF16, "avt")
            lamt, lamb = A.alloc([128, 8], F32, "lam")
            m2 = A.mark()
            cosT, cosb = A.alloc([128, T], F32, "cosT")
            sinT, sinb = A.alloc([128, T], F32, "sinT")
            m3 = A.mark()
            posi, posib = A.alloc([128, T], I32, "posi")
            ang, angb = A.alloc([128, T], F32, "ang")
            tmp, tmpb = A.alloc([128, T], F32, "angt")
            P.dma("sp", posi, dr["pos"].partition_broadcast(128), posib, writes=[posib])
            CP("dve", ang, posi, [posib], [angb])
            TS("dve", ang, ang, cst[:, C_INVF:C_INVF + 1], None, ALU.mult, None, [angb, b_cst], [angb])
            MAGIC = 12582912.0
            for (dst, dstb, shift) in ((sinT, sinb, 0.0), (cosT, cosb, math.pi / 2)):
                TS("dve", tmp, ang, shift, 1.0 / (2 * math.pi), ALU.add, ALU.mult, [angb], [tmpb])
                TS("dve", tmp, tmp, MAGIC, None, ALU.add, None, [tmpb], [tmpb])
                TS("dve", tmp, tmp, -MAGIC, None, ALU.add, None, [tmpb], [tmpb])
                STT("dve", tmp, tmp, -2 * math.pi, ang, ALU.mult, ALU.add, [tmpb, angb], [tmpb])
                TS("dve", tmp, tmp, shift, None, ALU.add, None, [tmpb], [tmpb])
                TS("dve", tmp, tmp, math.pi, -math.pi, ALU.min, ALU.max, [tmpb], [tmpb])
                ACT(dst, tmp, AF.Sin, [tmpb], [dstb])
            A.release(m3)
            if os.environ.get("MS") == "rope":
                return
            lam_init = 0.8 - 0.6 * math.exp(-0.3 * 0)
            lp, lpb = A.alloc([128, 128], F32, "lamp")
            lv = pv[0][:, PV0_LAM:PV0_LAM + 256]
            TT("dve", lp[:, 0:64], lv[:, 0:64], lv[:, 64:128], ALU.mult, [b_pv[0]], [lpb])
            TT("dve", lp[:, 64:128], lv[:, 128:192], lv[:, 192:256], ALU.mult, [b_pv[0]], [lpb])
            P.op("dve", lambda en: en.tensor_reduce(out=lamt[:, 0:2], in_=lp.rearrange("p (a b) -> p a b", a=2), axis=AX.X, op=ALU.add), [lpb], [lamb])
            ACT(lamt[:, 2:4], lamt[:, 0:2], AF.Exp, [lamb], [lamb])
            TT("dve", lamt[:, 4:5], lamt[:, 2:3], lamt[:, 3:4], ALU.subtract, [lamb], [lamb])
            TS("dve", lamt[:, 4:5], lamt[:, 4:5], lam_init, None, ALU.add, None, [lamb], [lamb])
            TS("dve", lamt[:, 5:6], lamt[:, 4:5], -1.0, None, ALU.mult, None, [lamb], [lamb])
            TS("dve", lamt[:, 6:7], pv[0][:, PV0_DN:PV0_DN + 1], 1.0 - lam_init, None, ALU.mult, None, [b_pv[0]], [lamb])
            LAMNEG = lamt[:, 5:6]
            DGAIN = lamt[:, 6:7]
            if os.environ.get("MS") == "lam":
                return
            xbf = [A.alloc([128, 512], BF16, f"xbf{i}") for i in range(2)]
            t1 = [A.alloc([128, 512], F32, f"rt1{i}") for i in range(2)]
            t2 = [A.alloc([128, 512], F32, f"rt2{i}") for i in range(2)]
            ii = 0
            for (c0, dst, dstb, scale) in ((0, aqT, aqb, 0.125), (512, akT, akb, 1.0)):
                w, wb = wload("w_in_0", 0, 8, c0, 512)
                for h in range(4):
                    for b in range(4):
                        bk, bkb = bank()
                        for c in range(8):
                            MM(bk, w[:, c, h * 128:(h + 1) * 128], hTb[:, c, b * 512:(b + 1) * 512], c == 0, c == 7, [wb, hb[b]], [bkb])
                        xb_, xbb = xbf[ii % 2]
                        a1, a1b = t1[ii % 2]
                        a2, a2b = t2[ii % 2]
                        ii += 1
                        CP("act", xb_, bk, [bkb], [xbb])
                        bk2, bk2b = bank()
                        MM(bk2, PERM_BF, xb_, True, True, [b_cbf, xbb], [bk2b])
                        STT("dve", a1, bk, scale, cosT[:, b * 512:(b + 1) * 512], ALU.mult, ALU.mult, [bkb, cosb], [a1b])
                        STT("dve", a2, bk2, scale, sinT[:, b * 512:(b + 1) * 512], ALU.mult, ALU.mult, [bk2b, sinb], [a2b])
                        TT("dve", dst[:, h, b * 512:(b + 1) * 512], a1, a2, ALU.add, [a1b, a2b], [dstb])
            A.release(m2)
            if os.environ.get("MS") == "p0a":
                return
            w, wb = wload("w_in_0", 0, 8, 1024, 512)
            for i in range(16):
                bk, bkb = bank()
                for c in range(8):
                    MM(bk, hTb[:, c, i * 128:(i + 1) * 128], w[:, c, :], c == 0, c == 7, [wb, hb[i // 4]], [bkb])
                CP("act", avt[:, i, :], bk, [bkb], [avb])
            if os.environ.get("MS") == "av":
                return
            NE = 3
            Eb = [[A.alloc([128, 512], BF16, f"E{i}{m}") for m in range(2)] for i in range(NE)]
            epi = []
            for i in range(2):
                epi.append(dict(of=A.alloc([128, 512], F32, f"dof{i}"), o2=A.alloc([128, 512], F32, f"do2{i}"),
                                r0=A.alloc([128, 512], F32, f"dr0{i}"), r1=A.alloc([128, 512], F32, f"dr1{i}"),
                                sq=A.alloc([128, 512], BF16, f"dsq{i}"), rs=A.alloc([128, 512], F32, f"drs{i}")))
            steps = [(h, qb, kt) for h in range(4) for qb in range(4) for kt in range(4 * (qb + 1))]
            sums = [bank_at(0), bank_at(1)]
            pvs = [bank_at(2), bank_at(3)]

            def geom(st):
                h, qb, kt = steps[st]
                jd = kt - 4 * qb
                cs = 128 * jd if jd > 0 else 0
                return h, qb, kt, jd, cs, 512 - cs

            def emit_S(st):
                h, qb, kt, jd, cs, n = geom(st)
                q0 = qb * 512
                for m in range(2):
                    bk, bkb = bank_at(4 + 2 * (st % 2) + m)
                    pr = slice(m * 64, (m + 1) * 64)
                    MM(bk[:, 0:n], akT[pr, h, kt * 128:(kt + 1) * 128], aqT[pr, h, q0 + cs:q0 + 512], True, True, [akb, aqb], [bkb])
                    e_, eb_ = Eb[st % NE][m]
                    ACT(e_[:, 0:n], bk[:, 0:n], AF.Exp, [bkb], [eb_])
                    if jd >= 0:
                        TT("dve", e_[:, 0:128], e_[:, 0:128], U_BF, ALU.mult, [eb_, b_cbf], [eb_])

            def emit_A(st):
                h, qb, kt, jd, cs, n = geom(st)
                nkt = 4 * (qb + 1)
                for m in range(2):
                    e_, eb_ = Eb[st % NE][m]
                    sm, smb = sums[m]
                    pv_, pvb = pvs[m]
                    MM(sm[:, cs:512], ONES_BF, e_[:, 0:n], kt == 0, kt == nkt - 1, [b_cbf, eb_], [smb])
                    MM(pv_[:, cs:512], avt[:, kt, h * 128:(h + 1) * 128], e_[:, 0:n], kt == 0, kt == nkt - 1, [avb, eb_], [pvb])

            def emit_epi(st, gi):
                h, qb, kt, jd, cs, n = geom(st)
                q0 = qb * 512
                E_ = epi[gi % 2]
                (of_, ofb), (o2_, o2b), (r0, r0b), (r1, r1b), (sq_, sqb), (rs_, rsb_) = E_["of"], E_["o2"], E_["r0"], E_["r1"], E_["sq"], E_["rs"]
                P.op("dve", lambda en: en.reciprocal(out=r0, in_=sums[0][0]), [sums[0][1]], [r0b])
                P.op("dve", lambda en: en.reciprocal(out=r1, in_=sums[1][0]), [sums[1][1]], [r1b])
                TT("dve", of_, pvs[0][0], r0, ALU.mult, [pvs[0][1], r0b], [ofb])
                TT("dve", o2_, pvs[1][0], r1, ALU.mult, [pvs[1][1], r1b], [o2b])
                STT("dve", of_, o2_, LAMNEG, of_, ALU.mult, ALU.add, [o2b, lamb, ofb], [ofb])
                ACT(sq_, of_, AF.Square, [ofb], [sqb])
                bk, bkb = bank_at(4 + 2 * (st % 2))
                MM(bk, ONES_BF, sq_, True, True, [b_cbf, sqb], [bkb])
                ACT(rs_, bk, AF.Ln, [bkb, b_small], [rsb_], bias=C_EPSRMS, scale=1.0 / 128.0)
                ACT(rs_, rs_, AF.Exp, [rsb_], [rsb_], scale=-0.5)
                STT("dve", oT[:, h, q0:q0 + 512], of_, DGAIN, rs_, ALU.mult, ALU.mult, [ofb, lamb, rsb_], [oTb])

            emit_S(0)
            gi = 0
            for st in range(len(steps)):
                if st + 1 < len(steps):
                    emit_S(st + 1)
                emit_A(st)
                h, qb, kt = steps[st]
                if kt == 4 * (qb + 1) - 1:
                    emit_epi(st, gi)
                    gi += 1
            A.release(m1)
            if os.environ.get("MS") == "attn":
                return
            dec, decb = A.alloc([128, 2, 16], F32, "dec")
            qbT, qbTb = A.alloc([128, 2, T], BF16, "qbT")
            kdT, kdTb = A.alloc([128, 2, T], BF16, "kdT")
            kgt, kgtb = A.alloc([128, 16, 256], BF16, "kgt")
            bvt, bvtb = A.alloc([128, 16, 512], BF16, "bvt")
            m4 = A.mark()
            gaug, gaugb = A.alloc([128, T], F32, "gaug")
            ebT, ebTb = A.alloc([128, 2, T], BF16, "ebT")
            enT, enTb = A.alloc([128, 2, T], BF16, "enT")
            erb, erbb = A.alloc([128, 16, 256], BF16, "erb")
            MEMSET("dve", gaug[0:32, :], 1.0, [gaugb])
            w, wb = wload("w_in_0", 0, 8, 3072, 16)
            for b in range(4):
                bk, bkb = bank()
                for c in range(8):
                    MM(bk[0:16, :], w[:, c, :], hTb[:, c, b * 512:(b + 1) * 512], c == 0, c == 7, [wb, hb[b]], [bkb])
                CP("act", gaug[0:16, b * 512:(b + 1) * 512], bk[0:16, :], [bkb], [gaugb])
            w2aug = pv[0][0:17, PV0_W2:PV0_W2 + 256]
            spb = [A.alloc([128, 256], F32, f"sp{i}") for i in range(2)]
            TRIS = cst[:, C_TRIS:C_TRIS + 128]
            TGTS = cst[:, C_TGTS:C_TGTS + 128]
            for i in range(16):
                sp_, spb_ = spb[i % 2]
                bk, bkb = bank()
                MM(bk[:, 0:256], gaug[0:17, i * 128:(i + 1) * 128], w2aug, True, True, [gaugb, b_pv[0]], [bkb])
                ACT(sp_, bk[:, 0:256], AF.Exp, [bkb], [spb_], scale=-1.0)
                ACT(sp_, sp_, AF.Ln, [spb_, b_small], [spb_], bias=C_ONE)
                bk2, bk2b = bank()
                for c in range(2):
                    MM(bk2[:, c * 128:(c + 1) * 128], sp_[:, c * 128:(c + 1) * 128], TRIS, True, True, [spb_, b_cst], [bk2b])
                b3 = bk2[:, 0:256].rearrange("p (a b) -> p a b", a=2)
                ACT(ebT[:, :, i * 128:(i + 1) * 128], b3, AF.Exp, [bk2b], [ebTb])
                ACT(enT[:, :, i * 128:(i + 1) * 128], b3, AF.Exp, [bk2b], [enTb], scale=-1.0)
                ACT(dec[:, :, i:i + 1], b3[:, :, 127:128], AF.Exp, [bk2b], [decb])
                bk3, bk3b = bank()
                MM(bk3[:, 0:256], TGTS, sp_, True, True, [b_cst, spb_], [bk3b])
                ACT(erb[:, i, :], bk3[:, 0:256], AF.Exp, [bk3b], [erbb])
            if os.environ.get("MS") == "gk":
                return
            w, wb = wload("w_in_0", 0, 8, 1536, 512)
            for jc in range(4):
                for b in range(4):
                    bk, bkb = bank()
                    for c in range(8):
                        MM(bk, w[:, c, jc * 128:(jc + 1) * 128], hTb[:, c, b * 512:(b + 1) * 512], c == 0, c == 7, [wb, hb[b]], [bkb])
                    if jc < 2:
                        STT("dve", qbT[:, jc, b * 512:(b + 1) * 512], bk, 0.125, ebT[:, jc, b * 512:(b + 1) * 512], ALU.mult, ALU.mult, [bkb, ebTb], [qbTb])
                    else:
                        TT("dve", kdT[:, jc - 2, b * 512:(b + 1) * 512], bk, enT[:, jc - 2, b * 512:(b + 1) * 512], ALU.mult, [bkb, enTb], [kdTb])
            for i in range(16):
                bk, bkb = bank()
                for c in range(8):
                    MM(bk[:, 0:256], hTb[:, c, i * 128:(i + 1) * 128], w[:, c, 256:512], c == 0, c == 7, [wb, hb[i // 4]], [bkb])
                TT("dve", kgt[:, i, :], bk[:, 0:256], erb[:, i, :], ALU.mult, [bkb, erbb], [kgtb])
            A.release(m4)
            w, wb = wload("w_in_0", 0, 8, 2048, 512)
            for i in range(16):
                bk, bkb = bank()
                for c in range(8):
                    MM(bk, hTb[:, c, i * 128:(i + 1) * 128], w[:, c, :], c == 0, c == 7, [wb, hb[i // 4]], [bkb])
                CP("act", bvt[:, i, :], bk, [bkb], [bvtb])
            srT, srTb = A.alloc([128, 4, T], BF16, "srT")
            w, wb = wload("w_in_0", 0, 8, 2560, 512)
            for jc in range(4):
                for b in range(4):
                    bk, bkb = bank()
                    for c in range(8):
                        MM(bk, w[:, c, jc * 128:(jc + 1) * 128], hTb[:, c, b * 512:(b + 1) * 512], c == 0, c == 7, [wb, hb[b]], [bkb])
                    ACT(srT[:, jc, b * 512:(b + 1) * 512], bk, AF.Silu, [bkb], [srTb])
            wmo, wmob = wload("w_mix_out_0", 0, 8, 0, D)
            if os.environ.get("MS") == "glaprep":
                return
            S, Sb = A.alloc([128, 2, 128], F32, "glaS")
            Sbf, Sbfb = A.alloc([128, 2, 128], BF16, "glaSb")
            MEMSET("dve", S, 0.0, [Sb])
            CP("act", Sbf, S, [Sb], [Sbfb])
            attm = [A.alloc([128, 4, 128], BF16, f"attm{i}") for i in range(2)]
            gsq, gsqb = A.alloc([128, 512], BF16, "gsq")
            grs, grsb = A.alloc([128, 512], F32, "grs")
            gt_, gtb = A.alloc([128, 512], F32, "gt")
            GG = pv[0][:, PV0_GN:PV0_GN + 1]
            GLS = int(os.environ.get("GLS", "9"))
            for i in range(int(os.environ.get("GLN", "16"))):
                tsl = slice(i * 128, (i + 1) * 128)
                am, amb = attm[i % 2]
                bkA2 = [bank(), bank()]
                for h in range(4):
                    c, hp = h // 2, h % 2
                    pr = slice(hp * 64, (hp + 1) * 64)
                    bkA, bkAb = bkA2[hp]
                    MM(bkA[:, c * 128:(c + 1) * 128], kdT[pr, c, tsl], qbT[pr, c, tsl], True, True, [kdTb, qbTb], [bkAb])
                for hp in range(2):
                    bkA, bkAb = bkA2[hp]
                    TT("dve", am[:, hp::2, :], bkA[:, 0:256].rearrange("p (a b) -> p a b", a=2), bm(U_F, 2), ALU.mult, [bkAb, b_cst], [amb])
                if GLS < 2:
                    continue
                bkB, bkBb = bank()
                for h in range(4):
                    c, hp = h // 2, h % 2
                    pr = slice(hp * 64, (hp + 1) * 64)
                    MM(bkB[:, h * 128:(h + 1) * 128], Sbf[pr, c, :], qbT[pr, c, tsl], True, False, [Sbfb, qbTb], [bkBb])
                    MM(bkB[:, h * 128:(h + 1) * 128], bvt[:, i, h * 128:(h + 1) * 128], am[:, h, :], False, True, [bvtb, amb], [bkBb])
                if GLS < 3:
                    continue
                ACT(gsq, bkB, AF.Square, [bkBb], [gsqb])
                bkC, bkCb = bank()
                MM(bkC, ONES_BF, gsq, True, True, [b_cbf, gsqb], [bkCb])
                ACT(grs, bkC, AF.Ln, [bkCb, b_small], [grsb], bias=C_EPSRMS, scale=1.0 / 128.0)
                ACT(grs, grs, AF.Exp, [grsb], [grsb], scale=-0.5)
                TT("dve", gt_, bkB, grs, ALU.mult, [bkBb, grsb], [gtb])
                STT("dve", oT[:, 4:8, tsl], gt_.rearrange("p (a b) -> p a b", a=4), GG, srT[:, :, tsl], ALU.mult, ALU.mult,
                    [gtb, b_pv[0], srTb], [oTb])
                if i < 15 and GLS >= 4:
                    bkD, bkDb = bank()
                    for c in range(2):
                        MM(bkD[:, c * 256:(c + 1) * 256], kgt[:, i, c * 128:(c + 1) * 128], bvt[:, i, c * 256:(c + 1) * 256], True, True, [kgtb, bvtb], [bkDb])
                    for c in range(2):
                        for hp in range(2):
                            pr = slice(hp * 64, (hp + 1) * 64)
                            STT("dve", S[pr, c, :], S[pr, c, :], dec[pr, c, i:i + 1], bkD[pr, c * 256 + hp * 128:c * 256 + hp * 128 + 128],
                                ALU.mult, ALU.add, [Sb, decb, bkDb], [Sb])
                    CP("act", Sbf, S, [Sb], [Sbfb])
            if dbg and not os.environ.get("NODUMP"):
                for c_ in range(8):
                    P.dma("sp", dbg_o[c_], oT[:, c_, :], oTb, reads=[oTb], writes=[outb])
            A.release(m1)
            if os.environ.get("MS") == "glaloop":
                return
            zbufs = [A.alloc([128, 8, 512], F32, f"mz{i}") for i in range(3)]
            proj_res_ln(0, 0, 8,
                        lambda c, j: (wmo[:, c, j * 128:(j + 1) * 128], [wmob]),
                        lambda c, b: (oT[:, c, b * 512:(b + 1) * 512], [oTb]),
                        range(4), zbufs)
            A.release(m0)

        def stage_mixer1():
            l = 1
            m0 = A.mark()
            ba, bab = A.alloc([128, 16, 16], F32, "gba")
            m1 = A.mark()
            pre = [A.alloc([128, T + 3], F32, f"gpre{i}") for i in range(2)]
            cvb = [A.alloc([128, T], F32, f"gcv{i}") for i in range(2)]
            sqb_ = [A.alloc([128, T], BF16, f"gsq{i}") for i in range(2)]
            rsb2 = [A.alloc([128, 512], F32, f"grs{i}") for i in range(2)]
            stg = [A.alloc([128, T], BF16, f"gst{i}") for i in range(2)]
            cw = pv[1][:, PV1_CW:PV1_CW + 96]
            pi = 0
            pend1 = [None]
            for g in range(6):
                w, wb = wload("w_in_1", 0, 8, g * 512, 512)
                for cc in range(4):
                    ch = g * 4 + cc
                    p_, pb_ = pre[pi % 2]
                    y_, yb_ = cvb[pi % 2]
                    s_, sb_ = sqb_[pi % 2]
                    sg, sgb = stg[pi % 2]
                    pi += 1
                    MEMSET("dve", p_[:, 0:3], 0.0, [pb_])
                    for b in range(4):
                        bk, bkb = bank()
                        for c in range(8):
                            MM(bk, w[:, c, cc * 128:(cc + 1) * 128], hTb[:, c, b * 512:(b + 1) * 512], c == 0, c == 7, [wb, hb[b]], [bkb])
                        CP("act", p_[:, 3 + b * 512:3 + (b + 1) * 512], bk, [bkb], [pb_])
                    TS("dve", y_, p_[:, 3:T + 3], cw[:, ch * 4 + 3:ch * 4 + 4], None, ALU.mult, None, [pb_, b_pv[1]], [yb_])
                    for j in range(3):
                        STT("dve", y_, p_[:, j:T + j], cw[:, ch * 4 + j:ch * 4 + j + 1], y_, ALU.mult, ALU.add, [pb_, b_pv[1], yb_], [yb_])
                    def tail(ch=ch, y_=y_, yb_=yb_, s_=s_, sb_=sb_, sg=sg, sgb=sgb):
                        ACT(y_, y_, AF.Silu, [yb_], [yb_])
                        if ch < 16:
                            hh = ch % 8
                            scale = (128.0 ** -0.5) if ch < 8 else 1.0
                            ACT(s_, y_, AF.Square, [yb_], [sb_])
                            for b in range(4):
                                bk, bkb = bank()
                                MM(bk, ONES_BF, s_[:, b * 512:(b + 1) * 512], True, True, [b_cbf, sb_], [bkb])
                                r_, rb_ = rsb2[b % 2]
                                ACT(r_, bk, AF.Ln, [bkb, b_small], [rb_], bias=C_EPSRMS)
                                ACT(r_, r_, AF.Exp, [rb_], [rb_], scale=-0.5)
                                STT("dve", sg[:, b * 512:(b + 1) * 512], y_[:, b * 512:(b + 1) * 512], scale, r_, ALU.mult, ALU.mult, [yb_, rb_], [sgb])
                            if ch < 8:
                                P.dma("sp", sc_q[hh], sg, sgb, reads=[sgb], writes=[scqb])
                            else:
                                P.dma("sp", sc_k[hh], sg, sgb, reads=[sgb], writes=[sckb])
                        else:
                            CP("dve", sg, y_, [yb_], [sgb])
                            P.dma("sp", sc_v[ch - 16], sg, sgb, reads=[sgb], writes=[scvb])
                    tail()
            for g in range(2):
                w, wb = wload("w_in_1", 0, 8, 3072 + g * 512, 512)
                for cc in range(4):
                    ch = g * 4 + cc
                    sg, sgb = stg[ch % 2]
                    for b in range(4):
                        bk, bkb = bank()
                        for c in range(8):
                            MM(bk, w[:, c, cc * 128:(cc + 1) * 128], hTb[:, c, b * 512:(b + 1) * 512], c == 0, c == 7, [wb, hb[b]], [bkb])
                        ACT(sg[:, b * 512:(b + 1) * 512], bk, AF.Silu, [bkb], [sgb])
                    P.dma("sp", sc_z[ch], sg, sgb, reads=[sgb], writes=[sczb])
            w, wb = wload("w_in_1", 0, 8, 4096, 16)
            for i in range(16):
                bk, bkb = bank()
                for c in range(8):
                    MM(bk[:, 0:16], hTb[:, c, i * 128:(i + 1) * 128], w[:, c, :], c == 0, c == 7, [wb, hb[i // 4]], [bkb])
                CP("act", ba[:, i, :], bk[:, 0:16], [bkb], [bab])
            A.release(m1)
            wmo, wmob = wload("w_mix_out_1", 0, 8, 0, D)
            beta, betab = A.alloc([128, 16, 8], F32, "gbeta")
            lbt, lbtb = A.alloc([128, 16, 8], F32, "glb")
            gg, ggb = A.alloc([128, 16, 8], F32, "gg")
            negA, negAb = A.alloc([128, 8], F32, "gnegA")
            ACT(beta, ba[:, :, 0:8], AF.Exp, [bab], [betab], scale=-1.0)
            TS("dve", beta, beta, 1.0, None, ALU.add, None, [betab], [betab])
            P.op("dve", lambda en: en.reciprocal(out=beta, in_=beta), [betab], [betab])
            ACT(lbt, beta, AF.Ln, [betab], [lbtb])
            TT("dve", gg, ba[:, :, 8:16], pv[1][:, PV1_DT:PV1_DT + 8].unsqueeze(1).to_broadcast([128, 16, 8]), ALU.add, [bab, b_pv[1]], [ggb])
            ACT(gg, gg, AF.Exp, [ggb], [ggb])
            ACT(gg, gg, AF.Ln, [ggb, b_small], [ggb], bias=C_ONE)
            ACT(negA, pv[1][:, PV1_AL:PV1_AL + 8], AF.Exp, [b_pv[1]], [negAb])
            TS("dve", negA, negA, -1.0, None, ALU.mult, None, [negAb], [negAb])
            TT("dve", gg, gg, negA.unsqueeze(1).to_broadcast([128, 16, 8]), ALU.mult, [ggb, negAb], [ggb])
            S, Sb = A.alloc([128, 8, 128], F32, "gS")
            Sbf, Sbfb = A.alloc([128, 8, 128], BF16, "gSbf")
            MEMSET("dve", S, 0.0, [Sb])
            CP("act", Sbf, S, [Sb], [Sbfb])
            vti = [A.alloc([128, 8, 128], BF16, f"gvt{i}") for i in range(2)]
            zti = [A.alloc([128, 8, 128], BF16, f"gzt{i}") for i in range(3)]
            qti = [A.alloc([128, 8, 128], BF16, f"gqt{i}") for i in range(2)]
            kti = [A.alloc([128, 8, 128], BF16, f"gkt{i}") for i in range(2)]
            sc = [A.alloc([128, 48], F32, f"gsc{i}") for i in range(2)]
            Dg, Dgb = A.alloc([128, 8, 128], F32, "gDg")
            Da, Dab = A.alloc([128, 8, 128], F32, "gDa")
            X1, X1b = A.alloc([128, 8, 128], F32, "gX1")
            X2, X2b = A.alloc([128, 8, 128], F32, "gX2")
            X3, X3b = A.alloc([128, 8, 128], F32, "gX3")
            egr, egrb = A.alloc([128, 8, 128], F32, "gegr")
            Qa = [A.alloc([128, 8, 128], F32, f"gQ{i}") for i in range(2)]
            QTa = [A.alloc([128, 8, 128], F32, f"gQT{i}") for i in range(2)]
            RT, RTb = A.alloc([128, 8, 128], F32, "gRT")
            vbt, vbtb = A.alloc([128, 8, 128], F32, "gvb")
            kbg, kbgb = A.alloc([128, 8, 128], F32, "gkbg")
            qkTs = [A.alloc([128, 8, 128], BF16, f"gqk{i}") for i in range(2)]
            qgTs = [A.alloc([128, 8, 128], BF16, f"gqg{i}") for i in range(2)]
            kgbs = [A.alloc([128, 8, 128], BF16, f"gkg{i}") for i in range(2)]
            uus = [A.alloc([128, 8, 128], F32, f"gu{i}") for i in range(2)]
            wTs = [A.alloc([128, 8, 128], BF16, f"gwT{i}") for i in range(2)]
            vn, vnb = A.alloc([128, 8, 128], BF16, "gvn")
            osq, osqb = A.alloc([128, 8, 128], BF16, "gosq")
            ors, orsb = A.alloc([128, 8, 128], F32, "gors")
            ot_, otb_ = A.alloc([128, 8, 128], F32, "got")
            GN = pv[1][:, PV1_GN:PV1_GN + 1]
            B1 = cst[:, C_B1:C_B1 + 512]
            B2 = cst[:, C_B2:C_B2 + 512]
            B3 = cst[:, C_B3:C_B3 + 512]

            def v3(ap):
                return ap.rearrange("p (a b) -> p a b", a=8)

            F32R = mybir.dt.float32r
            USE_R = os.environ.get("F32R", "0") == "1"

            def rr_(ap):
                return ap.bitcast(F32R) if USE_R else ap

            def ld(i):
                vt_, vtb_ = vti[i % 2]
                zt_, ztb_ = zti[i % 3]
                P.dma("sp", vt_, sc_v.rearrange("c p t -> p c t")[:, :, i * 128:(i + 1) * 128], vtb_, reads=[scvb], writes=[vtb_])
                P.dma("sp", zt_, sc_z.rearrange("c p t -> p c t")[:, :, i * 128:(i + 1) * 128], ztb_, reads=[sczb], writes=[ztb_])
                qt_, qtb_ = qti[i % 2]
                kt_, ktb_ = kti[i % 2]
                P.dma("sp", qt_, sc_q.rearrange("c p t -> p c t")[:, :, i * 128:(i + 1) * 128], qtb_, reads=[scqb], writes=[qtb_])
                P.dma("sp", kt_, sc_k.rearrange("c p t -> p c t")[:, :, i * 128:(i + 1) * 128], ktb_, reads=[sckb], writes=[ktb_])

            def make_prep(i):
                par = i % 2
                vt_, vtb_ = vti[i % 2]
                qt_, qtb_ = qti[i % 2]
                kt_, ktb_ = kti[i % 2]
                s_, sb_ = sc[par]
                gc = s_[:, 0:8]
                aa = s_[:, 8:16]
                glast = s_[:, 16:24]
                kgs = s_[:, 24:32]
                bgc = s_[:, 32:40]
                qkT, qkTb = qkTs[par]
                qgT, qgTb = qgTs[par]
                kgb_, kgbb = kgbs[par]
                uu, uub = uus[par]
                wT, wTb = wTs[par]
                Dg2 = Dg.rearrange("p a b -> p (a b)")
                Da2 = Da.rearrange("p a b -> p (a b)")
                segs = []

                def s0():
                    if i + 1 < 16:
                        ld(i + 1)
                    bk, bkb = bank()
                    MM(bk[:, 0:8], U_F, gg[:, i, :], True, True, [b_cst, ggb], [bkb])
                    CP("act", gc, bk[:, 0:8], [bkb], [sb_])
                    TT("dve", aa, gc, lbt[:, i, :], ALU.add, [sb_, lbtb], [sb_])
                    TT("dve", Dg, bm(ID_F, 8), bc(gc, 128), ALU.mult, [b_cst, sb_], [Dgb])
                    TT("dve", Da, bm(ID_F, 8), bc(aa, 128), ALU.mult, [b_cst, sb_], [Dab])
                    pR, pRb = pair()
                    for hh in range(2):
                        MM(pR[:, hh * 512:(hh + 1) * 512], ONES_F, Dg2[:, hh * 512:(hh + 1) * 512], True, True, [b_cst, Dgb], pRb)
                    ACT(egr, v3(pR), AF.Exp, pRb, [egrb])
                    ACT(glast, v3(pR)[:, :, 127], AF.Exp, pRb, [sb_])
                    TT("dve", kgs, v3(pR)[:, :, 127], gc, ALU.subtract, pRb + [sb_], [sb_])
                    ACT(kgs, kgs, AF.Exp, [sb_], [sb_])
                    ACT(bgc, aa, AF.Exp, [sb_], [sb_])
                    TT("dve", qgT, qt_, egr, ALU.mult, [qtb_, egrb], [qgTb])
                segs.append(s0)

                def s1():
                    p1, p1b = pair()
                    for hh in range(2):
                        MM(p1[:, hh * 512:(hh + 1) * 512], ONES_F, Dg2[:, hh * 512:(hh + 1) * 512], True, False, [b_cst, Dgb], p1b)
                        MM(p1[:, hh * 512:(hh + 1) * 512], ID_F, B1, False, True, [b_cst], p1b)
                    STT("dve", X1, v3(p1), -1.0, bc(aa, 128), ALU.mult, ALU.add, p1b + [sb_], [X1b])
                    ACT(X1, X1, AF.Exp, [X1b], [X1b])
                    p2, p2b = pair()
                    for hh in range(2):
                        MM(p2[:, hh * 512:(hh + 1) * 512], ONES_F, Da2[:, hh * 512:(hh + 1) * 512], True, False, [b_cst, Dab], p2b)
                        MM(p2[:, hh * 512:(hh + 1) * 512], ID_F, B2, False, True, [b_cst], p2b)
                    TT("dve", X2, v3(p2), bc(gc, 128), ALU.subtract, p2b + [sb_], [X2b])
                    ACT(X2, X2, AF.Exp, [X2b], [X2b])
                segs.append(s1)

                def s2():
                    p3, p3b = pair()
                    for hh in range(2):
                        MM(p3[:, hh * 512:(hh + 1) * 512], ONES_F, Dg2[:, hh * 512:(hh + 1) * 512], True, False, [b_cst, Dgb], p3b)
                        MM(p3[:, hh * 512:(hh + 1) * 512], ID_F, B3, False, True, [b_cst], p3b)
                    TT("dve", X3, v3(p3), bc(gc, 128), ALU.subtract, p3b + [sb_], [X3b])
                    ACT(X3, X3, AF.Exp, [X3b], [X3b])
                    pA, pAb = pair()
                    for h in range(8):
                        MM(pA[:, h * 128:(h + 1) * 128], kt_[:, h, :], kt_[:, h, :], True, True, [ktb_], pAb)
                    Q0, Q0b = Qa[0]
                    QT0, QT0b = QTa[0]
                    TT("dve", Q0, v3(pA), X1, ALU.mult, pAb + [X1b], [Q0b])
                    TT("dve", QT0, v3(pA), X2, ALU.mult, pAb + [X2b], [QT0b])
                    pB, pBb = pair()
                    for h in range(8):
                        MM(pB[:, h * 128:(h + 1) * 128], kt_[:, h, :], qt_[:, h, :], True, True, [ktb_, qtb_], pBb)
                    TT("dve", qkT, v3(pB), X3, ALU.mult, pBb + [X3b], [qkTb])
                    TT("dve", RT, bm(ID_F, 8), QT0, ALU.subtract, [b_cst, QT0b], [RTb])
                segs.append(s2)

                def mk_neu(k):
                    def f():
                        cur = (k - 1) % 2
                        Qp, Qpb = Qa[cur]
                        QTp, QTpb = QTa[cur]
                        Qn, Qnb = Qa[1 - cur]
                        QTn, QTnb = QTa[1 - cur]
                        pq, pqb = pair()
                        for h in range(8):
                            MM(pq[:, h * 128:(h + 1) * 128], rr_(QTp[:, h, :]), rr_(Qp[:, h, :]), True, True, [QTpb, Qpb], pqb)
                        CP("act", Qn, v3(pq), pqb, [Qnb])
                        if k < 6:
                            pqt, pqtb = pair()
                            if os.environ.get("QTMM", "1") == "1":
                                for h in range(8):
                                    MM(pqt[:, h * 128:(h + 1) * 128], rr_(Qp[:, h, :]), rr_(QTp[:, h, :]), True, True, [QTpb, Qpb], pqtb)
                            else:
                                for h in range(8):
                                    TR(pqt[:, h * 128:(h + 1) * 128], Qn[:, h, :], ID_F, [Qnb, b_cst], pqtb)
                            CP("act", QTn, v3(pqt), pqtb, [QTnb])
                        pr_, prb_ = pair()
                        for h in range(8):
                            MM(pr_[:, h * 128:(h + 1) * 128], rr_(Qn[:, h, :]), rr_(RT[:, h, :]), True, True, [Qnb, RTb], prb_)
                        TT("dve", RT, RT, v3(pr_), ALU.add, [RTb] + prb_, [RTb])
                    return f
                for k in range(1, 7):
                    segs.append(mk_neu(k))

                def s9():
                    bkk, bkkb = bank()
                    kk3 = bkk.bitcast(BF16).rearrange("p (a b) -> p a b", a=8)
                    for h in range(8):
                        TR(kk3[:, h, :], kt_[:, h, :], ID_BF, [ktb_, b_cbf], [bkkb])
                    TT("dve", kbg, kk3, bc(bgc, 128), ALU.mult, [bkkb, sb_], [kbgb])
                    TT("dve", kgb_, kk3, bc(kgs, 128), ALU.mult, [bkkb, sb_], [kgbb])
                    bkv, bkvb = bank()
                    vv3 = bkv.bitcast(BF16).rearrange("p (a b) -> p a b", a=8)
                    for h in range(8):
                        TR(vv3[:, h, :], vt_[:, h, :], ID_BF, [vtb_, b_cbf], [bkvb])
                    TT("dve", vbt, vv3, bc(beta[:, i, :], 128), ALU.mult, [bkvb, betab], [vbtb])
                segs.append(s9)

                def s10():
                    pu, pub = pair()
                    for h in range(8):
                        MM(pu[:, h * 128:(h + 1) * 128], rr_(RT[:, h, :]), rr_(vbt[:, h, :]), True, True, [RTb, vbtb], pub)
                    CP("act", uu, v3(pu), pub, [uub])
                    pw, pwb = pair()
                    for h in range(8):
                        MM(pw[:, h * 128:(h + 1) * 128], rr_(kbg[:, h, :]), rr_(RT[:, h, :]), True, True, [kbgb, RTb], pwb)
                    CP("act", wT, v3(pw), pwb, [wTb])
                segs.append(s10)
                return segs

            def make_scan(i):
                par = i % 2
                tsl = slice(i * 128, (i + 1) * 128)
                zt_, ztb_ = zti[i % 3]
                s_, sb_ = sc[par]
                glast = s_[:, 16:24]
                qkT, qkTb = qkTs[par]
                qgT, qgTb = qgTs[par]
                kgb_, kgbb = kgbs[par]
                uu, uub = uus[par]
                wT, wTb = wTs[par]
                hold = {}

                def t0():
                    pv_, pvb_ = pair()
                    for h in range(8):
                        MM(pv_[:, h * 128:(h + 1) * 128], wT[:, h, :], Sbf[:, h, :], True, True, [wTb, Sbfb], pvb_)
                    TT("dve", vn, uu, v3(pv_), ALU.subtract, [uub] + pvb_, [vnb])

                def t1():
                    po, pob = pair()
                    hold["po"] = (po, pob)
                    for h in range(8):
                        MM(po[:, h * 128:(h + 1) * 128], Sbf[:, h, :], qgT[:, h, :], True, False, [Sbfb, qgTb], pob)
                        MM(po[:, h * 128:(h + 1) * 128], vn[:, h, :], qkT[:, h, :], False, True, [vnb, qkTb], pob)
                    if i < 15:
                        pS, pSb = pair()
                        for h in range(8):
                            MM(pS[:, h * 128:(h + 1) * 128], kgb_[:, h, :], vn[:, h, :], True, True, [kgbb, vnb], pSb)
                        TT("dve", S, S, bc(glast, 128), ALU.mult, [Sb, sb_], [Sb])
                        TT("dve", S, S, v3(pS), ALU.add, [Sb] + pSb, [Sb])
                        CP("act", Sbf, S, [Sb], [Sbfb])
                    po, pob = hold["po"]
                    ACT(osq, v3(po), AF.Square, pob, [osqb])
                    CP("act", ot_, v3(po), pob, [otb_])

                def t2():
                    pn, pnb = pair()
                    osq2 = osq.rearrange("p a b -> p (a b)")
                    for hh in range(2):
                        MM(pn[:, hh * 512:(hh + 1) * 512], ONES_BF, osq2[:, hh * 512:(hh + 1) * 512], True, True, [b_cbf, osqb], pnb)
                    ACT(ors, v3(pn), AF.Ln, pnb + [b_small], [orsb], bias=C_EPSRMS, scale=1.0 / 128.0)
                    ACT(ors, ors, AF.Exp, [orsb], [orsb], scale=-0.5)

                def t3():
                    TT("dve", ot_, ot_, ors, ALU.mult, [otb_, orsb], [otb_])
                    STT("dve", hTb[:, :, tsl], ot_, GN, zt_, ALU.mult, ALU.mult, [otb_, b_pv[1], ztb_], [hb[i // 4]])
                return [t0, t1, t2, t3]

            ld(0)
            for f in make_prep(0):
                f()
            for i in range(16):
                Pq = make_prep(i + 1) if i + 1 < 16 else []
                Sq = make_scan(i)
                order = []
                pi_, si_ = 0, 0
                plan = "PPSPPSPPSPPSPPP"
                for ch_ in plan:
                    if ch_ == "P":
                        if pi_ < len(Pq):
                            order.append(Pq[pi_])
                            pi_ += 1
                    else:
                        order.append(Sq[si_])
                        si_ += 1
                while pi_ < len(Pq):
                    order.append(Pq[pi_])
                    pi_ += 1
                while si_ < len(Sq):
                    order.append(Sq[si_])
                    si_ += 1
                for f in order:
                    f()
            if dbg:
                P.dma("sp", dbg_o.rearrange("c p t -> p c t"), hTb[:], hb[0], reads=hb, writes=[outb])
            A.release(m0)
            zbufs = [A.alloc([128, 8, 512], F32, f"mz{i}") for i in range(4)]
            proj_res_ln(1, 0, 8,
                        lambda c, j: (wmo[:, c, j * 128:(j + 1) * 128], [wmob]),
                        lambda c, b: (hTb[:, c, b * 512:(b + 1) * 512], [hb[b]]),
                        range(4), zbufs)
            A.release(m0)

        def stage_touch():
            m0 = A.mark()
            tt_, ttb = A.alloc([128, 64], F32, "touch")
            for n in ["x", "mem"] + WNAMES:
                P.dma("sp", tt_[0:1, 0:16], dr[n][0:1, 0:16], ttb, writes=[ttb])
            ti_, tib = A.alloc([128, 64], I32, "touchi")
            P.dma("sp", ti_[0:1, 0:16], dr["pos"][0:1, 0:16], tib, writes=[tib])
            P.dma("sp", out_d[0:1, 0:16], tt_[0:1, 0:16], ttb, reads=[ttb], writes=[outb])
            A.release(m0)

        stages = [("touch", stage_touch), ("none", lambda: None), ("in", stage_in), ("mix0", stage_mixer0), ("xa0", lambda: stage_xattn(0)), ("ffn0", lambda: stage_ffn2(0)),
                  ("mix1", stage_mixer1), ("xa1", lambda: stage_xattn(1)), ("ffn1", lambda: stage_ffn2(1, final=True))]
        build.stage_cost = {}
        for nm, fn in stages:
            c0 = dict(cost)
            A.peak = 0
            fn()
            build.stage_cost.setdefault("_peak", {})[nm] = A.peak
            build.stage_cost[nm] = {k: round(cost[k] - c0[k], 1) for k in cost}
            if stop == nm:
                break
        P.op("sp", lambda en: en.nop(), reads=[outb] + resb + [scvb, sczb, scqb, sckb])
        P.emit()
        build.stats = P.stats
    return nc


_NC_CACHE = {}


def kernel(**inputs):
    inp = {k: np.asarray(v) for k, v in inputs.items()}
    if "full" not in _NC_CACHE:
        _NC_CACHE["full"] = build()
    nc = _NC_CACHE["full"]
    consts = make_consts()
    pv0 = make_pv(inp, 0)
    pv1 = make_pv(inp, 1)
    shared = {"consts": consts, "pv0": pv0, "pv1": pv1}
    for n in WNAMES:
        shared[n] = np.ascontiguousarray(inp[n], dtype=np.float32)
    in_maps = []
    for b in range(8):
        m = dict(shared)
        m["x"] = np.ascontiguousarray(inp["x"][b], dtype=np.float32)
        m["mem"] = np.ascontiguousarray(inp["mem"][b], dtype=np.float32)
        m["pos"] = np.ascontiguousarray(inp["positions"][b].reshape(1, T).astype(np.int32))
        in_maps.append(m)
    res = run_bass_kernel_spmd(nc, in_maps, core_ids=list(range(8)))
    return np.stack([np.asarray(r["out"], dtype=np.float32) for r in res.results], axis=0)
```

```python
import math
import os
from contextlib import ExitStack
import numpy as np
import concourse.bass as bass
import concourse.mybir as mybir
from concourse.bass_utils import run_bass_kernel_spmd

F32 = mybir.dt.float32
BF16 = mybir.dt.bfloat16
I32 = mybir.dt.int32
U8 = mybir.dt.uint8
AF = mybir.ActivationFunctionType
ALU = mybir.AluOpType
AX = mybir.AxisListType

ENGS = ("pe", "act", "dve", "pool", "sp")


class Buf:
    __slots__ = ("name", "lastw", "readers", "dsem", "excl")

    def __init__(self, name):
        self.name = name
        self.lastw = []
        self.readers = []
        self.dsem = None
        self.excl = False


class DmaSem:
    __slots__ = ("h", "total", "last", "key")

    def __init__(self, h, key):
        self.h = h
        self.total = 0
        self.last = None
        self.key = key


class Op:
    __slots__ = ("eng", "fn", "deps", "dma", "sig", "cnt", "clock", "signal")

    def __init__(self, eng, fn, dma=None):
        self.eng = eng
        self.fn = fn
        self.deps = []
        self.dma = dma
        self.sig = None
        self.cnt = None
        self.clock = None
        self.signal = False


class Prog:
    def __init__(self, nc, stack):
        self.nc = nc
        self.stack = stack
        self.ops = []
        self.nsem = 0
        self.esem = {e: stack.enter_context(nc.semaphore("es_" + e)) for e in ENGS}
        self.nbuf = 0

    def sb(self, name, shape, dtype):
        return self.stack.enter_context(self.nc.sbuf_tensor(name, list(shape), dtype))

    def ps(self, name, shape, dtype=F32):
        return self.stack.enter_context(self.nc.psum_tensor(name, list(shape), dtype))

    def buf(self, name=None):
        self.nbuf += 1
        return Buf(name or f"b{self.nbuf}")

    def _dsem(self, b):
        if b.dsem is None:
            self.nsem += 1
            h = self.stack.enter_context(self.nc.semaphore(f"ds{self.nsem}"))
            b.dsem = DmaSem(h, self.nsem)
        return b.dsem

    def _deps(self, op, reads, writes):
        deps = []
        for b in reads:
            deps.extend(b.lastw)
            if b.excl:
                deps.extend(b.readers)
        for b in writes:
            deps.extend(b.lastw)
            deps.extend(b.readers)
        seen = set(id(d) for d in op.deps)
        for d in deps:
            if d is op or id(d) in seen:
                continue
            seen.add(id(d))
            op.deps.append(d)

    def _update(self, op, reads, writes):
        for b in reads:
            if b not in writes:
                b.readers.append(op)
        for b in writes:
            b.lastw = [op]
            b.readers = []

    def op(self, eng, fn, reads=(), writes=()):
        o = Op(eng, fn)
        reads = list(reads)
        writes = list(writes)
        self._deps(o, reads, writes)
        self._update(o, reads, writes)
        self.ops.append(o)
        return o

    def dma(self, eng, out, in_, sbuf, reads=(), writes=()):
        return self.dma_group(eng, [(out, in_)], sbuf, reads, writes)

    def dma_group(self, eng, pairs, sbuf, reads=(), writes=()):
        ds = self._dsem(sbuf)
        reads = list(reads)
        writes = list(writes)
        ops = []
        for (out, in_) in pairs:
            o = Op(eng, (lambda e, out=out, in_=in_: e.dma_start(out=out, in_=in_)), dma=ds)
            if ds.last is not None:
                o.deps.append(ds.last)
            self._deps(o, reads, writes)
            ops.append(o)
        for o in ops:
            ds.total += 16
            o.sig = ds.total
            self.ops.append(o)
        last = ops[-1]
        ds.last = last
        for o in ops[:-1]:
            for b in reads:
                if b not in writes:
                    b.readers.append(o)
        self._update(last, reads, writes)
        return last

    def emit(self):
        nc = self.nc
        for o in self.ops:
            for d in o.deps:
                if d.dma is None:
                    if d.eng == "pe" and o.eng == "pe" and o.dma is None:
                        continue
                    d.signal = True
        cnt = {e: 0 for e in ENGS}
        seen = {e: {x: 0 for x in ENGS} for e in ENGS}
        seen_d = {e: {} for e in ENGS}
        per_eng = {e: [] for e in ENGS}
        nwait = 0
        for o in self.ops:
            E = o.eng
            sE = seen[E]
            wm = {}
            for d in o.deps:
                if d.dma is not None:
                    k = d.dma.key
                    if seen_d[E].get(k, 0) < d.sig:
                        seen_d[E][k] = d.sig
                        kk = ("d", k)
                        if kk not in wm or wm[kk][1] < d.sig:
                            wm[kk] = (d.dma.h, d.sig)
                else:
                    if d.eng == "pe" and E == "pe" and o.dma is None:
                        continue
                    if sE[d.eng] < d.cnt:
                        kk = ("e", d.eng)
                        if kk not in wm or wm[kk][1] < d.cnt:
                            wm[kk] = (self.esem[d.eng], d.cnt)
                        for x in ENGS:
                            if d.clock[x] > sE[x]:
                                sE[x] = d.clock[x]
            waits = list(wm.values())
            nwait += len(waits)
            if o.dma is None and o.signal:
                cnt[E] += 1
                o.cnt = cnt[E]
                clk = dict(sE)
                clk[E] = o.cnt
                o.clock = clk
            per_eng[E].append((o, waits))
        import os as _os
        if _os.environ.get("DUMPW"):
            names = {id(self.esem[e]): "E_" + e for e in ENGS}
            for e in ENGS:
                print("ENGINE", e)
                for o, waits in per_eng[e]:
                    ws = [(names.get(id(h), "dsem"), v) for h, v in waits]
                    print("   ", "dma" if o.dma is not None else "op", "sig" if o.signal else "", o.cnt, o.sig, (o.dma.key if o.dma else ""), ws)
        self.stats = dict(nops=len(self.ops), nwait=nwait, cnt=dict(cnt), nsem=self.nsem,
                          per_eng={e: len(per_eng[e]) for e in ENGS})
        assert max(cnt.values()) < 60000, cnt
        esem = self.esem
        with nc.Block() as block:
            def run(e, lst, E):
                for o, waits in lst:
                    for h, v in waits:
                        e.wait_ge(h, v)
                    ins = o.fn(e)
                    if o.dma is not None:
                        ins.then_inc(o.dma.h, 16)
                    elif o.signal:
                        ins.then_inc(esem[E], 1)

            @block.tensor
            def _(e):
                run(e, per_eng["pe"], "pe")

            @block.scalar
            def _(e):
                run(e, per_eng["act"], "act")

            @block.vector
            def _(e):
                run(e, per_eng["dve"], "dve")

            @block.gpsimd
            def _(e):
                run(e, per_eng["pool"], "pool")

            @block.sync
            def _(e):
                run(e, per_eng["sp"], "sp")


class Arena:
    def __init__(self, P, tensor, nbytes):
        self.P = P
        self.t = tensor
        self.n = nbytes
        self.off = 0
        self.hist = []

    def mark(self):
        return self.off

    def release(self, m):
        self.off = m

    def alloc(self, shape, dtype, name=None):
        esz = 4 if dtype in (F32, I32) else 2
        n = int(np.prod(shape[1:])) * esz
        s = (self.off + 63) // 64 * 64
        e = s + n
        assert e <= self.n, f"arena overflow {name} {e} > {self.n}"
        self.off = e
        self.peak = max(getattr(self, "peak", 0), e)
        v = self.t[:, s:e].bitcast(dtype)
        if len(shape) == 3:
            v = v.rearrange("p (a b) -> p a b", a=shape[1])
        if shape[0] < 128:
            v = v[0:shape[0]]
        b = self.P.buf(name)
        keep = []
        for (s2, e2, b2) in self.hist:
            if s2 < e and s < e2:
                b.readers.extend(b2.lastw)
                b.readers.extend(b2.readers)
                if s <= s2 and e2 <= e:
                    continue
            keep.append((s2, e2, b2))
        keep.append((s, e, b))
        self.hist = keep
        return v, b


T = 2048
D = 1024
NMEM = 256
DFF = 2816
P0W = 3088
P1W = 4112
ALPHA = (2.0 * 2) ** 0.25
LN_EPS = 1e-5
RMS_EPS = 1e-6
BIG = 1.0e30

C_ID, C_PERM, C_U, C_TRIS, C_TGTS, C_B1, C_B2, C_B3 = 0, 128, 256, 384, 512, 640, 1152, 1664
C_INVF = 2176
C_ONES = 2177
NCONST = 2305

PV_LN = 0
PV_FCW = 48
PV_FCB = 180
PV_X = 224
PV0_DN, PV0_GN, PV0_LAM, PV0_W2 = 224, 225, 226, 482
PV1_CW, PV1_AL, PV1_DT, PV1_GN = 224, 320, 328, 336
NPV = 768
NPV1 = 384


def make_consts():
    c = np.zeros((128, NCONST), np.float32)
    i = np.arange(128)
    c[:, C_ID:C_ID + 128] = np.eye(128)
    perm = np.zeros((128, 128), np.float32)
    for fo in range(128):
        r = fo % 64
        if r < 8:
            perm[fo + 8, fo] = -1.0
        elif r < 16:
            perm[fo - 8, fo] = 1.0
    c[:, C_PERM:C_PERM + 128] = perm
    le = (i[:, None] <= i[None, :]).astype(np.float32)
    c[:, C_U:C_U + 128] = le
    c[:, C_TRIS:C_TRIS + 128] = le * (-1.0 / 16.0)
    c[:, C_TGTS:C_TGTS + 128] = (i[:, None] > i[None, :]).astype(np.float32) * (-1.0 / 16.0)
    b1 = BIG * (i[None, :] >= i[:, None])
    b2 = -BIG * (i[None, :] <= i[:, None])
    b3 = -BIG * (i[None, :] < i[:, None])
    c[:, C_B1:C_B1 + 512] = np.tile(b1, (1, 4))
    c[:, C_B2:C_B2 + 512] = np.tile(b2, (1, 4))
    c[:, C_B3:C_B3 + 512] = np.tile(b3, (1, 4))
    invf = 500000.0 ** (-np.arange(0, 16, 2, dtype=np.float32) / 16.0)
    col = np.zeros(128, np.float32)
    for p in range(128):
        r = p % 64
        if r < 16:
            col[p] = invf[r % 8]
    c[:, C_INVF] = col
    c[:, C_ONES:C_ONES + 128] = 1.0
    return c


def chunkcols(v):
    return np.ascontiguousarray(v.reshape(-1, 128).T)


def make_pv(inp, l):
    pv = np.zeros((128, NPV), np.float32)
    lnn = (["ln1_g_0", "ln1_b_0", "ln2_g_0", "ln2_b_0", "ln3_g_0", "ln3_b_0"] if l == 0 else
           ["ln1_g_1", "ln1_b_1", "ln2_g_1", "ln2_b_1", "ln3_g_1", "ln3_b_1"])
    for k, nm in enumerate(lnn):
        pv[:, PV_LN + 8 * k:PV_LN + 8 * k + 8] = chunkcols(inp[nm])
    cw = inp[f"ffn_conv_w_{l}"]
    for j in range(3):
        pv[:, PV_FCW + j:PV_FCW + 132:3] = chunkcols(cw[j])
    pv[:, PV_FCB:PV_FCB + 44] = chunkcols(inp[f"ffn_conv_b_{l}"])
    if l == 0:
        pv[:, PV0_DN] = inp["diff_norm_0"]
        pv[:, PV0_GN] = inp["gla_norm_0"]
        pv[:, PV0_LAM:PV0_LAM + 256] = inp["diff_lambda_0"].reshape(1, 256)
        pv[0:16, PV0_W2:PV0_W2 + 256] = inp["gla_w2_0"]
        pv[16, PV0_W2:PV0_W2 + 256] = inp["gla_b2_0"]
    else:
        gw = inp["gdn_conv_w_1"]
        for j in range(4):
            pv[:, PV1_CW + j:PV1_CW + 96:4] = chunkcols(gw[j])
        pv[:, PV1_AL:PV1_AL + 8] = inp["gdn_a_log_1"][None, :]
        pv[:, PV1_DT:PV1_DT + 8] = inp["gdn_dt_bias_1"][None, :]
        pv[:, PV1_GN] = inp["gdn_norm_1"]
    return pv


WNAMES = ["w_in_0", "w_mix_out_0", "xa_wq_0", "xa_wkv_0", "xa_wo_0", "ffn_w_in_0", "ffn_w_out_0",
          "w_in_1", "w_mix_out_1", "xa_wq_1", "xa_wkv_1", "xa_wo_1", "ffn_w_in_1", "ffn_w_out_1"]
WSHAPES = {"w_in_0": (D, P0W), "w_in_1": (D, P1W)}
for _l in range(2):
    WSHAPES[f"w_mix_out_{_l}"] = (D, D)
    WSHAPES[f"xa_wq_{_l}"] = (D, D)
    WSHAPES[f"xa_wkv_{_l}"] = (D, 2 * D)
    WSHAPES[f"xa_wo_{_l}"] = (D, D)
    WSHAPES[f"ffn_w_in_{_l}"] = (D, 2 * DFF)
    WSHAPES[f"ffn_w_out_{_l}"] = (DFF, D)


def build(stop=None, dbg=False):
    nc = bass.Bass("TRN2", target_bir_lowering=False)
    dr = {}
    dr["x"] = nc.dram_tensor("x", [T, D], F32, kind="ExternalInput").ap()
    dr["mem"] = nc.dram_tensor("mem", [NMEM, D], F32, kind="ExternalInput").ap()
    dr["pos"] = nc.dram_tensor("pos", [1, T], I32, kind="ExternalInput").ap()
    dr["consts"] = nc.dram_tensor("consts", [128, NCONST], F32, kind="ExternalInput").ap()
    dr["pv0"] = nc.dram_tensor("pv0", [128, NPV], F32, kind="ExternalInput").ap()
    dr["pv1"] = nc.dram_tensor("pv1", [128, NPV], F32, kind="ExternalInput").ap()
    for n in WNAMES:
        dr[n] = nc.dram_tensor(n, list(WSHAPES[n]), F32, kind="ExternalInput").ap()
    out_d = nc.dram_tensor("out", [T, D], F32, kind="ExternalOutput").ap()
    res_d = nc.dram_tensor("resT", [8, 128, T], F32, kind=("ExternalOutput" if dbg else "Internal")).ap()
    sc_v = nc.dram_tensor("sc_v", [8, 128, T], BF16, kind="Internal").ap()
    sc_z = nc.dram_tensor("sc_z", [8, 128, T], BF16, kind="Internal").ap()
    sc_q = nc.dram_tensor("sc_q", [8, 128, T], BF16, kind="Internal").ap()
    sc_k = nc.dram_tensor("sc_k", [8, 128, T], BF16, kind="Internal").ap()
    if dbg:
        dbg_o = nc.dram_tensor("dbg_o", [8, 128, T], BF16, kind="ExternalOutput").ap()

    with ExitStack() as st:
        P = Prog(nc, st)
        hTb = P.sb("hTb", [128, 8, T], BF16)
        hb = [P.buf(f"hb{b}") for b in range(4)]
        cst = P.sb("cst", [128, NCONST], F32)
        b_cst = P.buf("cst")
        cbf = P.sb("cbf", [128, 128 * 4], BF16)
        b_cbf = P.buf("cbf")
        ID_BF = cbf[:, 0:128]
        PERM_BF = cbf[:, 128:256]
        ONES_BF = cbf[:, 256:384]
        U_BF = cbf[:, 384:512]
        ID_F = cst[:, C_ID:C_ID + 128]
        ONES_F = cst[:, C_ONES:C_ONES + 128]
        U_F = cst[:, C_U:C_U + 128]
        small = P.sb("small", [128, 16], F32)
        b_small = P.buf("small")
        C_EPSLN = small[:, 0:1]
        C_EPSRMS = small[:, 1:2]
        C_ONE = small[:, 2:3]
        C_ZERO = small[:, 3:4]
        pv = [P.sb("pv0s", [128, NPV], F32), P.sb("pv1s", [128, NPV1], F32)]
        memT = P.sb("memT", [128, 8, NMEM], BF16)
        memTb = P.buf("memT")
        b_pv = [P.buf("pv0"), P.buf("pv1")]
        RING_SLOT = 16 * 1024
        ring = P.sb("ring", [128, 2 * RING_SLOT], U8)
        ring_b = [P.buf("ring0"), P.buf("ring1")]
        ring_i = [0]
        ARENA_BYTES = 122 * 1024
        arena_t = P.sb("arena", [128, ARENA_BYTES], U8)
        A = Arena(P, arena_t, ARENA_BYTES)
        pst = [P.ps(f"ps{k}", [128, 1024], F32) for k in range(4)]
        pb = [P.buf(f"pb{i}") for i in range(8)]
        for _b in pb:
            _b.excl = True
        bank_i = [0]
        pair_i = [0]

        def bank():
            i = bank_i[0] % 8
            bank_i[0] += 1
            return pst[i // 2][:, (i % 2) * 512:(i % 2) * 512 + 512], pb[i]

        def bank_at(i):
            return pst[i // 2][:, (i % 2) * 512:(i % 2) * 512 + 512], pb[i]

        def pair():
            k = pair_i[0] % 4
            pair_i[0] += 1
            return pst[k][:, :], [pb[2 * k], pb[2 * k + 1]]

        resb = [P.buf(f"res{b}") for b in range(4)]
        outb = P.buf("outd")
        scvb = P.buf("scv")
        sczb = P.buf("scz")
        scqb = P.buf("scq")
        sckb = P.buf("sck")

        cost = {"pe": 0.0, "act": 0.0, "dve": 0.0, "pool": 0.0}
        build.cost = cost

        def fsz(ap):
            n = 1
            for d in ap.shape[1:]:
                n *= d
            return n

        def MM(out, lhsT, rhs, s, e, R, W, **kw):
            cost["pe"] += max(fsz(out), 64) / 2.4e3 * (4 if rhs.dtype == F32 else 1) + 0.01
            P.op("pe", lambda en: en.matmul(out, lhsT=lhsT, rhs=rhs, start=s, stop=e, **kw), R, W)

        def TR(out, in_, ident, R, W):
            P.op("pe", lambda en: en.transpose(out=out, in_=in_, identity=ident), R, W)

        def ACT(out, in_, func, R, W, bias=None, scale=1.0):
            cost["act"] += fsz(out) / 1.2e3 + 0.22
            if bias is None:
                P.op("act", lambda en: en.activation(out=out, in_=in_, func=func, scale=scale), R, W)
            else:
                P.op("act", lambda en: en.activation(out=out, in_=in_, func=func, bias=bias, scale=scale), R, W)

        def TT(eng, out, a, b, op, R, W):
            cost[eng] += fsz(out) / 0.96e3 + 0.1
            P.op(eng, lambda en: en.tensor_tensor(out=out, in0=a, in1=b, op=op), R, W)

        def TS(eng, out, a, s1, s2, op0, op1, R, W):
            cost[eng] += fsz(out) / 0.96e3 + 0.1
            if s2 is None:
                P.op(eng, lambda en: en.tensor_scalar(out=out, in0=a, scalar1=s1, scalar2=None, op0=op0), R, W)
            else:
                P.op(eng, lambda en: en.tensor_scalar(out=out, in0=a, scalar1=s1, scalar2=s2, op0=op0, op1=op1), R, W)

        def STT(eng, out, a, s, b, op0, op1, R, W):
            cost[eng] += fsz(out) / 0.96e3 + 0.1
            P.op(eng, lambda en: en.scalar_tensor_tensor(out=out, in0=a, scalar=s, in1=b, op0=op0, op1=op1), R, W)

        def CP(eng, out, in_, R, W):
            cost[eng] += fsz(out) / (1.2e3 if eng == "act" else 0.96e3) + (0.22 if eng == "act" else 0.1)
            if eng == "act":
                P.op("act", lambda en: en.copy(out=out, in_=in_), R, W)
            else:
                P.op(eng, lambda en: en.tensor_copy(out=out, in_=in_), R, W)

        def MEMSET(eng, ap, val, W):
            P.op(eng, lambda en: en.memset(ap, val), (), W)

        def bc(ap2, n):
            return ap2.unsqueeze(2).to_broadcast([ap2.shape[0], ap2.shape[1], n])

        def bm(ap2, h):
            return ap2.unsqueeze(1).to_broadcast([ap2.shape[0], h, ap2.shape[1]])

        def wload(name, k0, nk, c0, ncols, dcol=0, slot=None, newslot=True):
            if newslot:
                ring_i[0] += 1
            si = ring_i[0] % 2
            base = si * RING_SLOT + dcol
            nbytes = nk * ncols * 2
            assert dcol + nbytes <= RING_SLOT
            v = ring[:, base:base + nbytes].bitcast(BF16).rearrange("p (a b) -> p a b", a=nk)
            src = dr[name][k0 * 128:(k0 + nk) * 128, c0:c0 + ncols].rearrange("(c p) n -> p c n", p=128)
            pairs = []
            step = max(1, 1024 // 128 // 1)
            kk = 0
            while kk < nk:
                k2 = min(nk, kk + 8)
                pairs.append((v[:, kk:k2, :], src[:, kk:k2, :]))
                kk = k2
            P.dma_group("pool", pairs, ring_b[si], writes=[ring_b[si]])
            return v, ring_b[si]

        P.dma("sp", cst[:], dr["consts"], b_cst, writes=[b_cst])
        P.dma("sp", pv[0][:], dr["pv0"], b_pv[0], writes=[b_pv[0]])
        P.dma("sp", pv[1][:], dr["pv1"][:, 0:NPV1], b_pv[1], writes=[b_pv[1]])
        CP("dve", cbf[:, 0:128], cst[:, C_ID:C_ID + 128], [b_cst], [b_cbf])
        CP("dve", cbf[:, 128:256], cst[:, C_PERM:C_PERM + 128], [b_cst], [b_cbf])
        CP("dve", cbf[:, 256:384], cst[:, C_ONES:C_ONES + 128], [b_cst], [b_cbf])
        CP("dve", cbf[:, 384:512], cst[:, C_U:C_U + 128], [b_cst], [b_cbf])
        MEMSET("dve", small[:, 0:1], LN_EPS, [b_small])
        MEMSET("dve", small[:, 1:2], RMS_EPS, [b_small])
        MEMSET("dve", small[:, 2:3], 1.0, [b_small])
        MEMSET("dve", small[:, 3:4], 0.0, [b_small])
        MEMSET("dve", small[:, 4:5], math.pi / 2, [b_small])
        C_HPI = small[:, 4:5]

        def res_ap(b):
            return res_d.rearrange("c p t -> p c t")[:, :, b * 512:(b + 1) * 512]

        def stage_in():
            m0 = A.mark()
            xin = [A.alloc([128, D], F32, f"xin{i}") for i in range(4)]
            xTf = [A.alloc([128, 8, 512], F32, f"xTf{i}") for i in range(2)]
            for b in range(int(os.environ.get("NBLK", "4"))):
                xt, xtb = xTf[b % 2]
                for ti in range(int(os.environ.get("NTI", "4"))):
                    i = 4 * b + ti
                    xi, xib = xin[i % 4]
                    P.dma("sp", xi, dr["x"][i * 128:(i + 1) * 128, :], xib, writes=[xib])
                    for g in range(2):
                        bk, bkb = bank()
                        for k in range(4):
                            TR(bk[:, k * 128:(k + 1) * 128], xi[:, (4 * g + k) * 128:(4 * g + k + 1) * 128], ID_F,
                               [xib, b_cst], [bkb])
                        bk3 = bk.rearrange("p (a b) -> p a b", a=4)
                        if True:
                            CP("act", hTb[:, 4 * g:4 * g + 4, i * 128:(i + 1) * 128], bk3, [bkb], [hb[b]])
                        CP("dve", xt[:, 4 * g:4 * g + 4, ti * 128:(ti + 1) * 128], bk3, [bkb], [xtb])
                P.dma("act", res_ap(b), xt, xtb, reads=[xtb], writes=[resb[b]])
            for i in range(2):
                xi, xib = xin[i]
                P.dma("sp", xi, dr["mem"][i * 128:(i + 1) * 128, :], xib, writes=[xib])
                for g in range(2):
                    bk, bkb = bank()
                    for k in range(4):
                        TR(bk[:, k * 128:(k + 1) * 128], xi[:, (4 * g + k) * 128:(4 * g + k + 1) * 128], ID_F, [xib, b_cst], [bkb])
                    CP("act", memT[:, 4 * g:4 * g + 4, i * 128:(i + 1) * 128], bk.rearrange("p (a b) -> p a b", a=4), [bkb], [memTb])
            A.release(m0)

        def ln_alloc(n):
            out = []
            for i in range(n):
                out.append(dict(zb=A.alloc([128, 8, 512], BF16, f"ln_zb{i}"), zq=A.alloc([128, 8, 512], BF16, f"ln_zq{i}"),
                                mm=A.alloc([128, 512], F32, f"ln_m{i}"), vv=A.alloc([128, 512], F32, f"ln_v{i}"),
                                rs=A.alloc([128, 512], F32, f"ln_r{i}"), nm=A.alloc([128, 512], F32, f"ln_nm{i}")))
            return out

        def ln_head(tmp, z, zb_):
            (zb, zbb), (zq, zqb) = tmp["zb"], tmp["zq"]
            CP("act", zb, z, [zb_], [zbb])
            ACT(zq, z, AF.Square, [zb_], [zqb])

        def ln_tail(tmp, b, z, zb_, lcol, l, final=False, finalbuf=None):
            (zb, zbb), (zq, zqb), (mm_, mmb), (vv, vvb), (rs, rsb), (nm, nmb) = (tmp[k] for k in ("zb", "zq", "mm", "vv", "rs", "nm"))
            s1, s1b = bank()
            s2, s2b = bank()
            for j in range(8):
                MM(s1, ONES_BF, zb[:, j, :], j == 0, j == 7, [b_cbf, zbb], [s1b])
            for j in range(8):
                MM(s2, ONES_BF, zq[:, j, :], j == 0, j == 7, [b_cbf, zqb], [s2b])
            TS("dve", mm_, s1, 1.0 / D, None, ALU.mult, None, [s1b], [mmb])
            TT("dve", vv, mm_, mm_, ALU.mult, [mmb], [vvb])
            STT("dve", vv, s2, 1.0 / D, vv, ALU.mult, ALU.subtract, [s2b, vvb], [vvb])
            ACT(rs, vv, AF.Ln, [vvb, b_small], [rsb], bias=C_EPSLN)
            ACT(rs, rs, AF.Exp, [rsb], [rsb], scale=-0.5)
            TT("dve", nm, mm_, rs, ALU.mult, [mmb, rsb], [nmb])
            TT("dve", z, z, bm(rs, 8), ALU.mult, [zb_, rsb], [zb_])
            TT("dve", z, z, bm(nm, 8), ALU.subtract, [zb_, nmb], [zb_])
            gc_ = pv[l][:, PV_LN + 16 * lcol:PV_LN + 16 * lcol + 8]
            bc_ = pv[l][:, PV_LN + 16 * lcol + 8:PV_LN + 16 * lcol + 16]
            for j in range(8):
                ACT(z[:, j, :], z[:, j, :], AF.Identity, [zb_, b_pv[l]], [zb_], bias=bc_[:, j:j + 1], scale=gc_[:, j:j + 1])
            if not final:
                CP("dve", hTb[:, :, b * 512:(b + 1) * 512], z, [zb_], [hb[b]])
                P.dma("sp", res_ap(b), z, zb_, reads=[zb_], writes=[resb[b]])
            else:
                for ti in range(4):
                    ot, otb = finalbuf[ti % 2]
                    for g in range(2):
                        bk, bkb = bank()
                        for k in range(4):
                            TR(bk[:, k * 128:(k + 1) * 128], z[:, 4 * g + k, ti * 128:(ti + 1) * 128], ID_F, [zb_, b_cst], [bkb])
                        if g == 0:
                            CP("act", ot[:, 0:512], bk, [bkb], [otb])
                        else:
                            CP("dve", ot[:, 512:1024], bk, [bkb], [otb])
                    r0 = b * 512 + ti * 128
                    P.dma("sp", out_d[r0:r0 + 128, :], ot, otb, reads=[otb], writes=[outb])

        def ln_block(b, z, zb_, lcol, l, final=False, finalbuf=None):
            m0 = A.mark()
            tmp = ln_alloc(1)[0]
            ln_head(tmp, z, zb_)
            ln_tail(tmp, b, z, zb_, lcol, l, final=final, finalbuf=finalbuf)
            A.release(m0)

        def proj_z(b, nk, lhs_fn, rhs_fn, z, zb_):
            P.dma("sp", z, res_ap(b), zb_, reads=[resb[b]], writes=[zb_])
            for j in range(8):
                bk, bkb = bank()
                for c in range(nk):
                    la, lr = lhs_fn(c, j)
                    ra, rr = rhs_fn(c, b)
                    MM(bk, la, ra, c == 0, c == nk - 1, lr + rr, [bkb])
                STT("dve", z[:, j, :], z[:, j, :], ALPHA, bk, ALU.mult, ALU.add, [zb_, bkb], [zb_])

        def proj_res_ln(l, lcol, nk, lhs_fn, rhs_fn, blocks, zbufs, final=False, finalbuf=None):
            m0 = A.mark()
            tmps = ln_alloc(2)
            pend = None
            for b in blocks:
                z, zb_ = zbufs[b % len(zbufs)]
                proj_z(b, nk, lhs_fn, rhs_fn, z, zb_)
                ln_head(tmps[b % 2], z, zb_)
                if pend is not None:
                    ln_tail(*pend)
                pend = (tmps[b % 2], b, z, zb_, lcol, l, final, finalbuf)
            ln_tail(*pend)
            A.release(m0)

        def stage_xattn(l):
            m0 = A.mark()
            kT, kTb = A.alloc([128, 8, NMEM], BF16, "xkT")
            vt, vtb = A.alloc([128, 2, D], BF16, "xv")
            w, wb = wload(f"xa_wkv_{l}", 0, 8, 0, D)
            for jj in range(8):
                bk, bkb = bank()
                for c in range(8):
                    MM(bk[:, 0:NMEM], w[:, c, jj * 128:(jj + 1) * 128], memT[:, c, :], c == 0, c == 7, [wb, memTb], [bkb])
                CP("act", kT[:, jj, :], bk[:, 0:NMEM], [bkb], [kTb])
            w, wb = wload(f"xa_wkv_{l}", 0, 8, D, D)
            for gi in range(2):
                for i in range(2):
                    bk, bkb = bank()
                    for c in range(8):
                        MM(bk, memT[:, c, i * 128:(i + 1) * 128], w[:, c, gi * 512:(gi + 1) * 512], c == 0, c == 7, [wb, memTb], [bkb])
                    CP("act", vt[:, i, gi * 512:(gi + 1) * 512], bk, [bkb], [vtb])
            wq, wqb = wload(f"xa_wq_{l}", 0, 8, 0, D)
            wo, wob = wload(f"xa_wo_{l}", 0, 8, 0, D)
            qT = [A.alloc([128, 8, 512], BF16, f"xq{i}") for i in range(2)]
            oT = [A.alloc([128, 8, 512], BF16, f"xo{i}") for i in range(1)]
            E = [A.alloc([128, 2, 512], BF16, f"xE{i}") for i in range(2)]
            rsm = [A.alloc([128, 512], F32, f"xr{i}") for i in range(2)]
            zbufs = [A.alloc([128, 8, 512], F32, f"xz{i}") for i in range(2)]
            tmps = ln_alloc(2)
            pend = None
            def qproj(b, js):
                q, qb_ = qT[b % 2]
                for j in js:
                    bk, bkb = bank()
                    for c in range(8):
                        MM(bk, wq[:, c, j * 128:(j + 1) * 128], hTb[:, c, b * 512:(b + 1) * 512], c == 0, c == 7, [wqb, hb[b]], [bkb])
                    CP("act", q[:, j, :], bk, [bkb], [qb_])
            qproj(0, range(8))
            for b in range(4):
                q, qb_ = qT[b % 2]
                o, ob_ = oT[0]

                def xs(h, q=q, qb_=qb_):
                    e, eb_ = E[h % 2]
                    for kt in range(2):
                        bk, bkb = bank()
                        for dc in range(2):
                            MM(bk, kT[:, 2 * h + dc, kt * 128:(kt + 1) * 128], q[:, 2 * h + dc, :], dc == 0, dc == 1, [kTb, qb_], [bkb])
                        ACT(e[:, kt, :], bk, AF.Exp, [bkb], [eb_], scale=1.0 / 16.0)

                def xa(h, o=o, ob_=ob_):
                    e, eb_ = E[h % 2]
                    r, rb_ = rsm[h % 2]
                    sm, smb = bank()
                    for kt in range(2):
                        MM(sm, ONES_BF, e[:, kt, :], kt == 0, kt == 1, [b_cbf, eb_], [smb])
                    ACT(r, sm, AF.Ln, [smb], [rb_])
                    ACT(r, r, AF.Exp, [rb_], [rb_], scale=-1.0)
                    for dvc in range(2):
                        bk, bkb = bank()
                        for kt in range(2):
                            MM(bk, vt[:, kt, h * 256 + dvc * 128:h * 256 + dvc * 128 + 128], e[:, kt, :], kt == 0, kt == 1, [vtb, eb_], [bkb])
                        TT("dve", o[:, 2 * h + dvc, :], bk, r, ALU.mult, [bkb, rb_], [ob_])
                xs(0)
                for h in range(4):
                    if b + 1 < 4:
                        qproj(b + 1, [2 * h, 2 * h + 1])
                    if h + 1 < 4:
                        xs(h + 1)
                    xa(h)
                z, zb_ = zbufs[b % 2]
                proj_z(b, 8,
                       lambda c, j: (wo[:, c, j * 128:(j + 1) * 128], [wob]),
                       lambda c, bb, o=o, ob_=ob_: (o[:, c, :], [ob_]),
                       z, zb_)
                ln_head(tmps[b % 2], z, zb_)
                if pend is not None:
                    ln_tail(*pend)
                pend = (tmps[b % 2], b, z, zb_, 1, l)
            ln_tail(*pend)
            A.release(m0)

        def stage_ffn2(l, final=False):
            m0 = A.mark()
            halo, halob = A.alloc([128, 44, 2], F32, "halo")
            MEMSET("dve", halo, 0.0, [halob])
            acth, acthb = A.alloc([128, 22, 1024], BF16, "acth")
            cw = pv[l][:, PV_FCW:PV_FCW + 132]
            cb = pv[l][:, PV_FCB:PV_FCB + 44]
            groups = [(g * 4, 4) for g in range(5)] + [(20, 2)]
            pi = 0
            pendf = [None]
            for hf in range(2):
                mh = A.mark()
                NPB = 3
                pre = [[A.alloc([128, 1026], F32, f"pre{i}{k}") for k in range(2)] for i in range(NPB)]
                cv = [[A.alloc([128, 1024], F32, f"cv{i}{k}") for k in range(2)] for i in range(NPB)]
                tpl = [A.alloc([128, 1024], F32, f"tp{i}") for i in range(2)] if os.environ.get("POOLCONV", "0") == "1" else None
                for (c0, ncn) in groups:
                    w, wb = wload(f"ffn_w_in_{l}", 0, 8, c0 * 128, ncn * 128)
                    w2_, _ = wload(f"ffn_w_in_{l}", 0, 8, DFF + c0 * 128, ncn * 128, dcol=8 * ncn * 128 * 2, newslot=False)
                    for cc in range(ncn):
                        ch = c0 + cc
                        pr = pre[pi % NPB]
                        cvv = cv[pi % NPB]
                        pi += 1
                        for k, (wk, chk) in enumerate(((w, ch), (w2_, ch + 22))):
                            p_, pb_ = pr[k]
                            y_, yb_ = cvv[k]
                            CP("act", p_[:, 0:2], halo[:, chk, :], [halob], [pb_])
                            for bb in range(2):
                                bk, bkb = bank()
                                blk = 2 * hf + bb
                                for c in range(8):
                                    MM(bk, wk[:, c, cc * 128:(cc + 1) * 128], hTb[:, c, blk * 512:(blk + 1) * 512], c == 0, c == 7, [wb, hb[blk]], [bkb])
                                CP("act", p_[:, 2 + bb * 512:2 + (bb + 1) * 512], bk, [bkb], [pb_])
                            CP("act", halo[:, chk, :], p_[:, 1024:1026], [pb_], [halob])
                            ce = "pool" if (k == 1 and os.environ.get("POOLCONV", "0") == "1") else "dve"
                            ACT(y_, p_[:, 2:1026], AF.Identity, [pb_, b_pv[l]], [yb_], bias=cb[:, chk:chk + 1], scale=cw[:, chk * 3 + 2:chk * 3 + 3])
                            if ce == "dve":
                                STT("dve", y_, p_[:, 1:1025], cw[:, chk * 3 + 1:chk * 3 + 2], y_, ALU.mult, ALU.add, [pb_, b_pv[l], yb_], [yb_])
                                STT("dve", y_, p_[:, 0:1024], cw[:, chk * 3:chk * 3 + 1], y_, ALU.mult, ALU.add, [pb_, b_pv[l], yb_], [yb_])
                            else:
                                tp_, tpb_ = tpl[pi % 2]
                                for jj in (1, 0):
                                    TS("pool", tp_, p_[:, jj:1024 + jj], cw[:, chk * 3 + jj:chk * 3 + jj + 1], None, ALU.mult, None, [pb_, b_pv[l]], [tpb_])
                                    TT("pool", y_, y_, tp_, ALU.add, [yb_, tpb_], [yb_])
                        if pendf[0] is not None:
                            pendf[0]()

                        def fin(cvv=cvv, ch=ch):
                            g_, gb_ = cvv[0]
                            u_, ub_ = cvv[1]
                            ACT(g_, g_, AF.Silu, [gb_], [gb_])
                            TT("dve", acth[:, ch, :], g_, u_, ALU.mult, [gb_, ub_], [acthb])
                        pendf[0] = fin
                pendf[0]()
                pendf[0] = None
                A.release(mh)
                zh = [A.alloc([128, 8, 512], F32, f"fz{i}") for i in range(2)]
                finalbuf = None
                if final:
                    finalbuf = [A.alloc([128, D], F32, f"fo{i}") for i in range(2)]
                for bb in range(2):
                    blk = 2 * hf + bb
                    z, zb_ = zh[bb]
                    P.dma("sp", z, res_ap(blk), zb_, reads=[resb[blk]], writes=[zb_])
                for jg in range(4):
                    w, wb = wload(f"ffn_w_out_{l}", 0, 22, jg * 256, 256)
                    for j2 in range(2):
                        j = jg * 2 + j2
                        for bb in range(2):
                            z, zb_ = zh[bb]
                            bk, bkb = bank()
                            for c in range(22):
                                MM(bk, w[:, c, j2 * 128:(j2 + 1) * 128], acth[:, c, bb * 512:(bb + 1) * 512], c == 0, c == 21, [wb, acthb], [bkb])
                            STT("dve", z[:, j, :], z[:, j, :], ALPHA, bk, ALU.mult, ALU.add, [zb_, bkb], [zb_])
                for bb in range(2):
                    blk = 2 * hf + bb
                    z, zb_ = zh[bb]
                    ln_block(blk, z, zb_, 2, l, final=final, finalbuf=finalbuf)
                A.release(mh)
            A.release(m0)

        def stage_mixer0():
            l = 0
            m0 = A.mark()
            oT, oTb = A.alloc([128, 8, T], BF16, "oT")
            m1 = A.mark()
            aqT, aqb = A.alloc([128, 4, T], BF16, "aqT")
            akT, akb = A.alloc([128, 4, T], BF16, "akT")
            avt, avb = A.alloc([128, 16, 512], BF16, "avt")
            lamt, lamb = A.alloc([128, 8], F32, "lam")
            m2 = A.mark()
            cosT, cosb = A.alloc([128, T], F32, "cosT")
            sinT, sinb = A.alloc([128, T], F32, "sinT")
            m3 = A.mark()
            posi, posib = A.alloc([128, T], I32, "posi")
            ang, angb = A.alloc([128, T], F32, "ang")
            tmp, tmpb = A.alloc([128, T], F32, "angt")
            P.dma("sp", posi, dr["pos"].partition_broadcast(128), posib, writes=[posib])
            CP("dve", ang, posi, [posib], [angb])
            TS("dve", ang, ang, cst[:, C_INVF:C_INVF + 1], None, ALU.mult, None, [angb, b_cst], [angb])
            MAGIC = 12582912.0
            for (dst, dstb, shift) in ((sinT, sinb, 0.0), (cosT, cosb, math.pi / 2)):
                TS("dve", tmp, ang, shift, 1.0 / (2 * math.pi), ALU.add, ALU.mult, [angb], [tmpb])
                TS("dve", tmp, tmp, MAGIC, None, ALU.add, None, [tmpb], [tmpb])
                TS("dve", tmp, tmp, -MAGIC, None, ALU.add, None, [tmpb], [tmpb])
                STT("dve", tmp, tmp, -2 * math.pi, ang, ALU.mult, ALU.add, [tmpb, angb], [tmpb])
                TS("dve", tmp, tmp, shift, None, ALU.add, None, [tmpb], [tmpb])
                TS("dve", tmp, tmp, math.pi, -math.pi, ALU.min, ALU.max, [tmpb], [tmpb])
                ACT(dst, tmp, AF.Sin, [tmpb], [dstb])
            A.release(m3)
            if os.environ.get("MS") == "rope":
                return
            lam_init = 0.8 - 0.6 * math.exp(-0.3 * 0)
            lp, lpb = A.alloc([128, 128], F32, "lamp")
            lv = pv[0][:, PV0_LAM:PV0_LAM + 256]
            TT("dve", lp[:, 0:64], lv[:, 0:64], lv[:, 64:128], ALU.mult, [b_pv[0]], [lpb])
            TT("dve", lp[:, 64:128], lv[:, 128:192], lv[:, 192:256], ALU.mult, [b_pv[0]], [lpb])
            P.op("dve", lambda en: en.tensor_reduce(out=lamt[:, 0:2], in_=lp.rearrange("p (a b) -> p a b", a=2), axis=AX.X, op=ALU.add), [lpb], [lamb])
            ACT(lamt[:, 2:4], lamt[:, 0:2], AF.Exp, [lamb], [lamb])
            TT("dve", lamt[:, 4:5], lamt[:, 2:3], lamt[:, 3:4], ALU.subtract, [lamb], [lamb])
            TS("dve", lamt[:, 4:5], lamt[:, 4:5], lam_init, None, ALU.add, None, [lamb], [lamb])
            TS("dve", lamt[:, 5:6], lamt[:, 4:5], -1.0, None, ALU.mult, None, [lamb], [lamb])
            TS("dve", lamt[:, 6:7], pv[0][:, PV0_DN:PV0_DN + 1], 1.0 - lam_init, None, ALU.mult, None, [b_pv[0]], [lamb])
            LAMNEG = lamt[:, 5:6]
            DGAIN = lamt[:, 6:7]
            if os.environ.get("MS") == "lam":
                return
            xbf = [A.alloc([128, 512], BF16, f"xbf{i}") for i in range(2)]
            t1 = [A.alloc([128, 512], F32, f"rt1{i}") for i in range(2)]
            t2 = [A.alloc([128, 512], F32, f"rt2{i}") for i in range(2)]
            ii = 0
            for (c0, dst, dstb, scale) in ((0, aqT, aqb, 0.125), (512, akT, akb, 1.0)):
                w, wb = wload("w_in_0", 0, 8, c0, 512)
                for h in range(4):
                    for b in range(4):
                        bk, bkb = bank()
                        for c in range(8):
                            MM(bk, w[:, c, h * 128:(h + 1) * 128], hTb[:, c, b * 512:(b + 1) * 512], c == 0, c == 7, [wb, hb[b]], [bkb])
                        xb_, xbb = xbf[ii % 2]
                        a1, a1b = t1[ii % 2]
                        a2, a2b = t2[ii % 2]
                        ii += 1
                        CP("act", xb_, bk, [bkb], [xbb])
                        bk2, bk2b = bank()
                        MM(bk2, PERM_BF, xb_, True, True, [b_cbf, xbb], [bk2b])
                        STT("dve", a1, bk, scale, cosT[:, b * 512:(b + 1) * 512], ALU.mult, ALU.mult, [bkb, cosb], [a1b])
                        STT("dve", a2, bk2, scale, sinT[:, b * 512:(b + 1) * 512], ALU.mult, ALU.mult, [bk2b, sinb], [a2b])
                        TT("dve", dst[:, h, b * 512:(b + 1) * 512], a1, a2, ALU.add, [a1b, a2b], [dstb])
            A.release(m2)
            if os.environ.get("MS") == "p0a":
                return
            w, wb = wload("w_in_0", 0, 8, 1024, 512)
            for i in range(16):
                bk, bkb = bank()
                for c in range(8):
                    MM(bk, hTb[:, c, i * 128:(i + 1) * 128], w[:, c, :], c == 0, c == 7, [wb, hb[i // 4]], [bkb])
                CP("act", avt[:, i, :], bk, [bkb], [avb])
            if os.environ.get("MS") == "av":
                return
            NE = 3
            Eb = [[A.alloc([128, 512], BF16, f"E{i}{m}") for m in range(2)] for i in range(NE)]
            epi = []
            for i in range(2):
                epi.append(dict(of=A.alloc([128, 512], F32, f"dof{i}"), o2=A.alloc([128, 512], F32, f"do2{i}"),
                                r0=A.alloc([128, 512], F32, f"dr0{i}"), r1=A.alloc([128, 512], F32, f"dr1{i}"),
                                sq=A.alloc([128, 512], BF16, f"dsq{i}"), rs=A.alloc([128, 512], F32, f"drs{i}")))
            steps = [(h, qb, kt) for h in range(4) for qb in range(4) for kt in range(4 * (qb + 1))]
            sums = [bank_at(0), bank_at(1)]
            pvs = [bank_at(2), bank_at(3)]

            def geom(st):
                h, qb, kt = steps[st]
                jd = kt - 4 * qb
                cs = 128 * jd if jd > 0 else 0
                return h, qb, kt, jd, cs, 512 - cs

            def emit_S(st):
                h, qb, kt, jd, cs, n = geom(st)
                q0 = qb * 512
                for m in range(2):
                    bk, bkb = bank_at(4 + 2 * (st % 2) + m)
                    pr = slice(m * 64, (m + 1) * 64)
                    MM(bk[:, 0:n], akT[pr, h, kt * 128:(kt + 1) * 128], aqT[pr, h, q0 + cs:q0 + 512], True, True, [akb, aqb], [bkb])
                    e_, eb_ = Eb[st % NE][m]
                    ACT(e_[:, 0:n], bk[:, 0:n], AF.Exp, [bkb], [eb_])
                    if jd >= 0:
                        TT("dve", e_[:, 0:128], e_[:, 0:128], U_BF, ALU.mult, [eb_, b_cbf], [eb_])

            def emit_A(st):
                h, qb, kt, jd, cs, n = geom(st)
                nkt = 4 * (qb + 1)
                for m in range(2):
                    e_, eb_ = Eb[st % NE][m]
                    sm, smb = sums[m]
                    pv_, pvb = pvs[m]
                    MM(sm[:, cs:512], ONES_BF, e_[:, 0:n], kt == 0, kt == nkt - 1, [b_cbf, eb_], [smb])
                    MM(pv_[:, cs:512], avt[:, kt, h * 128:(h + 1) * 128], e_[:, 0:n], kt == 0, kt == nkt - 1, [avb, eb_], [pvb])

            def emit_epi(st, gi):
                h, qb, kt, jd, cs, n = geom(st)
                q0 = qb * 512
                E_ = epi[gi % 2]
                (of_, ofb), (o2_, o2b), (r0, r0b), (r1, r1b), (sq_, sqb), (rs_, rsb_) = E_["of"], E_["o2"], E_["r0"], E_["r1"], E_["sq"], E_["rs"]
                ACT(r0, sums[0][0], AF.Ln, [sums[0][1]], [r0b])
                ACT(r1, sums[1][0], AF.Ln, [sums[1][1]], [r1b])
                ACT(r0, r0, AF.Exp, [r0b], [r0b], scale=-1.0)
                ACT(r1, r1, AF.Exp, [r1b], [r1b], scale=-1.0)
                TT("dve", of_, pvs[0][0], r0, ALU.mult, [pvs[0][1], r0b], [ofb])
                TT("dve", o2_, pvs[1][0], r1, ALU.mult, [pvs[1][1], r1b], [o2b])
                STT("dve", of_, o2_, LAMNEG, of_, ALU.mult, ALU.add, [o2b, lamb, ofb], [ofb])
                ACT(sq_, of_, AF.Square, [ofb], [sqb])
                bk, bkb = bank_at(4 + 2 * (st % 2))
                MM(bk, ONES_BF, sq_, True, True, [b_cbf, sqb], [bkb])
                ACT(rs_, bk, AF.Ln, [bkb, b_small], [rsb_], bias=C_EPSRMS, scale=1.0 / 128.0)
                ACT(rs_, rs_, AF.Exp, [rsb_], [rsb_], scale=-0.5)
                STT("dve", oT[:, h, q0:q0 + 512], of_, DGAIN, rs_, ALU.mult, ALU.mult, [ofb, lamb, rsb_], [oTb])

            emit_S(0)
            gi = 0
            for st in range(len(steps)):
                if st + 1 < len(steps):
                    emit_S(st + 1)
                emit_A(st)
                h, qb, kt = steps[st]
                if kt == 4 * (qb + 1) - 1:
                    emit_epi(st, gi)
                    gi += 1
            A.release(m1)
            if os.environ.get("MS") == "attn":
                return
            dec, decb = A.alloc([128, 2, 16], F32, "dec")
            qbT, qbTb = A.alloc([128, 2, T], BF16, "qbT")
            kdT, kdTb = A.alloc([128, 2, T], BF16, "kdT")
            kgt, kgtb = A.alloc([128, 16, 256], BF16, "kgt")
            bvt, bvtb = A.alloc([128, 16, 512], BF16, "bvt")
            m4 = A.mark()
            gaug, gaugb = A.alloc([128, T], F32, "gaug")
            ebT, ebTb = A.alloc([128, 2, T], BF16, "ebT")
            enT, enTb = A.alloc([128, 2, T], BF16, "enT")
            erb, erbb = A.alloc([128, 16, 256], BF16, "erb")
            MEMSET("dve", gaug[0:32, :], 1.0, [gaugb])
            w, wb = wload("w_in_0", 0, 8, 3072, 16)
            for b in range(4):
                bk, bkb = bank()
                for c in range(8):
                    MM(bk[0:16, :], w[:, c, :], hTb[:, c, b * 512:(b + 1) * 512], c == 0, c == 7, [wb, hb[b]], [bkb])
                CP("act", gaug[0:16, b * 512:(b + 1) * 512], bk[0:16, :], [bkb], [gaugb])
            w2aug = pv[0][0:17, PV0_W2:PV0_W2 + 256]
            spb = [A.alloc([128, 256], F32, f"sp{i}") for i in range(2)]
            TRIS = cst[:, C_TRIS:C_TRIS + 128]
            TGTS = cst[:, C_TGTS:C_TGTS + 128]
            for i in range(16):
                sp_, spb_ = spb[i % 2]
                bk, bkb = bank()
                MM(bk[:, 0:256], gaug[0:17, i * 128:(i + 1) * 128], w2aug, True, True, [gaugb, b_pv[0]], [bkb])
                ACT(sp_, bk[:, 0:256], AF.Exp, [bkb], [spb_], scale=-1.0)
                ACT(sp_, sp_, AF.Ln, [spb_, b_small], [spb_], bias=C_ONE)
                bk2, bk2b = bank()
                for c in range(2):
                    MM(bk2[:, c * 128:(c + 1) * 128], sp_[:, c * 128:(c + 1) * 128], TRIS, True, True, [spb_, b_cst], [bk2b])
                b3 = bk2[:, 0:256].rearrange("p (a b) -> p a b", a=2)
                ACT(ebT[:, :, i * 128:(i + 1) * 128], b3, AF.Exp, [bk2b], [ebTb])
                ACT(enT[:, :, i * 128:(i + 1) * 128], b3, AF.Exp, [bk2b], [enTb], scale=-1.0)
                ACT(dec[:, :, i:i + 1], b3[:, :, 127:128], AF.Exp, [bk2b], [decb])
                bk3, bk3b = bank()
                MM(bk3[:, 0:256], TGTS, sp_, True, True, [b_cst, spb_], [bk3b])
                ACT(erb[:, i, :], bk3[:, 0:256], AF.Exp, [bk3b], [erbb])
            if os.environ.get("MS") == "gk":
                return
            w, wb = wload("w_in_0", 0, 8, 1536, 512)
            for jc in range(4):
                for b in range(4):
                    bk, bkb = bank()
                    for c in range(8):
                        MM(bk, w[:, c, jc * 128:(jc + 1) * 128], hTb[:, c, b * 512:(b + 1) * 512], c == 0, c == 7, [wb, hb[b]], [bkb])
                    if jc < 2:
                        STT("dve", qbT[:, jc, b * 512:(b + 1) * 512], bk, 0.125, ebT[:, jc, b * 512:(b + 1) * 512], ALU.mult, ALU.mult, [bkb, ebTb], [qbTb])
                    else:
                        TT("dve", kdT[:, jc - 2, b * 512:(b + 1) * 512], bk, enT[:, jc - 2, b * 512:(b + 1) * 512], ALU.mult, [bkb, enTb], [kdTb])
            for i in range(16):
                bk, bkb = bank()
                for c in range(8):
                    MM(bk[:, 0:256], hTb[:, c, i * 128:(i + 1) * 128], w[:, c, 256:512], c == 0, c == 7, [wb, hb[i // 4]], [bkb])
                TT("dve", kgt[:, i, :], bk[:, 0:256], erb[:, i, :], ALU.mult, [bkb, erbb], [kgtb])
            A.release(m4)
            w, wb = wload("w_in_0", 0, 8, 2048, 512)
            for i in range(16):
                bk, bkb = bank()
                for c in range(8):
                    MM(bk, hTb[:, c, i * 128:(i + 1) * 128], w[:, c, :], c == 0, c == 7, [wb, hb[i // 4]], [bkb])
                CP("act", bvt[:, i, :], bk, [bkb], [bvtb])
            srT, srTb = A.alloc([128, 4, T], BF16, "srT")
            w, wb = wload("w_in_0", 0, 8, 2560, 512)
            for jc in range(4):
                for b in range(4):
                    bk, bkb = bank()
                    for c in range(8):
                        MM(bk, w[:, c, jc * 128:(jc + 1) * 128], hTb[:, c, b * 512:(b + 1) * 512], c == 0, c == 7, [wb, hb[b]], [bkb])
                    ACT(srT[:, jc, b * 512:(b + 1) * 512], bk, AF.Silu, [bkb], [srTb])
            wmo, wmob = wload("w_mix_out_0", 0, 8, 0, D)
            if os.environ.get("MS") == "glaprep":
                return
            S, Sb = A.alloc([128, 2, 128], F32, "glaS")
            Sbf, Sbfb = A.alloc([128, 2, 128], BF16, "glaSb")
            MEMSET("dve", S, 0.0, [Sb])
            CP("act", Sbf, S, [Sb], [Sbfb])
            attm = [A.alloc([128, 4, 128], BF16, f"attm{i}") for i in range(2)]
            gsq, gsqb = A.alloc([128, 512], BF16, "gsq")
            grs, grsb = A.alloc([128, 512], F32, "grs")
            gt_, gtb = A.alloc([128, 512], F32, "gt")
            GG = pv[0][:, PV0_GN:PV0_GN + 1]
            GLS = int(os.environ.get("GLS", "9"))
            for i in range(int(os.environ.get("GLN", "16"))):
                tsl = slice(i * 128, (i + 1) * 128)
                am, amb = attm[i % 2]
                bkA2 = [bank(), bank()]
                for h in range(4):
                    c, hp = h // 2, h % 2
                    pr = slice(hp * 64, (hp + 1) * 64)
                    bkA, bkAb = bkA2[hp]
                    MM(bkA[:, c * 128:(c + 1) * 128], kdT[pr, c, tsl], qbT[pr, c, tsl], True, True, [kdTb, qbTb], [bkAb])
                for hp in range(2):
                    bkA, bkAb = bkA2[hp]
                    TT("dve", am[:, hp::2, :], bkA[:, 0:256].rearrange("p (a b) -> p a b", a=2), bm(U_F, 2), ALU.mult, [bkAb, b_cst], [amb])
                if GLS < 2:
                    continue
                bkB, bkBb = bank()
                for h in range(4):
                    c, hp = h // 2, h % 2
                    pr = slice(hp * 64, (hp + 1) * 64)
                    MM(bkB[:, h * 128:(h + 1) * 128], Sbf[pr, c, :], qbT[pr, c, tsl], True, False, [Sbfb, qbTb], [bkBb])
                    MM(bkB[:, h * 128:(h + 1) * 128], bvt[:, i, h * 128:(h + 1) * 128], am[:, h, :], False, True, [bvtb, amb], [bkBb])
                if GLS < 3:
                    continue
                ACT(gsq, bkB, AF.Square, [bkBb], [gsqb])
                bkC, bkCb = bank()
                MM(bkC, ONES_BF, gsq, True, True, [b_cbf, gsqb], [bkCb])
                ACT(grs, bkC, AF.Ln, [bkCb, b_small], [grsb], bias=C_EPSRMS, scale=1.0 / 128.0)
                ACT(grs, grs, AF.Exp, [grsb], [grsb], scale=-0.5)
                TT("dve", gt_, bkB, grs, ALU.mult, [bkBb, grsb], [gtb])
                STT("dve", oT[:, 4:8, tsl], gt_.rearrange("p (a b) -> p a b", a=4), GG, srT[:, :, tsl], ALU.mult, ALU.mult,
                    [gtb, b_pv[0], srTb], [oTb])
                if i < 15 and GLS >= 4:
                    bkD, bkDb = bank()
                    for c in range(2):
                        MM(bkD[:, c * 256:(c + 1) * 256], kgt[:, i, c * 128:(c + 1) * 128], bvt[:, i, c * 256:(c + 1) * 256], True, True, [kgtb, bvtb], [bkDb])
                    for c in range(2):
                        for hp in range(2):
                            pr = slice(hp * 64, (hp + 1) * 64)
                            STT("dve", S[pr, c, :], S[pr, c, :], dec[pr, c, i:i + 1], bkD[pr, c * 256 + hp * 128:c * 256 + hp * 128 + 128],
                                ALU.mult, ALU.add, [Sb, decb, bkDb], [Sb])
                    CP("act", Sbf, S, [Sb], [Sbfb])
            if dbg and not os.environ.get("NODUMP"):
                for c_ in range(8):
                    P.dma("sp", dbg_o[c_], oT[:, c_, :], oTb, reads=[oTb], writes=[outb])
            A.release(m1)
            if os.environ.get("MS") == "glaloop":
                return
            zbufs = [A.alloc([128, 8, 512], F32, f"mz{i}") for i in range(2)]
            proj_res_ln(0, 0, 8,
                        lambda c, j: (wmo[:, c, j * 128:(j + 1) * 128], [wmob]),
                        lambda c, b: (oT[:, c, b * 512:(b + 1) * 512], [oTb]),
                        range(4), zbufs)
            A.release(m0)

        def stage_mixer1():
            l = 1
            m0 = A.mark()
            ba, bab = A.alloc([128, 16, 16], F32, "gba")
            m1 = A.mark()
            pre = [A.alloc([128, T + 3], F32, f"gpre{i}") for i in range(2)]
            cvb = [A.alloc([128, T], F32, f"gcv{i}") for i in range(2)]
            sqb_ = [A.alloc([128, T], BF16, f"gsq{i}") for i in range(2)]
            rsb2 = [A.alloc([128, 512], F32, f"grs{i}") for i in range(2)]
            stg = [A.alloc([128, T], BF16, f"gst{i}") for i in range(2)]
            cw = pv[1][:, PV1_CW:PV1_CW + 96]
            pi = 0
            pend1 = [None]
            for g in range(6):
                w, wb = wload("w_in_1", 0, 8, g * 512, 512)
                for cc in range(4):
                    ch = g * 4 + cc
                    p_, pb_ = pre[pi % 2]
                    y_, yb_ = cvb[pi % 2]
                    s_, sb_ = sqb_[pi % 2]
                    sg, sgb = stg[pi % 2]
                    pi += 1
                    MEMSET("dve", p_[:, 0:3], 0.0, [pb_])
                    for b in range(4):
                        bk, bkb = bank()
                        for c in range(8):
                            MM(bk, w[:, c, cc * 128:(cc + 1) * 128], hTb[:, c, b * 512:(b + 1) * 512], c == 0, c == 7, [wb, hb[b]], [bkb])
                        CP("act", p_[:, 3 + b * 512:3 + (b + 1) * 512], bk, [bkb], [pb_])
                    TS("dve", y_, p_[:, 3:T + 3], cw[:, ch * 4 + 3:ch * 4 + 4], None, ALU.mult, None, [pb_, b_pv[1]], [yb_])
                    for j in range(3):
                        STT("dve", y_, p_[:, j:T + j], cw[:, ch * 4 + j:ch * 4 + j + 1], y_, ALU.mult, ALU.add, [pb_, b_pv[1], yb_], [yb_])
                    def tail(ch=ch, y_=y_, yb_=yb_, s_=s_, sb_=sb_, sg=sg, sgb=sgb):
                        ACT(y_, y_, AF.Silu, [yb_], [yb_])
                        if ch < 16:
                            hh = ch % 8
                            scale = (128.0 ** -0.5) if ch < 8 else 1.0
                            ACT(s_, y_, AF.Square, [yb_], [sb_])
                            for b in range(4):
                                bk, bkb = bank()
                                MM(bk, ONES_BF, s_[:, b * 512:(b + 1) * 512], True, True, [b_cbf, sb_], [bkb])
                                r_, rb_ = rsb2[b % 2]
                                ACT(r_, bk, AF.Ln, [bkb, b_small], [rb_], bias=C_EPSRMS)
                                ACT(r_, r_, AF.Exp, [rb_], [rb_], scale=-0.5)
                                STT("dve", sg[:, b * 512:(b + 1) * 512], y_[:, b * 512:(b + 1) * 512], scale, r_, ALU.mult, ALU.mult, [yb_, rb_], [sgb])
                            if ch < 8:
                                P.dma("sp", sc_q[hh], sg, sgb, reads=[sgb], writes=[scqb])
                            else:
                                P.dma("sp", sc_k[hh], sg, sgb, reads=[sgb], writes=[sckb])
                        else:
                            CP("dve", sg, y_, [yb_], [sgb])
                            P.dma("sp", sc_v[ch - 16], sg, sgb, reads=[sgb], writes=[scvb])
                    tail()
            for g in range(2):
                w, wb = wload("w_in_1", 0, 8, 3072 + g * 512, 512)
                for cc in range(4):
                    ch = g * 4 + cc
                    sg, sgb = stg[ch % 2]
                    for b in range(4):
                        bk, bkb = bank()
                        for c in range(8):
                            MM(bk, w[:, c, cc * 128:(cc + 1) * 128], hTb[:, c, b * 512:(b + 1) * 512], c == 0, c == 7, [wb, hb[b]], [bkb])
                        ACT(sg[:, b * 512:(b + 1) * 512], bk, AF.Silu, [bkb], [sgb])
                    P.dma("sp", sc_z[ch], sg, sgb, reads=[sgb], writes=[sczb])
            w, wb = wload("w_in_1", 0, 8, 4096, 16)
            for i in range(16):
                bk, bkb = bank()
                for c in range(8):
                    MM(bk[:, 0:16], hTb[:, c, i * 128:(i + 1) * 128], w[:, c, :], c == 0, c == 7, [wb, hb[i // 4]], [bkb])
                CP("act", ba[:, i, :], bk[:, 0:16], [bkb], [bab])
            A.release(m1)
            wmo, wmob = wload("w_mix_out_1", 0, 8, 0, D)
            beta, betab = A.alloc([128, 16, 8], F32, "gbeta")
            lbt, lbtb = A.alloc([128, 16, 8], F32, "glb")
            gg, ggb = A.alloc([128, 16, 8], F32, "gg")
            negA, negAb = A.alloc([128, 8], F32, "gnegA")
            ACT(beta, ba[:, :, 0:8], AF.Exp, [bab], [betab], scale=-1.0)
            TS("dve", beta, beta, 1.0, None, ALU.add, None, [betab], [betab])
            P.op("dve", lambda en: en.reciprocal(out=beta, in_=beta), [betab], [betab])
            ACT(lbt, beta, AF.Ln, [betab], [lbtb])
            TT("dve", gg, ba[:, :, 8:16], pv[1][:, PV1_DT:PV1_DT + 8].unsqueeze(1).to_broadcast([128, 16, 8]), ALU.add, [bab, b_pv[1]], [ggb])
            ACT(gg, gg, AF.Exp, [ggb], [ggb])
            ACT(gg, gg, AF.Ln, [ggb, b_small], [ggb], bias=C_ONE)
            ACT(negA, pv[1][:, PV1_AL:PV1_AL + 8], AF.Exp, [b_pv[1]], [negAb])
            TS("dve", negA, negA, -1.0, None, ALU.mult, None, [negAb], [negAb])
            TT("dve", gg, gg, negA.unsqueeze(1).to_broadcast([128, 16, 8]), ALU.mult, [ggb, negAb], [ggb])
            S, Sb = A.alloc([128, 8, 128], F32, "gS")
            Sbf, Sbfb = A.alloc([128, 8, 128], BF16, "gSbf")
            MEMSET("dve", S, 0.0, [Sb])
            CP("act", Sbf, S, [Sb], [Sbfb])
            vti = [A.alloc([128, 8, 128], BF16, f"gvt{i}") for i in range(2)]
            zti = [A.alloc([128, 8, 128], BF16, f"gzt{i}") for i in range(3)]
            qti = [A.alloc([128, 8, 128], BF16, f"gqt{i}") for i in range(2)]
            kti = [A.alloc([128, 8, 128], BF16, f"gkt{i}") for i in range(2)]
            sc = [A.alloc([128, 48], F32, f"gsc{i}") for i in range(2)]
            Dg, Dgb = A.alloc([128, 8, 128], F32, "gDg")
            Da, Dab = A.alloc([128, 8, 128], F32, "gDa")
            X1, X1b = A.alloc([128, 8, 128], F32, "gX1")
            X2, X2b = A.alloc([128, 8, 128], F32, "gX2")
            X3, X3b = A.alloc([128, 8, 128], F32, "gX3")
            egr, egrb = A.alloc([128, 8, 128], F32, "gegr")
            Qa = [A.alloc([128, 8, 128], F32, f"gQ{i}") for i in range(2)]
            QTa = [A.alloc([128, 8, 128], F32, f"gQT{i}") for i in range(2)]
            RT, RTb = A.alloc([128, 8, 128], F32, "gRT")
            vbt, vbtb = A.alloc([128, 8, 128], F32, "gvb")
            kbg, kbgb = A.alloc([128, 8, 128], F32, "gkbg")
            qkTs = [A.alloc([128, 8, 128], BF16, f"gqk{i}") for i in range(2)]
            qgTs = [A.alloc([128, 8, 128], BF16, f"gqg{i}") for i in range(2)]
            kgbs = [A.alloc([128, 8, 128], BF16, f"gkg{i}") for i in range(2)]
            uus = [A.alloc([128, 8, 128], F32, f"gu{i}") for i in range(2)]
            wTs = [A.alloc([128, 8, 128], BF16, f"gwT{i}") for i in range(2)]
            vn, vnb = A.alloc([128, 8, 128], BF16, "gvn")
            osq, osqb = A.alloc([128, 8, 128], BF16, "gosq")
            ors, orsb = A.alloc([128, 8, 128], F32, "gors")
            ot_, otb_ = A.alloc([128, 8, 128], F32, "got")
            GN = pv[1][:, PV1_GN:PV1_GN + 1]
            B1 = cst[:, C_B1:C_B1 + 512]
            B2 = cst[:, C_B2:C_B2 + 512]
            B3 = cst[:, C_B3:C_B3 + 512]

            def v3(ap):
                return ap.rearrange("p (a b) -> p a b", a=8)

            def ld(i):
                vt_, vtb_ = vti[i % 2]
                zt_, ztb_ = zti[i % 3]
                P.dma("sp", vt_, sc_v.rearrange("c p t -> p c t")[:, :, i * 128:(i + 1) * 128], vtb_, reads=[scvb], writes=[vtb_])
                P.dma("sp", zt_, sc_z.rearrange("c p t -> p c t")[:, :, i * 128:(i + 1) * 128], ztb_, reads=[sczb], writes=[ztb_])
                qt_, qtb_ = qti[i % 2]
                kt_, ktb_ = kti[i % 2]
                P.dma("sp", qt_, sc_q.rearrange("c p t -> p c t")[:, :, i * 128:(i + 1) * 128], qtb_, reads=[scqb], writes=[qtb_])
                P.dma("sp", kt_, sc_k.rearrange("c p t -> p c t")[:, :, i * 128:(i + 1) * 128], ktb_, reads=[sckb], writes=[ktb_])

            def make_prep(i):
                par = i % 2
                vt_, vtb_ = vti[i % 2]
                qt_, qtb_ = qti[i % 2]
                kt_, ktb_ = kti[i % 2]
                s_, sb_ = sc[par]
                gc = s_[:, 0:8]
                aa = s_[:, 8:16]
                glast = s_[:, 16:24]
                kgs = s_[:, 24:32]
                bgc = s_[:, 32:40]
                qkT, qkTb = qkTs[par]
                qgT, qgTb = qgTs[par]
                kgb_, kgbb = kgbs[par]
                uu, uub = uus[par]
                wT, wTb = wTs[par]
                Dg2 = Dg.rearrange("p a b -> p (a b)")
                Da2 = Da.rearrange("p a b -> p (a b)")
                segs = []

                def s0():
                    if i + 1 < 16:
                        ld(i + 1)
                    bk, bkb = bank()
                    MM(bk[:, 0:8], U_F, gg[:, i, :], True, True, [b_cst, ggb], [bkb])
                    CP("act", gc, bk[:, 0:8], [bkb], [sb_])
                    TT("dve", aa, gc, lbt[:, i, :], ALU.add, [sb_, lbtb], [sb_])
                    TT("dve", Dg, bm(ID_F, 8), bc(gc, 128), ALU.mult, [b_cst, sb_], [Dgb])
                    TT("dve", Da, bm(ID_F, 8), bc(aa, 128), ALU.mult, [b_cst, sb_], [Dab])
                    pR, pRb = pair()
                    for hh in range(2):
                        MM(pR[:, hh * 512:(hh + 1) * 512], ONES_F, Dg2[:, hh * 512:(hh + 1) * 512], True, True, [b_cst, Dgb], pRb)
                    ACT(egr, v3(pR), AF.Exp, pRb, [egrb])
                    ACT(glast, v3(pR)[:, :, 127], AF.Exp, pRb, [sb_])
                    TT("dve", kgs, v3(pR)[:, :, 127], gc, ALU.subtract, pRb + [sb_], [sb_])
                    ACT(kgs, kgs, AF.Exp, [sb_], [sb_])
                    ACT(bgc, aa, AF.Exp, [sb_], [sb_])
                    TT("dve", qgT, qt_, egr, ALU.mult, [qtb_, egrb], [qgTb])
                segs.append(s0)

                def s1():
                    p1, p1b = pair()
                    for hh in range(2):
                        MM(p1[:, hh * 512:(hh + 1) * 512], ONES_F, Dg2[:, hh * 512:(hh + 1) * 512], True, False, [b_cst, Dgb], p1b)
                        MM(p1[:, hh * 512:(hh + 1) * 512], ID_F, B1, False, True, [b_cst], p1b)
                    STT("dve", X1, v3(p1), -1.0, bc(aa, 128), ALU.mult, ALU.add, p1b + [sb_], [X1b])
                    ACT(X1, X1, AF.Exp, [X1b], [X1b])
                    p2, p2b = pair()
                    for hh in range(2):
                        MM(p2[:, hh * 512:(hh + 1) * 512], ONES_F, Da2[:, hh * 512:(hh + 1) * 512], True, False, [b_cst, Dab], p2b)
                        MM(p2[:, hh * 512:(hh + 1) * 512], ID_F, B2, False, True, [b_cst], p2b)
                    TT("dve", X2, v3(p2), bc(gc, 128), ALU.subtract, p2b + [sb_], [X2b])
                    ACT(X2, X2, AF.Exp, [X2b], [X2b])
                segs.append(s1)

                def s2():
                    p3, p3b = pair()
                    for hh in range(2):
                        MM(p3[:, hh * 512:(hh + 1) * 512], ONES_F, Dg2[:, hh * 512:(hh + 1) * 512], True, False, [b_cst, Dgb], p3b)
                        MM(p3[:, hh * 512:(hh + 1) * 512], ID_F, B3, False, True, [b_cst], p3b)
                    TT("dve", X3, v3(p3), bc(gc, 128), ALU.subtract, p3b + [sb_], [X3b])
                    ACT(X3, X3, AF.Exp, [X3b], [X3b])
                    pA, pAb = pair()
                    for h in range(8):
                        MM(pA[:, h * 128:(h + 1) * 128], kt_[:, h, :], kt_[:, h, :], True, True, [ktb_], pAb)
                    Q0, Q0b = Qa[0]
                    QT0, QT0b = QTa[0]
                    TT("dve", Q0, v3(pA), X1, ALU.mult, pAb + [X1b], [Q0b])
                    TT("dve", QT0, v3(pA), X2, ALU.mult, pAb + [X2b], [QT0b])
                    pB, pBb = pair()
                    for h in range(8):
                        MM(pB[:, h * 128:(h + 1) * 128], kt_[:, h, :], qt_[:, h, :], True, True, [ktb_, qtb_], pBb)
                    TT("dve", qkT, v3(pB), X3, ALU.mult, pBb + [X3b], [qkTb])
                    TT("dve", RT, bm(ID_F, 8), QT0, ALU.subtract, [b_cst, QT0b], [RTb])
                segs.append(s2)

                def mk_neu(k):
                    def f():
                        cur = (k - 1) % 2
                        Qp, Qpb = Qa[cur]
                        QTp, QTpb = QTa[cur]
                        Qn, Qnb = Qa[1 - cur]
                        QTn, QTnb = QTa[1 - cur]
                        pq, pqb = pair()
                        for h in range(8):
                            MM(pq[:, h * 128:(h + 1) * 128], QTp[:, h, :], Qp[:, h, :], True, True, [QTpb, Qpb], pqb)
                        CP("act", Qn, v3(pq), pqb, [Qnb])
                        if k < 6:
                            pqt, pqtb = pair()
                            if os.environ.get("QTMM", "1") == "1":
                                for h in range(8):
                                    MM(pqt[:, h * 128:(h + 1) * 128], Qp[:, h, :], QTp[:, h, :], True, True, [QTpb, Qpb], pqtb)
                            else:
                                for h in range(8):
                                    TR(pqt[:, h * 128:(h + 1) * 128], Qn[:, h, :], ID_F, [Qnb, b_cst], pqtb)
                            CP("act", QTn, v3(pqt), pqtb, [QTnb])
                        pr_, prb_ = pair()
                        for h in range(8):
                            MM(pr_[:, h * 128:(h + 1) * 128], Qn[:, h, :], RT[:, h, :], True, True, [Qnb, RTb], prb_)
                        TT("dve", RT, RT, v3(pr_), ALU.add, [RTb] + prb_, [RTb])
                    return f
                for k in range(1, 7):
                    segs.append(mk_neu(k))

                def s9():
                    bkk, bkkb = bank()
                    kk3 = bkk.bitcast(BF16).rearrange("p (a b) -> p a b", a=8)
                    for h in range(8):
                        TR(kk3[:, h, :], kt_[:, h, :], ID_BF, [ktb_, b_cbf], [bkkb])
                    TT("dve", kbg, kk3, bc(bgc, 128), ALU.mult, [bkkb, sb_], [kbgb])
                    TT("dve", kgb_, kk3, bc(kgs, 128), ALU.mult, [bkkb, sb_], [kgbb])
                    bkv, bkvb = bank()
                    vv3 = bkv.bitcast(BF16).rearrange("p (a b) -> p a b", a=8)
                    for h in range(8):
                        TR(vv3[:, h, :], vt_[:, h, :], ID_BF, [vtb_, b_cbf], [bkvb])
                    TT("dve", vbt, vv3, bc(beta[:, i, :], 128), ALU.mult, [bkvb, betab], [vbtb])
                segs.append(s9)

                def s10():
                    pu, pub = pair()
                    for h in range(8):
                        MM(pu[:, h * 128:(h + 1) * 128], RT[:, h, :], vbt[:, h, :], True, True, [RTb, vbtb], pub)
                    CP("act", uu, v3(pu), pub, [uub])
                    pw, pwb = pair()
                    for h in range(8):
                        MM(pw[:, h * 128:(h + 1) * 128], kbg[:, h, :], RT[:, h, :], True, True, [kbgb, RTb], pwb)
                    CP("act", wT, v3(pw), pwb, [wTb])
                segs.append(s10)
                return segs

            def make_scan(i):
                par = i % 2
                tsl = slice(i * 128, (i + 1) * 128)
                zt_, ztb_ = zti[i % 3]
                s_, sb_ = sc[par]
                glast = s_[:, 16:24]
                qkT, qkTb = qkTs[par]
                qgT, qgTb = qgTs[par]
                kgb_, kgbb = kgbs[par]
                uu, uub = uus[par]
                wT, wTb = wTs[par]
                hold = {}

                def t0():
                    pv_, pvb_ = pair()
                    for h in range(8):
                        MM(pv_[:, h * 128:(h + 1) * 128], wT[:, h, :], Sbf[:, h, :], True, True, [wTb, Sbfb], pvb_)
                    TT("dve", vn, uu, v3(pv_), ALU.subtract, [uub] + pvb_, [vnb])

                def t1():
                    po, pob = pair()
                    hold["po"] = (po, pob)
                    for h in range(8):
                        MM(po[:, h * 128:(h + 1) * 128], Sbf[:, h, :], qgT[:, h, :], True, False, [Sbfb, qgTb], pob)
                        MM(po[:, h * 128:(h + 1) * 128], vn[:, h, :], qkT[:, h, :], False, True, [vnb, qkTb], pob)
                    if i < 15:
                        pS, pSb = pair()
                        for h in range(8):
                            MM(pS[:, h * 128:(h + 1) * 128], kgb_[:, h, :], vn[:, h, :], True, True, [kgbb, vnb], pSb)
                        TT("dve", S, S, bc(glast, 128), ALU.mult, [Sb, sb_], [Sb])
                        TT("dve", S, S, v3(pS), ALU.add, [Sb] + pSb, [Sb])
                        CP("act", Sbf, S, [Sb], [Sbfb])
                    po, pob = hold["po"]
                    ACT(osq, v3(po), AF.Square, pob, [osqb])
                    CP("act", ot_, v3(po), pob, [otb_])

                def t2():
                    pn, pnb = pair()
                    osq2 = osq.rearrange("p a b -> p (a b)")
                    for hh in range(2):
                        MM(pn[:, hh * 512:(hh + 1) * 512], ONES_BF, osq2[:, hh * 512:(hh + 1) * 512], True, True, [b_cbf, osqb], pnb)
                    ACT(ors, v3(pn), AF.Ln, pnb + [b_small], [orsb], bias=C_EPSRMS, scale=1.0 / 128.0)
                    ACT(ors, ors, AF.Exp, [orsb], [orsb], scale=-0.5)

                def t3():
                    TT("dve", ot_, ot_, ors, ALU.mult, [otb_, orsb], [otb_])
                    STT("dve", hTb[:, :, tsl], ot_, GN, zt_, ALU.mult, ALU.mult, [otb_, b_pv[1], ztb_], [hb[i // 4]])
                return [t0, t1, t2, t3]

            ld(0)
            for f in make_prep(0):
                f()
            for i in range(16):
                Pq = make_prep(i + 1) if i + 1 < 16 else []
                Sq = make_scan(i)
                order = []
                pi_, si_ = 0, 0
                plan = "PPSPPSPPSPPSPPP"
                for ch_ in plan:
                    if ch_ == "P":
                        if pi_ < len(Pq):
                            order.append(Pq[pi_])
                            pi_ += 1
                    else:
                        order.append(Sq[si_])
                        si_ += 1
                while pi_ < len(Pq):
                    order.append(Pq[pi_])
                    pi_ += 1
                while si_ < len(Sq):
                    order.append(Sq[si_])
                    si_ += 1
                for f in order:
                    f()
            if dbg:
                P.dma("sp", dbg_o.rearrange("c p t -> p c t"), hTb[:], hb[0], reads=hb, writes=[outb])
            A.release(m0)
            zbufs = [A.alloc([128, 8, 512], F32, f"mz{i}") for i in range(4)]
            proj_res_ln(1, 0, 8,
                        lambda c, j: (wmo[:, c, j * 128:(j + 1) * 128], [wmob]),
                        lambda c, b: (hTb[:, c, b * 512:(b + 1) * 512], [hb[b]]),
                        range(4), zbufs)
            A.release(m0)

        def stage_touch():
            m0 = A.mark()
            tt_, ttb = A.alloc([128, 64], F32, "touch")
            for n in ["x", "mem"] + WNAMES:
                P.dma("sp", tt_[0:1, 0:16], dr[n][0:1, 0:16], ttb, writes=[ttb])
            ti_, tib = A.alloc([128, 64], I32, "touchi")
            P.dma("sp", ti_[0:1, 0:16], dr["pos"][0:1, 0:16], tib, writes=[tib])
            P.dma("sp", out_d[0:1, 0:16], tt_[0:1, 0:16], ttb, reads=[ttb], writes=[outb])
            A.release(m0)

        stages = [("touch", stage_touch), ("none", lambda: None), ("in", stage_in), ("mix0", stage_mixer0), ("xa0", lambda: stage_xattn(0)), ("ffn0", lambda: stage_ffn2(0)),
                  ("mix1", stage_mixer1), ("xa1", lambda: stage_xattn(1)), ("ffn1", lambda: stage_ffn2(1, final=True))]
        build.stage_cost = {}
        for nm, fn in stages:
            c0 = dict(cost)
            A.peak = 0
            fn()
            build.stage_cost.setdefault("_peak", {})[nm] = A.peak
            build.stage_cost[nm] = {k: round(cost[k] - c0[k], 1) for k in cost}
            if stop == nm:
                break
        P.op("sp", lambda en: en.nop(), reads=[outb] + resb + [scvb, sczb, scqb, sckb])
        P.emit()
        build.stats = P.stats
    return nc


_NC_CACHE = {}


def kernel(**inputs):
    inp = {k: np.asarray(v) for k, v in inputs.items()}
    if "full" not in _NC_CACHE:
        _NC_CACHE["full"] = build()
    nc = _NC_CACHE["full"]
    consts = make_consts()
    pv0 = make_pv(inp, 0)
    pv1 = make_pv(inp, 1)
    shared = {"consts": consts, "pv0": pv0, "pv1": pv1}
    for n in WNAMES:
        shared[n] = np.ascontiguousarray(inp[n], dtype=np.float32)
    in_maps = []
    for b in range(8):
        m = dict(shared)
        m["x"] = np.ascontiguousarray(inp["x"][b], dtype=np.float32)
        m["mem"] = np.ascontiguousarray(inp["mem"][b], dtype=np.float32)
        m["pos"] = np.ascontiguousarray(inp["positions"][b].reshape(1, T).astype(np.int32))
        in_maps.append(m)
    res = run_bass_kernel_spmd(nc, in_maps, core_ids=list(range(8)))
    return np.stack([np.asarray(r["out"], dtype=np.float32) for r in res.results], axis=0)
```
